# Optimizing a Trainium2 kernel written in Bass

```python
import math
import jax, jax.numpy as jnp
from jax import lax
import numpy as np

D_MODEL = 1024
BATCH = 4
SEQ = 8192
DEPTH = 2

N_META = 16
D_ATTN = 512
D_REC = 512
D_MIX = D_ATTN + D_REC
ATTN_HEADS = 4
ATTN_HEAD_DIM = 64
V_HEAD_DIM = 2 * ATTN_HEAD_DIM
ROPE_DIM = ATTN_HEAD_DIM // 4
ROPE_THETA = 500000.0
REC_BLOCKS = 8
REC_BLOCK_DIM = D_REC // REC_BLOCKS
CONV_WIDTH = 4
LRU_C = 8.0
D_FF = -(-8 * D_MODEL // (3 * 256)) * 256
Q_BLOCK = 128
EPS = 1e-6
N_IN = 3 * D_ATTN + 2 * D_REC

kernel_name = "hymba_diffattn_rglru_hybrid"


def rms_norm(x, g):
    xf = x.astype(jnp.float32)
    y = xf * lax.rsqrt(jnp.mean(xf * xf, axis=-1, keepdims=True) + EPS)
    return (y * g.astype(jnp.float32)).astype(x.dtype)


def rope_tables(T):
    inv = ROPE_THETA ** (-jnp.arange(0, ROPE_DIM, 2, dtype=jnp.float32) / ROPE_DIM)
    ang = jnp.arange(T, dtype=jnp.float32)[:, None] * inv[None, :]
    return jnp.cos(ang), jnp.sin(ang)


def apply_partial_rope(x, cos, sin):
    half = ROPE_DIM // 2
    c = cos[None, :, None, None, :].astype(x.dtype)
    s = sin[None, :, None, None, :].astype(x.dtype)
    x1 = x[..., :half]
    x2 = x[..., half:ROPE_DIM]
    return jnp.concatenate([x1 * c - x2 * s, x2 * c + x1 * s, x[..., ROPE_DIM:]], axis=-1)


def diff_attention(q, k, v, lam):
    B, T, H = q.shape[:3]
    nb = -(-T // Q_BLOCK)
    Tp = nb * Q_BLOCK
    padw = ((0, 0), (0, Tp - T), (0, 0), (0, 0), (0, 0))
    q = jnp.pad(q, padw)
    k = jnp.pad(k, padw)
    v = jnp.pad(v, padw[:4])
    qb = q.reshape(B, nb, Q_BLOCK, H, 2, ATTN_HEAD_DIM).transpose(1, 0, 2, 3, 4, 5)
    starts = jnp.arange(nb) * Q_BLOCK
    kpos = jnp.arange(Tp)
    scale = ATTN_HEAD_DIM ** -0.5

    def one_block(args):
        qblk, start = args
        s = jnp.einsum('bqhcd,bkhcd->bhcqk', qblk, k,
                       preferred_element_type=jnp.float32) * scale
        qpos = start + jnp.arange(Q_BLOCK)
        s = jnp.where(qpos[:, None] >= kpos[None, :], s, -jnp.inf)
        p = jax.nn.softmax(s, axis=-1)
        a = p[:, :, 0] - lam * p[:, :, 1]
        return jnp.einsum('bhqk,bkhe->bqhe', a.astype(v.dtype), v)

    o = lax.map(one_block, (qb, starts))
    return o.transpose(1, 0, 2, 3, 4).reshape(B, Tp, H, V_HEAD_DIM)[:, :T]


def rg_lru_branch(xr, gate, conv_w, conv_b, w_rg, b_rg, w_ig, b_ig, lru_L):
    B, T, _ = xr.shape
    xc = lax.conv_general_dilated(
        xr, conv_w[:, None, :].astype(xr.dtype), window_strides=(1,),
        padding=[(CONV_WIDTH - 1, 0)], dimension_numbers=('NWC', 'WIO', 'NWC'),
        feature_group_count=D_REC) + conv_b.astype(xr.dtype)
    xb = xc.reshape(B, T, REC_BLOCKS, REC_BLOCK_DIM)
    r = jax.nn.sigmoid((jnp.einsum('btnd,nde->btne', xb, w_rg).reshape(B, T, D_REC)
                        + b_rg).astype(jnp.float32))
    i = jax.nn.sigmoid((jnp.einsum('btnd,nde->btne', xb, w_ig).reshape(B, T, D_REC)
                        + b_ig).astype(jnp.float32))
    log_a = LRU_C * r * jax.nn.log_sigmoid(lru_L.astype(jnp.float32))
    a = jnp.exp(log_a)
    mult = jnp.sqrt(jnp.maximum(-jnp.expm1(2.0 * log_a), 0.0))
    mult = jnp.where(jnp.arange(T)[None, :, None] == 0, 1.0, mult)
    u = mult * i * xc.astype(jnp.float32)

    def combine(left, right):
        a1, b1 = left
        a2, b2 = right
        return a1 * a2, a2 * b1 + b2

    _, h = lax.associative_scan(combine, (a, u), axis=1)
    y = h * jax.nn.gelu(gate.astype(jnp.float32))
    return y.astype(xr.dtype)


def setup_inputs(seed: int = 0) -> dict:
    key = jax.random.key(seed)
    ks = jax.random.split(key, 24)
    f32 = jnp.float32
    L = DEPTH

    def nrm(k, shape, scale):
        return jax.random.normal(k, shape, f32) * scale

    u = jax.random.uniform(ks[15], (L, D_REC), f32, 0.9, 0.999)
    s = u ** (1.0 / LRU_C)
    lru_L = jnp.log(s) - jnp.log1p(-s)
    return {
        "x": nrm(ks[0], (BATCH, SEQ, D_MODEL), 1.0),
        "meta_tokens": nrm(ks[1], (N_META, D_MODEL), 1.0),
        "norm_mix_g": 1.0 + nrm(ks[2], (L, D_MODEL), 0.02),
        "w_in": nrm(ks[3], (L, D_MODEL, N_IN), D_MODEL ** -0.5),
        "q_norm_g": 1.0 + nrm(ks[4], (L, ATTN_HEAD_DIM), 0.02),
        "k_norm_g": 1.0 + nrm(ks[5], (L, ATTN_HEAD_DIM), 0.02),
        "lambda_q1": nrm(ks[6], (L, ATTN_HEAD_DIM), 0.1),
        "lambda_k1": nrm(ks[7], (L, ATTN_HEAD_DIM), 0.1),
        "lambda_q2": nrm(ks[8], (L, ATTN_HEAD_DIM), 0.1),
        "lambda_k2": nrm(ks[9], (L, ATTN_HEAD_DIM), 0.1),
        "subln_g": 1.0 + nrm(ks[10], (L, V_HEAD_DIM), 0.02),
        "conv_w": nrm(ks[11], (L, CONV_WIDTH, D_REC), CONV_WIDTH ** -0.5),
        "conv_b": nrm(ks[12], (L, D_REC), 0.01),
        "w_rg": nrm(ks[13], (L, REC_BLOCKS, REC_BLOCK_DIM, REC_BLOCK_DIM), REC_BLOCK_DIM ** -0.5),
        "b_rg": nrm(ks[14], (L, D_REC), 0.01),
        "w_ig": nrm(ks[16], (L, REC_BLOCKS, REC_BLOCK_DIM, REC_BLOCK_DIM), REC_BLOCK_DIM ** -0.5),
        "b_ig": nrm(ks[17], (L, D_REC), 0.01),
        "lru_L": lru_L,
        "rec_norm_g": 1.0 + nrm(ks[18], (L, D_REC), 0.02),
        "w_out": nrm(ks[19], (L, D_MIX, D_MODEL), D_MIX ** -0.5),
        "norm_ffn_g": 1.0 + nrm(ks[20], (L, D_MODEL), 0.02),
        "w_gu": nrm(ks[21], (L, D_MODEL, 2 * D_FF), D_MODEL ** -0.5),
        "w_down": nrm(ks[22], (L, D_FF, D_MODEL), D_FF ** -0.5),
    }


def reference(x, meta_tokens, norm_mix_g, w_in, q_norm_g, k_norm_g, lambda_q1, lambda_k1,
              lambda_q2, lambda_k2, subln_g, conv_w, conv_b, w_rg, b_rg, w_ig, b_ig, lru_L,
              rec_norm_g, w_out, norm_ffn_g, w_gu, w_down):
    B = x.shape[0]
    meta = jnp.broadcast_to(meta_tokens.astype(x.dtype)[None], (B, N_META, D_MODEL))
    h = jnp.concatenate([meta, x], axis=1)
    T = h.shape[1]
    cos, sin = rope_tables(T)
    splits = [D_ATTN, 2 * D_ATTN, 3 * D_ATTN, 3 * D_ATTN + D_REC]

    for l in range(DEPTH):
        lam_init = 0.8 - 0.6 * math.exp(-0.3 * l)
        hn = rms_norm(h, norm_mix_g[l])
        proj = hn @ w_in[l]
        q, k, v, xr, gate = jnp.split(proj, splits, axis=-1)
        q = q.reshape(B, T, ATTN_HEADS, 2, ATTN_HEAD_DIM)
        k = k.reshape(B, T, ATTN_HEADS, 2, ATTN_HEAD_DIM)
        v = v.reshape(B, T, ATTN_HEADS, V_HEAD_DIM)
        q = apply_partial_rope(rms_norm(q, q_norm_g[l]), cos, sin)
        k = apply_partial_rope(rms_norm(k, k_norm_g[l]), cos, sin)
        lam = (jnp.exp(jnp.sum(lambda_q1[l].astype(jnp.float32) * lambda_k1[l].astype(jnp.float32)))
               - jnp.exp(jnp.sum(lambda_q2[l].astype(jnp.float32) * lambda_k2[l].astype(jnp.float32)))
               + lam_init)
        o_attn = diff_attention(q, k, v, lam)
        o_attn = (rms_norm(o_attn, subln_g[l]) * (1.0 - lam_init)).reshape(B, T, D_ATTN)
        o_rec = rg_lru_branch(xr, gate, conv_w[l], conv_b[l], w_rg[l], b_rg[l],
                              w_ig[l], b_ig[l], lru_L[l])
        o_rec = rms_norm(o_rec, rec_norm_g[l])
        h = h + jnp.concatenate([o_attn, o_rec], axis=-1) @ w_out[l]
        hn = rms_norm(h, norm_ffn_g[l])
        g, u = jnp.split(hn @ w_gu[l], [D_FF], axis=-1)
        h = h + (jax.nn.silu(g) * u) @ w_down[l]

    return h[:, N_META:, :]
```

```python
import contextlib
import math
import numpy as np
import ml_dtypes
import concourse.bass as bass
import concourse.mybir as mybir
from concourse.bass_utils import run_bass_kernel_spmd

F32 = mybir.dt.float32
BF16 = mybir.dt.bfloat16
AF = mybir.ActivationFunctionType
ALU = mybir.AluOpType
AX = mybir.AxisListType

D_MODEL = 1024
N_META = 16
SEQ = 8192
T_REAL = SEQ + N_META
BLK = 384
T_PAD = 8448
D_FF = 2816
EPS = 1e-6
ROPE_THETA = 500000.0
GELU_C = 1.5957691216057308

COMPUTE = ("pe", "act", "dve", "pool")
QUEUES = ("sp", "act", "pool")


class Buf:
    __slots__ = ("w", "r", "name")

    def __init__(self, name=""):
        self.w = None
        self.r = []
        self.name = name


class Tile(Buf):
    __slots__ = ("t",)

    def __init__(self, t, name=""):
        Buf.__init__(self, name)
        self.t = t

    def __getitem__(self, idx):
        return self.t[idx]


class Op:
    __slots__ = ("eng", "fn", "deps", "seq", "is_dma", "ring", "cnt", "clock",
                 "waits", "signal", "sigidx", "inc")

    def __init__(self, eng, fn, is_dma):
        self.eng = eng
        self.fn = fn
        self.deps = []
        self.is_dma = is_dma
        self.ring = None
        self.cnt = 0
        self.waits = []
        self.signal = False
        self.sigidx = 0
        self.clock = None
        self.inc = 16


class Prog:
    def __init__(self, nc, ring_sizes=None):
        self.nc = nc
        self.ops = {e: [] for e in ("pe", "act", "dve", "pool", "sp")}
        self.all = []
        self.ring_sizes = ring_sizes or {"sp": 16, "act": 8, "pool": 12}
        self.dma_n = {q: 0 for q in QUEUES}
        self.ring_last = {q: [None] * self.ring_sizes[q] for q in QUEUES}
        self.ring_cnt = {q: [0] * self.ring_sizes[q] for q in QUEUES}
        self.cc_last = None
        self.cc_n = 0

    def _deps(self, op, reads, writes):
        deps = []
        for t in reads:
            if t.w is not None:
                deps.append(t.w)
        for t in writes:
            if t.w is not None:
                deps.append(t.w)
            deps.extend(t.r)
        for t in reads:
            t.r.append(op)
        for t in writes:
            t.w = op
            t.r = []
        seen = set()
        out = []
        for d in deps:
            if id(d) not in seen and d is not op:
                seen.add(id(d))
                out.append(d)
        op.deps = out

    def _add(self, o):
        o.seq = len(self.ops[o.eng])
        self.ops[o.eng].append(o)
        self.all.append(o)
        return o

    def op(self, eng, fn, reads=(), writes=()):
        o = Op(eng, fn, False)
        self._deps(o, reads, writes)
        return self._add(o)

    def dma(self, q, out_ap, in_ap, reads=(), writes=(), **kw):
        def fn(eng):
            return eng.dma_start(out=out_ap, in_=in_ap, **kw)
        return self.dma_like(q, fn, reads, writes)

    def dma_like(self, q, fn, reads=(), writes=()):
        o = Op(q, fn, True)
        self._deps(o, reads, writes)
        n = self.dma_n[q]
        self.dma_n[q] = n + 1
        slot = n % self.ring_sizes[q]
        prev = self.ring_last[q][slot]
        if prev is not None:
            o.deps.append(prev)
        self.ring_last[q][slot] = o
        self.ring_cnt[q][slot] += 16
        o.ring = (q, slot)
        o.cnt = self.ring_cnt[q][slot]
        return self._add(o)

    def cc(self, fn, reads=(), writes=()):
        o = Op("pool", fn, True)
        self._deps(o, reads, writes)
        if self.cc_last is not None:
            o.deps.append(self.cc_last)
        self.cc_last = o
        self.cc_n += 1
        o.ring = ("cc", 0)
        o.cnt = self.cc_n
        o.inc = 1
        return self._add(o)

    def barrier(self):
        lasts = []
        for e in self.ops:
            if self.ops[e]:
                lasts.append(self.ops[e][-1])
        for q in QUEUES:
            for o in self.ring_last[q]:
                if o is not None:
                    lasts.append(o)
        if self.cc_last is not None:
            lasts.append(self.cc_last)
        for e in ("pe", "act", "dve", "pool", "sp"):
            o = Op(e, None, False)
            o.deps = [d for d in lasts]
            self._add(o)

    def emit(self, final_wait_eng="sp"):
        nc = self.nc
        fin = Op(final_wait_eng, None, False)
        for e in self.ops:
            if self.ops[e]:
                fin.deps.append(self.ops[e][-1])
        for q in QUEUES:
            for o in self.ring_last[q]:
                if o is not None:
                    fin.deps.append(o)
        if self.cc_last is not None:
            fin.deps.append(self.cc_last)
        self._add(fin)

        cur = {e: {} for e in self.ops}
        for o in self.all:
            ck = cur[o.eng]
            waits = {}
            for d in o.deps:
                if d.is_dma:
                    key, val = d.ring, d.cnt
                else:
                    if d.eng == "pe" and o.eng == "pe":
                        continue
                    key, val = d.eng, d.seq + 1
                if ck.get(key, 0) >= val:
                    continue
                if waits.get(key, (0, None))[0] < val:
                    waits[key] = (val, d)
            for key, (val, d) in waits.items():
                for k2, v2 in d.clock.items():
                    if ck.get(k2, 0) < v2:
                        ck[k2] = v2
                if ck.get(key, 0) < val:
                    ck[key] = val
                if not d.is_dma:
                    d.signal = True
            o.waits = [(key, val, d) for key, (val, d) in waits.items()]
            o.clock = dict(ck)
        for e in self.ops:
            n = 0
            for o in self.ops[e]:
                if o.signal and not o.is_dma:
                    n += 1
                    o.sigidx = n
        with contextlib.ExitStack() as st:
            sems = {}
            for e in self.ops:
                sems[e] = st.enter_context(nc.semaphore("p_" + e))
            for q in QUEUES:
                for s in range(self.ring_sizes[q]):
                    sems[(q, s)] = st.enter_context(nc.semaphore("r_%s%d" % (q, s)))
            sems[("cc", 0)] = st.enter_context(nc.semaphore("cc_sem"))
            block = st.enter_context(nc.Block())
            stats = {"waits": 0, "ops": 0}

            def run(ename):
                def body(eng):
                    for o in self.ops[ename]:
                        for key, val, d in o.waits:
                            if d.is_dma:
                                eng.wait_ge(sems[key], val)
                            else:
                                eng.wait_ge(sems[key], d.sigidx)
                            stats["waits"] += 1
                        if o.fn is None:
                            if o.signal:
                                eng.nop().then_inc(sems[ename], 1)
                            continue
                        ins = o.fn(eng)
                        stats["ops"] += 1
                        if o.is_dma:
                            ins.then_inc(sems[o.ring], o.inc)
                        elif o.signal:
                            ins.then_inc(sems[ename], 1)
                return body

            block.tensor(run("pe"))
            block.scalar(run("act"))
            block.vector(run("dve"))
            block.gpsimd(run("pool"))
            block.sync(run("sp"))
            self.stats = stats
        return self


class Ctx:
    def __init__(self, nc, st):
        self.nc = nc
        self.st = st
        self.P = Prog(nc)
        ps = st.enter_context(nc.psum_tensor("psum_all", [128, 8, 512], F32))
        self.ps = ps
        self.bank = [Buf("bank%d" % i) for i in range(8)]

    def sb(self, name, shape, dt):
        if getattr(self, "arena", None) is None:
            return Tile(self.st.enter_context(self.nc.sbuf_tensor(name, shape, dt)), name)
        esz = 4 if dt == F32 else 2
        n = 1
        for d in shape[1:]:
            n *= d
        words = (n * esz + 3) // 4
        words = (words + 7) // 8 * 8
        a = self.arena_off
        assert a + words <= self.arena_words, "arena overflow %s need %d have %d" % (name, words, self.arena_words - a)
        self.arena_off = a + words
        ap = self.arena[:, a:a + (n * esz + 3) // 4]
        if dt != F32:
            ap = ap.bitcast(dt)
            if ap.shape[-1] != n:
                ap = ap[:, 0:n]
        if len(shape) > 2:
            names = " ".join("d%d" % i for i in range(1, len(shape)))
            kw = {"d%d" % i: shape[i] for i in range(1, len(shape))}
            ap = ap.rearrange("p (%s) -> p %s" % (names, names), **kw)
        return Tile(ap, name)

    def use_arena(self, words):
        self.arena = self.st.enter_context(self.nc.sbuf_tensor("arena", [128, words], F32))
        self.arena_words = words
        self.arena_off = 0

    def arena_reset(self, keep=0):
        self.arena_off = keep

    def dram(self, name, shape, dt, kind):
        return self.nc.dram_tensor(name, shape, dt, kind=kind).ap()

    def pf(self, b, n=512, off=0):
        return self.ps[:, b, off:off + n]

    def pb(self, b):
        return self.ps[:, b, :].bitcast(BF16)

    def mm(self, out, lhsT, rhs, start, stop, R, W, **kw):
        self.P.op("pe", lambda e: e.matmul(out, lhsT, rhs, start=start, stop=stop, **kw), R, W)

    def tr(self, out, in_, ident, R, W):
        self.P.op("pe", lambda e: e.transpose(out, in_, ident[:]), list(R) + [ident], W)

    def act(self, out, in_, func, R, W, **kw):
        self.P.op("act", lambda e: e.activation(out=out, in_=in_, func=func, **kw), R, W)

    def tt(self, eng, out, in0, in1, op, R, W):
        self.P.op(eng, lambda e: e.tensor_tensor(out=out, in0=in0, in1=in1, op=op), R, W)

    def ts(self, eng, out, in0, s1, s2, op0, op1, R, W, **kw):
        if s2 is None:
            self.P.op(eng, lambda e: e.tensor_scalar(out=out, in0=in0, scalar1=s1, scalar2=None, op0=op0, **kw), R, W)
        else:
            self.P.op(eng, lambda e: e.tensor_scalar(out=out, in0=in0, scalar1=s1, scalar2=s2, op0=op0, op1=op1, **kw), R, W)

    def stt(self, out, in0, scalar, in1, op0, op1, R, W):
        self.P.op("dve", lambda e: e.scalar_tensor_tensor(out=out, in0=in0, scalar=scalar, in1=in1, op0=op0, op1=op1), R, W)

    def cp(self, eng, out, in_, R, W):
        if eng == "act":
            self.P.op("act", lambda e: e.activation(out=out, in_=in_, func=AF.Copy), R, W)
        else:
            self.P.op(eng, lambda e: e.tensor_copy(out=out, in_=in_), R, W)

    def red(self, out, in_, R, W):
        self.P.op("dve", lambda e: e.tensor_reduce(out=out, in_=in_, axis=AX.X, op=ALU.add), R, W)

    def scan(self, out, d0, d1, init, R, W):
        self.P.op("dve", lambda e: e.tensor_tensor_scan(out=out, data0=d0, data1=d1, initial=init,
                                                        op0=ALU.mult, op1=ALU.add), R, W)

    def recip(self, out, in_, R, W):
        self.P.op("dve", lambda e: e.reciprocal(out=out, in_=in_), R, W)

    def memset(self, eng, ap, val, W):
        self.P.op(eng, lambda e: e.memset(ap, val), [], W)

    def rstd(self, out, in_, n, R, W):
        self.act(out, in_, AF.Ln, R, W, scale=1.0 / n, bias=self.eps_col[:, 0:1])
        self.act(out, out, AF.Exp, W, W, scale=-0.5)

    def consts(self):
        self.eps_col = self.sb("eps_col", [128, 1], F32)
        self.memset("dve", self.eps_col[:], EPS, [self.eps_col])


def emit_norm_T(C, h_t, nt, gcol, outT, ident, pbank0, scratch, ss, hn):
    for t in range(nt):
        C.act(hn[:, t, :], h_t[:, t, :], AF.Square, [h_t], [hn, ss], accum_out=ss[:, t:t + 1])
    C.rstd(ss[:, 0:nt], ss[:, 0:nt], float(D_MODEL), [ss], [ss])
    for t in range(nt):
        C.ts("dve", hn[:, t, :], h_t[:, t, :], ss[:, t:t + 1], None, ALU.mult, None, [h_t, ss], [hn])
    w = nt * 128
    for c in range(8):
        b = pbank0 + (c % 2)
        pv = C.pb(b)[:, 0:w]
        for t in range(nt):
            C.tr(pv[:, t * 128:(t + 1) * 128], hn[:, t, c * 128:(c + 1) * 128], ident, [hn], [C.bank[b]])
        eng = "dve" if c % 2 == 0 else "pool"
        if eng == "pool":
            C.act(outT[:, c, 0:w], pv, AF.Copy, [C.bank[b], gcol], [outT], scale=gcol[:, c:c + 1])
        else:
            C.ts("dve", outT[:, c, 0:w], pv, gcol[:, c:c + 1], None, ALU.mult, None, [C.bank[b], gcol], [outT])


def load_cast_weight(C, dst, src_ap, nchunk, ncol, colstep=512):
    v = src_ap.rearrange("(c p) n -> p c n", p=128)
    for c in range(nchunk):
        for n0 in range(0, ncol, colstep):
            n1 = min(ncol, n0 + colstep)
            C.P.dma("pool", dst[:, c, n0:n1], v[:, c, n0:n1], writes=[dst])


def build_pre(Th):
    nb = Th // BLK
    nc = bass.Bass("TRN2", target_bir_lowering=False)
    with contextlib.ExitStack() as st:
        C = Ctx(nc, st)
        h_in = C.dram("h_in", [Th, D_MODEL], F32, "ExternalInput")
        gmix = C.dram("gmix", [128, 8], F32, "ExternalInput")
        ident_d = C.dram("ident", [128, 128], BF16, "ExternalInput")
        hnT_o = C.dram("hnT_o", [D_MODEL, Th], BF16, "ExternalOutput")
        C.consts()
        ident = C.sb("ident_sb", [128, 128], BF16)
        gcol = C.sb("gcol", [128, 8], F32)
        C.P.dma("sp", ident[:], ident_d, writes=[ident])
        C.P.dma("sp", gcol[:], gmix, writes=[gcol])
        hv = h_in.rearrange("(t p) d -> p t d", p=128)
        ov = hnT_o.rearrange("(c p) t -> p c t", p=128)
        hts = [C.sb("h%d" % i, [128, 3, D_MODEL], F32) for i in range(2)]
        outs = [C.sb("o%d" % i, [128, 8, BLK], BF16) for i in range(2)]
        scratch = None
        ss = C.sb("ss", [128, 4], F32)
        hn = C.sb("hn", [128, 3, D_MODEL], BF16)
        for j in range(nb):
            ht = hts[j % 2]
            oT = outs[j % 2]
            C.P.dma("sp", ht[:, :, :], hv[:, 3 * j:3 * j + 3, :], writes=[ht])
            emit_norm_T(C, ht, 3, gcol, oT, ident, 0, scratch, ss, hn)
            C.P.dma("sp", ov[:, :, j * BLK:(j + 1) * BLK], oT[:, :, :], reads=[oT])
        C.P.emit()
    return nc


def build_post(Th, last):
    nb = Th // BLK
    NF = D_FF // 128
    nc = bass.Bass("TRN2", target_bir_lowering=False)
    with contextlib.ExitStack() as st:
        C = Ctx(nc, st)
        h_in = C.dram("h_in", [Th, D_MODEL], F32, "ExternalInput")
        oattnT = C.dram("oattnT", [512, Th], BF16, "ExternalInput")
        orecT = C.dram("orecT", [512, Th], BF16, "ExternalInput")
        w_out = C.dram("w_out", [1024, D_MODEL], F32, "ExternalInput")
        w_gu = C.dram("w_gu", [D_MODEL, 2 * D_FF], F32, "ExternalInput")
        w_down = C.dram("w_down", [D_FF, D_MODEL], F32, "ExternalInput")
        gcols = C.dram("gcols", [128, 20], F32, "ExternalInput")
        ident_d = C.dram("ident", [128, 128], BF16, "ExternalInput")
        h_out = C.dram("h_out", [Th, D_MODEL], F32, "ExternalOutput")
        if not last:
            hnT_o = C.dram("hnT_o", [D_MODEL, Th], BF16, "ExternalOutput")
        C.consts()
        P = C.P
        ident = C.sb("ident_sb", [128, 128], BF16)
        gc = C.sb("gc", [128, 20], F32)
        P.dma("sp", ident[:], ident_d, writes=[ident])
        P.dma("sp", gc[:], gcols, writes=[gc])
        ones = C.sb("ones", [128, 128], BF16)
        C.memset("dve", ones[:], 1.0 / 512.0, [ones])
        Wout = C.sb("Wout", [128, 8, D_MODEL], BF16)
        Wgu = C.sb("Wgu", [128, 8, 2 * D_FF], BF16)
        Wdn = C.sb("Wdn", [128, NF, D_MODEL], BF16)
        load_cast_weight(C, Wout, w_out, 8, D_MODEL)
        load_cast_weight(C, Wgu, w_gu, 8, 2 * D_FF)
        load_cast_weight(C, Wdn, w_down, NF, D_MODEL)

        hv = h_in.rearrange("(t p) d -> p t d", p=128)
        hov = h_out.rearrange("(t p) d -> p t d", p=128)
        av = oattnT.rearrange("(c p) t -> p c t", p=128)
        rv = orecT.rearrange("(c p) t -> p c t", p=128)
        if not last:
            ov = hnT_o.rearrange("(c p) t -> p c t", p=128)

        mixTs = [C.sb("mixT%d" % i, [128, 8, BLK], BF16) for i in range(1)]
        ht = C.sb("ht", [128, 3, D_MODEL], F32)
        rstd_r = C.sb("rstd_r", [128, BLK], F32)
        mixn = C.sb("mixn", [128, 4, BLK], BF16)
        sq = mixn
        scratch = None
        ss = C.sb("ss", [128, 4], F32)
        hn = C.sb("hn", [128, 3, D_MODEL], BF16)
        hnT = C.sb("hnT", [128, 8, BLK], BF16)
        sg = C.sb("sg", [128, 2, BLK], F32)
        actT = C.sb("actT", [128, NF, BLK], BF16)
        nxT = hnT

        for j in range(nb):
            mixT = mixTs[0]
            P.dma("sp", mixT[:, 0:4, :], av[:, :, j * BLK:(j + 1) * BLK], writes=[mixT])
            P.dma("sp", mixT[:, 4:8, :], rv[:, :, j * BLK:(j + 1) * BLK], writes=[mixT])
            P.dma("sp", ht[:, :, :], hv[:, 3 * j:3 * j + 3, :], writes=[ht])
            C.act(sq[:, :, :], mixT[:, 4:8, :], AF.Square, [mixT], [sq])
            for c in range(4):
                C.mm(C.pf(6, BLK), ones[:, :], sq[:, c, :], c == 0, c == 3, [ones, sq], [C.bank[6]])
            C.rstd(rstd_r[:, :], C.pf(6, BLK), 1.0, [C.bank[6]], [rstd_r])
            for c in range(4):
                C.stt(mixn[:, c, :], mixT[:, 4 + c, :], gc[:, c:c + 1], rstd_r[:, :], ALU.mult, ALU.mult,
                      [mixT, gc, rstd_r], [mixn])
            for t in range(3):
                for n in range(2):
                    for c in range(8):
                        lhsT = mixT[:, c, t * 128:(t + 1) * 128] if c < 4 else mixn[:, c - 4, t * 128:(t + 1) * 128]
                        C.mm(C.pf(n), lhsT, Wout[:, c, n * 512:(n + 1) * 512], c == 0, c == 7,
                             [mixT, mixn, Wout], [C.bank[n]])
                for n in range(2):
                    C.tt("dve", ht[:, t, n * 512:(n + 1) * 512], ht[:, t, n * 512:(n + 1) * 512], C.pf(n), ALU.add,
                         [ht, C.bank[n]], [ht])
            emit_norm_T(C, ht, 3, _ColView(gc, 4), hnT, ident, 6, scratch, ss, hn)
            for f in range(NF):
                bg = 2 + 2 * (f % 2)
                bu = bg + 1
                for c in range(8):
                    C.mm(C.pf(bg, BLK), Wgu[:, c, f * 128:(f + 1) * 128], hnT[:, c, :], c == 0, c == 7,
                         [Wgu, hnT], [C.bank[bg]])
                for c in range(8):
                    C.mm(C.pf(bu, BLK), Wgu[:, c, D_FF + f * 128:D_FF + (f + 1) * 128], hnT[:, c, :], c == 0, c == 7,
                         [Wgu, hnT], [C.bank[bu]])
                C.act(sg[:, f % 2, :], C.pf(bg, BLK), AF.Silu, [C.bank[bg]], [sg])
                C.tt("dve", actT[:, f, :], sg[:, f % 2, :], C.pf(bu, BLK), ALU.mult, [sg, C.bank[bu]], [actT])
            for t in range(3):
                for n in range(2):
                    for f in range(NF):
                        C.mm(C.pf(n), actT[:, f, t * 128:(t + 1) * 128], Wdn[:, f, n * 512:(n + 1) * 512],
                             f == 0, f == NF - 1, [actT, Wdn], [C.bank[n]])
                for n in range(2):
                    C.tt("dve", ht[:, t, n * 512:(n + 1) * 512], ht[:, t, n * 512:(n + 1) * 512], C.pf(n), ALU.add,
                         [ht, C.bank[n]], [ht])
            P.dma("sp", hov[:, 3 * j:3 * j + 3, :], ht[:, :, :], reads=[ht])
            if not last:
                emit_norm_T(C, ht, 3, _ColView(gc, 12), nxT, ident, 6, scratch, ss, hn)
                P.dma("sp", ov[:, :, j * BLK:(j + 1) * BLK], nxT[:, :, :], reads=[nxT])
        P.emit()
        print("post stats", P.stats)
    return nc


class _ColView:
    def __init__(self, base, off):
        self.base = base
        self.off = off

    @property
    def w(self):
        return self.base.w

    @w.setter
    def w(self, v):
        self.base.w = v

    @property
    def r(self):
        return self.base.r

    @r.setter
    def r(self, v):
        self.base.r = v

    def __getitem__(self, idx):
        p, c = idx
        start = (c.start or 0) + self.off
        stop = c.stop + self.off
        return self.base.t[p, start:stop]


def build_mix(T, layer):
    NB = T // BLK
    NT = T // 128
    lam_init = 0.8 - 0.6 * math.exp(-0.3 * layer)
    nc = bass.Bass("TRN2", target_bir_lowering=False)
    with contextlib.ExitStack() as st:
        C = Ctx(nc, st)
        P = C.P
        hnT_d = C.dram("hnT", [D_MODEL, T], BF16, "ExternalInput")
        w_in = C.dram("w_in", [D_MODEL, 1280], F32, "ExternalInput")
        gqk_d = C.dram("gqk", [128, 512], F32, "ExternalInput")
        lamv_d = C.dram("lamv", [128, 4, 64], F32, "ExternalInput")
        gsub_d = C.dram("gsub", [128, 384], F32, "ExternalInput")
        pc_d = C.dram("pc", [128, 2, 8], F32, "ExternalInput")
        wrg_d = C.dram("wrg", [4, 64, 64], F32, "ExternalInput")
        wig_d = C.dram("wig", [4, 64, 64], F32, "ExternalInput")
        cs_d = C.dram("cs", [128, NT, 16], F32, "ExternalInput")
        tri_d = C.dram("tri", [128, 128], BF16, "ExternalInput")
        ident_d = C.dram("ident", [128, 128], BF16, "ExternalInput")
        oattnT_o = C.dram("oattnT_o", [256, T], BF16, "ExternalOutput")
        orecT_o = C.dram("orecT_o", [256, T], BF16, "ExternalOutput")
        C.consts()
        emit_mix(C, T, lam_init, hnT_d, w_in, gqk_d, lamv_d, gsub_d, pc_d, wrg_d, wig_d, cs_d, tri_d, ident_d,
                 oattnT_o, orecT_o)
        P.emit()
        print("mix stats", P.stats)
    return nc


def emit_mix(C, T, lam_init, hnT_d, w_in, gqk_d, lamv_d, gsub_d, pc_d, wrg_d, wig_d, cs_d, tri_d, ident_d,
             oattnT_o, orecT_o, hooks=None):
    P = C.P
    NB = T // BLK
    NT = T // 128
    ident = C.sb("ident_sb", [128, 128], BF16)
    tri = C.sb("tri_sb", [128, 128], BF16)
    Gqk = C.sb("Gqk", [128, 512], F32)
    Gsub = C.sb("Gsub", [128, 3, 128], F32)
    lamt = C.sb("lamt", [128, 4, 64], F32)
    pcs = C.sb("pcs", [128, 2, 8], F32)
    CS = C.sb("CS", [128, NT, 16], F32)
    P.dma("sp", ident[:], ident_d, writes=[ident])
    P.dma("sp", tri[:], tri_d, writes=[tri])
    P.dma("sp", Gqk[:], gqk_d, writes=[Gqk])
    P.dma("sp", Gsub[:, :, :], gsub_d.rearrange("p (q e) -> p q e", e=128), writes=[Gsub])
    P.dma("sp", lamt[:], lamv_d, writes=[lamt])
    P.dma("sp", pcs[:], pc_d, writes=[pcs])
    P.dma("sp", CS[:], cs_d, writes=[CS])
    one_col = C.sb("one_col", [128, 1], F32)
    C.memset("dve", one_col[:], 1.0, [one_col])
    Win = C.sb("Win", [128, 8, 1280], BF16)
    load_cast_weight(C, Win, w_in, 8, 1280)
    WRG = C.sb("WRG", [128, 2, 128], BF16)
    WIG = C.sb("WIG", [128, 2, 128], BF16)
    C.memset("pool", WRG[:], 0.0, [WRG])
    C.memset("pool", WIG[:], 0.0, [WIG])
    for ct in range(2):
        for bl in range(2):
            P.dma("pool", WRG[bl * 64:(bl + 1) * 64, ct, bl * 64:(bl + 1) * 64], wrg_d[2 * ct + bl], writes=[WRG])
            P.dma("pool", WIG[bl * 64:(bl + 1) * 64, ct, bl * 64:(bl + 1) * 64], wig_d[2 * ct + bl], writes=[WIG])
    der = C.sb("der", [128, 2, 2], F32)
    lsl = C.sb("lsl", [128, 2], F32)
    C.act(lsl[:, :], pcs[:, :, 7], AF.Sigmoid, [pcs], [lsl])
    C.act(lsl[:, :], lsl[:, :], AF.Ln, [lsl], [lsl])
    C.ts("dve", der[:, :, 0], lsl[:, :], 8.0, None, ALU.mult, None, [lsl], [der])
    C.ts("dve", der[:, :, 1], lsl[:, :], 16.0, None, ALU.mult, None, [lsl], [der])
    lprod = C.sb("lprod", [128, 2, 64], F32)
    lsum = C.sb("lsum", [128, 2], F32)
    lam_col = C.sb("lam_col", [128, 1], F32)
    C.tt("dve", lprod[:, :, :], lamt[:, 0:2, :], lamt[:, 2:4, :], ALU.mult, [lamt], [lprod])
    C.red(lsum[:, :], lprod[:, :, :], [lprod], [lsum])
    C.act(lsum[:, :], lsum[:, :], AF.Exp, [lsum], [lsum])
    C.tt("dve", lam_col[:, :], lsum[:, 0:1], lsum[:, 1:2], ALU.subtract, [lsum], [lam_col])
    C.ts("dve", lam_col[:, :], lam_col[:, :], lam_init, None, ALU.add, None, [lam_col], [lam_col])
    C.ts("dve", Gsub[:, :, :], Gsub[:, :, :], 1.0 - lam_init, None, ALU.mult, None, [Gsub], [Gsub])

    KT = C.sb("KT", [128, 2, T], BF16)
    Vr = C.sb("Vr", [128, NT, 2, 130], BF16)
    KTb = [Buf("KT%d" % j) for j in range(NB)]
    Vrb = [Buf("Vr%d" % j) for j in range(NB)]
    for j in range(NB):
        C.memset("pool", Vr[:, 3 * j:3 * j + 3, :, 128:130], 1.0, [Vrb[j]])

    xTs = [C.sb("xT%d" % i, [128, 8, BLK], BF16) for i in range(2)]
    qkf = C.sb("qkf", [128, 3, 512], F32)
    sqq = C.sb("sqq", [128, 3, 512], F32)
    ssq = C.sb("ssq", [128, 24], F32)
    rt = C.sb("rt", [128, 4, 192], F32)
    qkb = C.sb("qkb", [128, 3, 512], BF16)
    QTs = [C.sb("QT%d" % i, [128, 2, BLK], BF16) for i in range(2)]
    Xs = [[C.sb("X%d_%d" % (ct, i), [128, BLK + 3], F32) for i in range(2)] for ct in range(2)]
    Hs = [[C.sb("H%d_%d" % (ct, i), [128, BLK], F32) for i in range(2)] for ct in range(2)]

    def rt_tiles(ct, names):
        return {n: C.sb("%s%d" % (n, ct), [128, BLK], F32) for n in names}
    RT = [rt_tiles(ct, ["xc", "sqg", "ug", "gl", "r", "a", "m", "ig", "u"]) for ct in range(2)]
    xcb = [C.sb("xcb%d" % ct, [128, BLK], BF16) for ct in range(2)]
    yb = [[C.sb("yb%d_%d" % (ct, i), [128, BLK], BF16) for i in range(2)] for ct in range(2)]
    Pt = [[C.sb("Pt%d_%d" % (k, c), [128, BLK], BF16) for c in range(2)] for k in range(3)]
    lt = C.sb("lt", [128, 2, 3], F32)
    t1 = C.sb("t1", [128, 3, 128], F32)
    Dt = C.sb("Dt", [128, 3, 128], F32)
    sqd = C.sb("sqd", [128, 3, 128], F32)
    ssd = C.sb("ssd", [128, 3], F32)
    ob = C.sb("ob", [128, 3, 128], BF16)
    oTs = [C.sb("oT%d" % i, [128, BLK], BF16) for i in range(2)]

    bank = C.bank
    if hooks is None:
        hv = hnT_d.rearrange("(c p) t -> p c t", p=128)

        def load_x(j):
            xT = xTs[j % 2]
            P.dma("sp", xT[:, :, :], hv[:, :, j * BLK:(j + 1) * BLK], writes=[xT])

        def store_orec(j, ct, Y):
            P.dma("sp", orecT_o[ct * 128:(ct + 1) * 128, j * BLK:(j + 1) * BLK], Y[:, :], reads=[Y])

        def store_oattn(j, hl, oT):
            P.dma("sp", oattnT_o[hl * 128:(hl + 1) * 128, j * BLK:(j + 1) * BLK], oT[:, :], reads=[oT])

        def end_block(j):
            pass
    else:
        def load_x(j):
            hooks["load_x"](j, xTs[j % 2])
        store_orec = hooks["store_orec"]
        store_oattn = hooks["store_oattn"]
        end_block = hooks["end_block"]

    load_x(0)
    ocount = [0]
    for j in range(NB):
        xT = xTs[j % 2]
        QT = QTs[j % 2]
        if j + 1 < NB:
            load_x(j + 1)
        for t in range(3):
            b0 = 0 if t % 2 == 0 else 2
            b1 = b0 + 1
            for c in range(8):
                C.mm(C.pf(b0), xT[:, c, t * 128:(t + 1) * 128], Win[:, c, 0:512], c == 0, c == 7, [xT, Win], [bank[b0]])
            for c in range(8):
                C.mm(C.pf(b1, 256), xT[:, c, t * 128:(t + 1) * 128], Win[:, c, 512:768], c == 0, c == 7,
                     [xT, Win], [bank[b1]])
            C.cp("act", qkf[:, t, :], C.pf(b0), [bank[b0]], [qkf])
            C.cp("dve", Vr[:, 3 * j + t, :, 0:128], C.pf(b1, 256).rearrange("p (h e) -> p h e", e=128),
                 [bank[b1]], [Vrb[j]])
        C.act(sqq[:, :, :], qkf[:, :, :], AF.Square, [qkf], [sqq])
        C.red(ssq[:, :], sqq[:, :, :].rearrange("p t (g d) -> p (t g) d", d=64), [sqq], [ssq])
        C.rstd(ssq[:, :], ssq[:, :], 64.0, [ssq], [ssq])
        qg = qkf[:, :, :].rearrange("p t (g d) -> p (t g) d", d=64)
        C.tt("dve", qg, qg, ssq[:, :].rearrange("p (g o) -> p g o", o=1).to_broadcast([128, 24, 64]), ALU.mult,
             [qkf, ssq], [qkf])
        C.tt("dve", qkf[:, :, :], qkf[:, :, :], Gqk[:, :].rearrange("p (o n) -> p o n", o=1).to_broadcast([128, 3, 512]),
             ALU.mult, [qkf, Gqk], [qkf])
        qv = qkf[:, :, :].rearrange("p t (g d) -> p t g d", d=64)
        qbv = qkb[:, :, :].rearrange("p t (g d) -> p t g d", d=64)
        x1 = qv[:, :, :, 0:8]
        x2 = qv[:, :, :, 8:16]
        cosb = CS[:, 3 * j:3 * j + 3, 0:8].rearrange("p t (o d) -> p t o d", o=1).to_broadcast([128, 3, 8, 8])
        sinb = CS[:, 3 * j:3 * j + 3, 8:16].rearrange("p t (o d) -> p t o d", o=1).to_broadcast([128, 3, 8, 8])

        def rtv(k):
            return rt[:, k, :].rearrange("p (t g d) -> p t g d", t=3, g=8)
        C.tt("pool", rtv(0), x1, cosb, ALU.mult, [qkf, CS], [rt])
        C.tt("pool", rtv(1), x2, sinb, ALU.mult, [qkf, CS], [rt])
        C.tt("pool", rtv(2), x2, cosb, ALU.mult, [qkf, CS], [rt])
        C.tt("pool", rtv(3), x1, sinb, ALU.mult, [qkf, CS], [rt])
        C.tt("pool", qbv[:, :, :, 0:8], rtv(0), rtv(1), ALU.subtract, [rt], [qkb])
        C.tt("pool", qbv[:, :, :, 8:16], rtv(2), rtv(3), ALU.add, [rt], [qkb])
        C.cp("act", qbv[:, :, :, 16:64], qv[:, :, :, 16:64], [qkf], [qkb])
        pq = C.pb(6)
        pk = C.pb(7)
        for t in range(3):
            for g4 in range(4):
                dstb = pq if g4 < 2 else pk
                hl = g4 % 2
                C.tr(dstb[:, hl * BLK + t * 128: hl * BLK + (t + 1) * 128], qkb[:, t, g4 * 128:(g4 + 1) * 128], ident,
                     [qkb], [bank[6 if g4 < 2 else 7]])
        C.cp("act", QT[:, :, :], pq[:, 0:2 * BLK].rearrange("p (h t) -> p h t", h=2), [bank[6]], [QT])
        C.cp("dve", KT[:, :, j * BLK:(j + 1) * BLK], pk[:, 0:2 * BLK].rearrange("p (h t) -> p h t", h=2),
             [bank[7]], [KTb[j]])
        for ct in range(2):
            R = RT[ct]
            X = Xs[ct][j % 2]
            Xp = Xs[ct][(j + 1) % 2]
            H = Hs[ct][j % 2]
            Hp = Hs[ct][(j + 1) % 2]
            Y = yb[ct][j % 2]
            bx, bg, br, bi = 4 + ct, 0 + ct, 2 + ct, 6 + ct
            for c in range(8):
                C.mm(C.pf(bx, BLK), Win[:, c, 768 + ct * 128:768 + (ct + 1) * 128], xT[:, c, :], c == 0, c == 7,
                     [Win, xT], [bank[bx]])
            for c in range(8):
                C.mm(C.pf(bg, BLK), Win[:, c, 1024 + ct * 128:1024 + (ct + 1) * 128], xT[:, c, :], c == 0, c == 7,
                     [Win, xT], [bank[bg]])
            if j == 0:
                C.memset("pool", X[:, 0:3], 0.0, [X])
            else:
                C.cp("pool", X[:, 0:3], Xp[:, BLK:BLK + 3], [Xp], [X])
            C.cp("act", X[:, 3:BLK + 3], C.pf(bx, BLK), [bank[bx]], [X])
            xc = R["xc"]
            C.ts("dve", xc[:, :], X[:, 3:BLK + 3], pcs[:, ct, 3:4], pcs[:, ct, 4:5], ALU.mult, ALU.add, [X, pcs], [xc])
            for i in (2, 1, 0):
                C.stt(xc[:, :], X[:, i:i + BLK], pcs[:, ct, i:i + 1], xc[:, :], ALU.mult, ALU.add, [X, pcs, xc], [xc])
            C.cp("pool", xcb[ct][:, :], xc[:, :], [xc], [xcb[ct]])
            gps = C.pf(bg, BLK)
            C.act(R["sqg"][:, :], gps, AF.Square, [bank[bg]], [R["sqg"]])
            C.ts("pool", R["sqg"][:, :], R["sqg"][:, :], 0.044715, 1.0, ALU.mult, ALU.add, [R["sqg"]], [R["sqg"]])
            C.tt("dve", R["ug"][:, :], R["sqg"][:, :], gps, ALU.mult, [R["sqg"], bank[bg]], [R["ug"]])
            C.act(R["ug"][:, :], R["ug"][:, :], AF.Sigmoid, [R["ug"]], [R["ug"]], scale=GELU_C)
            C.tt("dve", R["gl"][:, :], R["ug"][:, :], gps, ALU.mult, [R["ug"], bank[bg]], [R["gl"]])
            C.mm(C.pf(br, BLK), WRG[:, ct, :], xcb[ct][:, :], True, True, [WRG, xcb[ct]], [bank[br]])
            C.mm(C.pf(bi, BLK), WIG[:, ct, :], xcb[ct][:, :], True, True, [WIG, xcb[ct]], [bank[bi]])
            C.act(R["r"][:, :], C.pf(br, BLK), AF.Sigmoid, [bank[br], pcs], [R["r"]], bias=pcs[:, ct, 5:6])
            C.act(R["ig"][:, :], C.pf(bi, BLK), AF.Sigmoid, [bank[bi], pcs], [R["ig"]], bias=pcs[:, ct, 6:7])
            C.act(R["a"][:, :], R["r"][:, :], AF.Exp, [R["r"], der], [R["a"]], scale=der[:, ct, 0:1])
            C.act(R["m"][:, :], R["r"][:, :], AF.Exp, [R["r"], der], [R["m"]], scale=der[:, ct, 1:2])
            C.act(R["m"][:, :], R["m"][:, :], AF.Sqrt, [R["m"], one_col], [R["m"]], scale=-1.0, bias=one_col[:, 0:1])
            if j == 0:
                C.memset("pool", R["m"][:, 0:1], 1.0, [R["m"]])
            C.tt("pool", R["u"][:, :], R["ig"][:, :], xc[:, :], ALU.mult, [R["ig"], xc], [R["u"]])
            C.tt("pool", R["u"][:, :], R["u"][:, :], R["m"][:, :], ALU.mult, [R["u"], R["m"]], [R["u"]])
            a_t, u_t = R["a"], R["u"]
            if j == 0:
                C.scan(H[:, :], a_t[:, :], u_t[:, :], 0.0, [a_t, u_t], [H])
            else:
                C.scan(H[:, :], a_t[:, :], u_t[:, :], Hp[:, BLK - 1:BLK], [a_t, u_t, Hp], [H])
            C.tt("pool", Y[:, :], H[:, :], R["gl"][:, :], ALU.mult, [H, R["gl"]], [Y])
            store_orec(j, ct, Y)
        ntile = 3 * j + 3
        for hl in range(2):
            def q0_of(i):
                return max(0, i - 3 * j) * 128

            def emit_qk(i):
                s = i % 2
                q0 = q0_of(i)
                for c in range(2):
                    b = 2 * s + c
                    C.mm(C.pf(b, BLK)[:, q0:BLK], KT[64 * c:64 * c + 64, hl, i * 128:(i + 1) * 128],
                         QT[64 * c:64 * c + 64, hl, q0:BLK], True, True, [KTb[i // 3], QT], [bank[b]])

            def emit_exp_av(i):
                s = i % 2
                q0 = q0_of(i)
                for c in range(2):
                    pt = Pt[i % 3][c]
                    C.act(pt[:, q0:BLK], C.pf(2 * s + c, BLK)[:, q0:BLK], AF.Exp, [bank[2 * s + c]], [pt], scale=0.125)
                    if i >= 3 * j:
                        C.tt("pool", pt[:, q0:q0 + 128], pt[:, q0:q0 + 128], tri[:, :], ALU.mult, [pt, tri], [pt])
                for c in range(2):
                    pt = Pt[i % 3][c]
                    ov = C.pf(4 + c, 390).rearrange("p (q e) -> p q e", e=130)
                    for qc in range(q0 // 128, 3):
                        first = (i == 0 and qc == 0)
                        lastk = (i == 3 * j + qc)
                        C.mm(ov[:, qc, 0:129], pt[:, qc * 128:(qc + 1) * 128], Vr[:, i, hl, 0:129], first, lastk,
                             [pt, Vrb[i // 3]], [bank[4 + c]], skip_group_check=True)

            emit_qk(0)
            for i in range(ntile):
                if i + 1 < ntile:
                    emit_qk(i + 1)
                emit_exp_av(i)
            ov0 = C.pf(4, 390).rearrange("p (q e) -> p q e", e=130)
            ov1 = C.pf(5, 390).rearrange("p (q e) -> p q e", e=130)
            C.cp("dve", lt[:, 0, :], ov0[:, :, 128], [bank[4]], [lt])
            C.cp("dve", lt[:, 1, :], ov1[:, :, 128], [bank[5]], [lt])
            C.recip(lt[:, :, :], lt[:, :, :], [lt], [lt])
            C.ts("dve", lt[:, 1, :], lt[:, 1, :], lam_col[:, 0:1], None, ALU.mult, None, [lt, lam_col], [lt])
            C.tt("dve", t1[:, :, :], ov1[:, :, 0:128],
                 lt[:, 1, :].rearrange("p (q o) -> p q o", o=1).to_broadcast([128, 3, 128]), ALU.mult,
                 [bank[5], lt], [t1])
            C.tt("dve", Dt[:, :, :], ov0[:, :, 0:128],
                 lt[:, 0, :].rearrange("p (q o) -> p q o", o=1).to_broadcast([128, 3, 128]), ALU.mult,
                 [bank[4], lt], [Dt])
            C.tt("dve", Dt[:, :, :], Dt[:, :, :], t1[:, :, :], ALU.subtract, [Dt, t1], [Dt])
            C.act(sqd[:, :, :], Dt[:, :, :], AF.Square, [Dt], [sqd])
            C.red(ssd[:, :], sqd[:, :, :], [sqd], [ssd])
            C.rstd(ssd[:, :], ssd[:, :], 128.0, [ssd], [ssd])
            C.tt("dve", Dt[:, :, :], Dt[:, :, :],
                 ssd[:, :].rearrange("p (q o) -> p q o", o=1).to_broadcast([128, 3, 128]), ALU.mult, [Dt, ssd], [Dt])
            C.tt("dve", ob[:, :, :], Dt[:, :, :], Gsub[:, :, :], ALU.mult, [Dt, Gsub], [ob])
            po = C.pb(6)
            for qc in range(3):
                C.tr(po[:, qc * 128:(qc + 1) * 128], ob[:, qc, :], ident, [ob], [bank[6]])
            oT = oTs[ocount[0] % 2]
            ocount[0] += 1
            C.cp("act", oT[:, :], po[:, 0:BLK], [bank[6]], [oT])
            store_oattn(j, hl, oT)
        end_block(j)


def _colform(v, n):
    return np.ascontiguousarray(np.asarray(v, np.float32).reshape(n, 128).T)


def _const_tables(T):
    NT = T // 128
    inv = (ROPE_THETA ** (-np.arange(0, 16, 2, dtype=np.float32) / 16.0)).astype(np.float32)
    ang = np.arange(T, dtype=np.float32)[:, None] * inv[None, :]
    cs = np.concatenate([np.cos(ang), np.sin(ang)], 1).astype(np.float32)
    cs = np.ascontiguousarray(cs.reshape(NT, 128, 16).transpose(1, 0, 2))
    tri = (np.arange(128)[None, :] >= np.arange(128)[:, None]).astype(np.float32).astype(ml_dtypes.bfloat16)
    ident = np.eye(128, dtype=np.float32).astype(ml_dtypes.bfloat16)
    return cs, tri, ident


def _mix_inputs(inp, l, p, cs, tri, ident):
    f = np.float32
    sl = slice(256 * p, 256 * p + 256)
    w = inp["w_in"][l]
    w_core = np.ascontiguousarray(np.concatenate(
        [w[:, 0:512][:, sl], w[:, 512:1024][:, sl], w[:, 1024:1536][:, sl], w[:, 1536:2048][:, sl],
         w[:, 2048:2560][:, sl]], axis=1).astype(f))
    gq, gk = inp["q_norm_g"][l], inp["k_norm_g"][l]
    gqk = np.ascontiguousarray(np.broadcast_to(np.concatenate([np.tile(gq, 4), np.tile(gk, 4)])[None, :], (128, 512)).astype(f))
    lamv = np.ascontiguousarray(np.broadcast_to(
        np.stack([inp["lambda_q1"][l], inp["lambda_q2"][l], inp["lambda_k1"][l], inp["lambda_k2"][l]])[None],
        (128, 4, 64)).astype(f))
    gs = np.ascontiguousarray(np.broadcast_to(np.tile(inp["subln_g"][l], 3)[None, :], (128, 384)).astype(f))
    pc = np.zeros((128, 2, 8), f)
    for ct in range(2):
        c0 = 256 * p + ct * 128
        pc[:, ct, 0:4] = inp["conv_w"][l][:, c0:c0 + 128].T
        pc[:, ct, 4] = inp["conv_b"][l][c0:c0 + 128]
        pc[:, ct, 5] = inp["b_rg"][l][c0:c0 + 128]
        pc[:, ct, 6] = inp["b_ig"][l][c0:c0 + 128]
        pc[:, ct, 7] = inp["lru_L"][l][c0:c0 + 128]
    return dict(w_in=w_core, gqk=gqk, lamv=lamv, gsub=gs, pc=pc,
                wrg=np.ascontiguousarray(inp["w_rg"][l][4 * p:4 * p + 4].astype(f)),
                wig=np.ascontiguousarray(inp["w_ig"][l][4 * p:4 * p + 4].astype(f)),
                cs=cs, tri=tri, ident=ident)


def kernel_unfused(**inp):
    inp = {k: np.asarray(v) for k, v in inp.items()}
    x = inp["x"]
    B = x.shape[0]
    depth = inp["w_in"].shape[0]
    T = T_PAD
    Th = T // 2
    cores = [(b, p) for b in range(B) for p in range(2)]
    ncores = len(cores)
    cs, tri, ident = _const_tables(T)
    h = np.zeros((B, T, D_MODEL), np.float32)
    h[:, :N_META] = inp["meta_tokens"][None]
    h[:, N_META:N_META + SEQ] = x

    nc = build_pre(Th)
    gm = _colform(inp["norm_mix_g"][0], 8)
    maps = [dict(h_in=np.ascontiguousarray(h[b, p * Th:(p + 1) * Th]), gmix=gm, ident=ident) for b, p in cores]
    res = run_bass_kernel_spmd(nc, maps, core_ids=list(range(ncores))).results
    hnT = [np.concatenate([res[2 * b]["hnT_o"], res[2 * b + 1]["hnT_o"]], axis=1) for b in range(B)]
    for l in range(depth):
        last = l == depth - 1
        nc = build_mix(T, l)
        maps = []
        for b, p in cores:
            d = _mix_inputs(inp, l, p, cs, tri, ident)
            d["hnT"] = np.ascontiguousarray(hnT[b])
            maps.append(d)
        res = run_bass_kernel_spmd(nc, maps, core_ids=list(range(ncores))).results
        oattnT = [np.concatenate([res[2 * b]["oattnT_o"], res[2 * b + 1]["oattnT_o"]], axis=0) for b in range(B)]
        orecT = [np.concatenate([res[2 * b]["orecT_o"], res[2 * b + 1]["orecT_o"]], axis=0) for b in range(B)]
        nc = build_post(Th, last)
        gnext = inp["norm_mix_g"][l + 1] if not last else np.ones(D_MODEL, np.float32)
        gcols = np.ascontiguousarray(np.concatenate(
            [_colform(inp["rec_norm_g"][l], 4), _colform(inp["norm_ffn_g"][l], 8), _colform(gnext, 8)], axis=1))
        maps = []
        for b, p in cores:
            tsl = slice(p * Th, (p + 1) * Th)
            maps.append(dict(h_in=np.ascontiguousarray(h[b, tsl]),
                             oattnT=np.ascontiguousarray(oattnT[b][:, tsl]),
                             orecT=np.ascontiguousarray(orecT[b][:, tsl]),
                             w_out=np.ascontiguousarray(inp["w_out"][l].astype(np.float32)),
                             w_gu=np.ascontiguousarray(inp["w_gu"][l].astype(np.float32)),
                             w_down=np.ascontiguousarray(inp["w_down"][l].astype(np.float32)),
                             gcols=gcols, ident=ident))
        res = run_bass_kernel_spmd(nc, maps, core_ids=list(range(ncores))).results
        for i, (b, p) in enumerate(cores):
            h[b, p * Th:(p + 1) * Th] = res[i]["h_out"]
        if not last:
            hnT = [np.concatenate([res[2 * b]["hnT_o"], res[2 * b + 1]["hnT_o"]], axis=1) for b in range(B)]
    return np.ascontiguousarray(h[:, N_META:N_META + SEQ]).astype(np.float32)


NLB = (T_PAD // 2) // BLK
X1_CH = [(0, 2), (2, 2), (4, 2), (6, 2), (8, 2), (10, 1)]
PAIRS = [[0, 1], [2, 3], [4, 5], [6, 7]]
REC_POS = [2, 3, 6, 7]
ATT_POS = [0, 1, 4, 5]


def emit_pre_f(C, x_in, gmix_d, ident, store_nx, end_lblock):
    P = C.P
    gcol = C.sb("gcol", [128, 8], F32)
    P.dma("sp", gcol[:], gmix_d, writes=[gcol])
    hv = x_in.rearrange("(t p) d -> p t d", p=128)
    hts = [C.sb("h%d" % i, [128, 3, D_MODEL], F32) for i in range(2)]
    outs = [C.sb("o%d" % i, [128, 8, BLK], BF16) for i in range(2)]
    ss = C.sb("ss", [128, 4], F32)
    hn = C.sb("hn", [128, 3, D_MODEL], BF16)
    for j in range(NLB):
        ht = hts[j % 2]
        oT = outs[j % 2]
        P.dma("sp", ht[:, :, :], hv[:, 3 * j:3 * j + 3, :], writes=[ht])
        emit_norm_T(C, ht, 3, gcol, oT, ident, 0, None, ss, hn)
        store_nx(j, oT)
        end_lblock(j)


def emit_post_f(C, last, w_out, w_gu, w_down, gcols, ident, sel, load_ab, load_h, store_h, store_nx, end_lblock):
    P = C.P
    NF = D_FF // 128
    gc = C.sb("gc", [128, 20], F32)
    P.dma("sp", gc[:], gcols, writes=[gc])
    ones = C.sb("ones", [128, 128], BF16)
    C.memset("dve", ones[:], 1.0 / 512.0, [ones])
    Wout = C.sb("Wout", [128, 8, D_MODEL], BF16)
    Wgu = C.sb("Wgu", [128, 8, 2 * D_FF], BF16)
    Wdn = C.sb("Wdn", [128, NF, D_MODEL], BF16)
    load_cast_weight(C, Wout, w_out, 8, D_MODEL)
    load_cast_weight(C, Wgu, w_gu, 8, 2 * D_FF)
    load_cast_weight(C, Wdn, w_down, NF, D_MODEL)
    mixT = C.sb("mixT", [128, 8, BLK], BF16)
    mixA = mixT
    mixB = C.sb("mixB", [128, 8, BLK], BF16)
    ht = C.sb("ht", [128, 3, D_MODEL], F32)
    rstd_r = C.sb("rstd_r", [128, BLK], F32)
    mixn = C.sb("mixn", [128, 4, BLK], BF16)
    ss = C.sb("ss", [128, 4], F32)
    hn = C.sb("hn", [128, 3, D_MODEL], BF16)
    hnT = C.sb("hnT", [128, 8, BLK], BF16)
    sg = C.sb("sg", [128, 2, BLK], BF16)
    actT = C.sb("actT", [128, NF, BLK], BF16)
    nxT = hnT
    for j in range(NLB):
        load_ab(j, mixA, mixB)
        load_h(j, ht)
        C.ts("pool", mixT[:, :, :], mixA[:, :, :], sel[:, 0:1], None, ALU.mult, None, [mixA, sel], [mixT])
        C.stt(mixT[:, :, :], mixB[:, :, :], sel[:, 1:2], mixT[:, :, :], ALU.mult, ALU.add, [mixB, sel, mixT], [mixT])
        for k, c in enumerate(REC_POS):
            C.act(mixn[:, k, :], mixT[:, c, :], AF.Square, [mixT], [mixn])
        for k in range(4):
            C.mm(C.pf(6, BLK), ones[:, :], mixn[:, k, :], k == 0, k == 3, [ones, mixn], [C.bank[6]])
        C.rstd(rstd_r[:, :], C.pf(6, BLK), 1.0, [C.bank[6]], [rstd_r])
        for k, c in enumerate(REC_POS):
            C.stt(mixn[:, k, :], mixT[:, c, :], gc[:, k:k + 1], rstd_r[:, :], ALU.mult, ALU.mult,
                  [mixT, gc, rstd_r], [mixn])
        for t in range(3):
            for n in range(2):
                for c in range(8):
                    if c in REC_POS:
                        lhsT = mixn[:, REC_POS.index(c), t * 128:(t + 1) * 128]
                    else:
                        lhsT = mixT[:, c, t * 128:(t + 1) * 128]
                    C.mm(C.pf(n), lhsT, Wout[:, c, n * 512:(n + 1) * 512], c == 0, c == 7,
                         [mixT, mixn, Wout], [C.bank[n]])
            for n in range(2):
                C.tt("dve", ht[:, t, n * 512:(n + 1) * 512], ht[:, t, n * 512:(n + 1) * 512], C.pf(n), ALU.add,
                     [ht, C.bank[n]], [ht])
        emit_norm_T(C, ht, 3, _ColView(gc, 4), hnT, ident, 6, None, ss, hn)
        for f in range(NF):
            bg = 2 + 2 * (f % 2)
            bu = bg + 1
            for c in range(8):
                C.mm(C.pf(bg, BLK), Wgu[:, c, f * 128:(f + 1) * 128], hnT[:, c, :], c == 0, c == 7,
                     [Wgu, hnT], [C.bank[bg]])
            for c in range(8):
                C.mm(C.pf(bu, BLK), Wgu[:, c, D_FF + f * 128:D_FF + (f + 1) * 128], hnT[:, c, :], c == 0, c == 7,
                     [Wgu, hnT], [C.bank[bu]])
            C.act(sg[:, f % 2, :], C.pf(bg, BLK), AF.Silu, [C.bank[bg]], [sg])
            C.tt("dve", actT[:, f, :], sg[:, f % 2, :], C.pf(bu, BLK), ALU.mult, [sg, C.bank[bu]], [actT])
        for t in range(3):
            for n in range(2):
                for f in range(NF):
                    C.mm(C.pf(n), actT[:, f, t * 128:(t + 1) * 128], Wdn[:, f, n * 512:(n + 1) * 512],
                         f == 0, f == NF - 1, [actT, Wdn], [C.bank[n]])
            for n in range(2):
                C.tt("dve", ht[:, t, n * 512:(n + 1) * 512], ht[:, t, n * 512:(n + 1) * 512], C.pf(n), ALU.add,
                     [ht, C.bank[n]], [ht])
        store_h(j, ht)
        if not last:
            emit_norm_T(C, ht, 3, _ColView(gc, 12), nxT, ident, 6, None, ss, hn)
            store_nx(j, nxT)
            end_lblock(j)


def build_fused(depth=2):
    T = T_PAD
    Th = T // 2
    NT = T // 128
    nc = bass.Bass("TRN2", target_bir_lowering=False)
    with contextlib.ExitStack() as st:
        C = Ctx(nc, st)
        P = C.P
        IN = "ExternalInput"
        x_in = C.dram("x_in", [Th, D_MODEL], F32, IN)
        sel_d = C.dram("sel", [128, 2], F32, IN)
        gmix0 = C.dram("gmix0", [128, 8], F32, IN)
        cs_d = C.dram("cs", [128, NT, 16], F32, IN)
        tri_d = C.dram("tri", [128, 128], BF16, IN)
        ident_d = C.dram("ident", [128, 128], BF16, IN)
        L = []
        for l in range(depth):
            L.append(dict(
                w_in=C.dram("w_in%d" % l, [D_MODEL, 1280], F32, IN),
                gqk=C.dram("gqk%d" % l, [128, 512], F32, IN),
                lamv=C.dram("lamv%d" % l, [128, 4, 64], F32, IN),
                gsub=C.dram("gsub%d" % l, [128, 384], F32, IN),
                pc=C.dram("pc%d" % l, [128, 2, 8], F32, IN),
                wrg=C.dram("wrg%d" % l, [4, 64, 64], F32, IN),
                wig=C.dram("wig%d" % l, [4, 64, 64], F32, IN),
                w_out=C.dram("w_out%d" % l, [1024, D_MODEL], F32, IN),
                w_gu=C.dram("w_gu%d" % l, [D_MODEL, 2 * D_FF], F32, IN),
                w_down=C.dram("w_down%d" % l, [D_FF, D_MODEL], F32, IN),
                gcols=C.dram("gcols%d" % l, [128, 20], F32, IN)))
        h_out = C.dram("h_out", [Th, D_MODEL], F32, "ExternalOutput")
        hres = nc.dram_tensor("hres", [Th, D_MODEL], F32).ap()
        hres_b = [Buf("hres%d" % j) for j in range(NLB)]
        x1s = [nc.dram_tensor("x1s%d" % i, [D_MODEL, n * BLK], BF16).ap() for i, (b0, n) in enumerate(X1_CH)]
        x1d = [nc.dram_tensor("x1d%d" % i, [2 * D_MODEL, n * BLK], BF16).ap() for i, (b0, n) in enumerate(X1_CH)]
        x1s_b = [Buf("x1s%d" % i) for i in range(len(X1_CH))]
        x1d_b = [Buf("x1d%d" % i) for i in range(len(X1_CH))]
        NX2 = (T // BLK) // 2
        x2s = [nc.dram_tensor("x2s%d" % i, [512, 2 * BLK], BF16).ap() for i in range(NX2)]
        x2d = [nc.dram_tensor("x2d%d" % i, [1024, 2 * BLK], BF16).ap() for i in range(NX2)]
        x2s_b = [Buf("x2s%d" % i) for i in range(NX2)]
        x2d_b = [Buf("x2d%d" % i) for i in range(NX2)]

        C.consts()
        ident = C.sb("ident_sb", [128, 128], BF16)
        sel = C.sb("sel_sb", [128, 2], F32)
        P.dma("sp", ident[:], ident_d, writes=[ident])
        P.dma("sp", sel[:], sel_d, writes=[sel])
        C.use_arena(53100)

        def x1_chunk_of(jl):
            for i, (b0, n) in enumerate(X1_CH):
                if b0 <= jl < b0 + n:
                    return i, jl - b0, n
            raise AssertionError

        def store_nx(jl, oT):
            i, k, n = x1_chunk_of(jl)
            P.dma("sp", x1s[i].rearrange("(c p) t -> p c t", p=128)[:, :, k * BLK:(k + 1) * BLK], oT[:, :, :],
                  reads=[oT], writes=[x1s_b[i]])

        def end_lblock(jl):
            i, k, n = x1_chunk_of(jl)
            if k == n - 1:
                src, dst = x1s[i], x1d[i]
                P.cc(lambda e: e.collective_compute("AllGather", ALU.bypass, replica_groups=PAIRS,
                                                    ins=[src.opt()], outs=[dst.opt()]),
                     reads=[x1s_b[i]], writes=[x1d_b[i]])

        def mix_load_x(j, xT):
            half, jl = j // NLB, j % NLB
            i, k, n = x1_chunk_of(jl)
            v = x1d[i].rearrange("(r c p) t -> r p c t", r=2, p=128)
            P.dma("sp", xT[:, :, :], v[half][:, :, k * BLK:(k + 1) * BLK], reads=[x1d_b[i]], writes=[xT])

        def store_orec(j, ct, Y):
            c, k = j // 2, j % 2
            P.dma("sp", x2s[c][256 + ct * 128:256 + (ct + 1) * 128, k * BLK:(k + 1) * BLK], Y[:, :],
                  reads=[Y], writes=[x2s_b[c]])

        def store_oattn(j, hl, oT):
            c, k = j // 2, j % 2
            P.dma("sp", x2s[c][hl * 128:(hl + 1) * 128, k * BLK:(k + 1) * BLK], oT[:, :],
                  reads=[oT], writes=[x2s_b[c]])

        def mix_end_block(j):
            if j % 2 == 1:
                c = j // 2
                src, dst = x2s[c], x2d[c]
                P.cc(lambda e: e.collective_compute("AllGather", ALU.bypass, replica_groups=PAIRS,
                                                    ins=[src.opt()], outs=[dst.opt()]),
                     reads=[x2s_b[c]], writes=[x2d_b[c]])

        def load_ab(jl, mixA, mixB):
            for tile_, j in ((mixA, jl), (mixB, NLB + jl)):
                c, k = j // 2, j % 2
                v = x2d[c].rearrange("(q p) t -> p q t", p=128)
                P.dma("sp", tile_[:, :, :], v[:, :, k * BLK:(k + 1) * BLK], reads=[x2d_b[c]], writes=[tile_])

        keep = C.arena_off
        emit_pre_f(C, x_in, gmix0, ident, store_nx, end_lblock)
        for l in range(depth):
            last = l == depth - 1
            lam_init = 0.8 - 0.6 * math.exp(-0.3 * l)
            P.barrier()
            C.arena_reset(keep)
            emit_mix(C, T, lam_init, None, L[l]["w_in"], L[l]["gqk"], L[l]["lamv"], L[l]["gsub"], L[l]["pc"],
                     L[l]["wrg"], L[l]["wig"], cs_d, tri_d, ident_d, None, None,
                     hooks=dict(load_x=mix_load_x, store_orec=store_orec, store_oattn=store_oattn,
                                end_block=mix_end_block))
            P.barrier()
            C.arena_reset(keep)
            src_h = x_in if l == 0 else hres
            dst_h = h_out if last else hres

            def load_h(jl, ht, src_h=src_h, l=l):
                rd = [] if l == 0 else [hres_b[jl]]
                P.dma("sp", ht[:, :, :], src_h.rearrange("(t p) d -> p t d", p=128)[:, 3 * jl:3 * jl + 3, :],
                      reads=rd, writes=[ht])

            def store_h(jl, ht, dst_h=dst_h, last=last):
                wr = [] if last else [hres_b[jl]]
                P.dma("sp", dst_h.rearrange("(t p) d -> p t d", p=128)[:, 3 * jl:3 * jl + 3, :], ht[:, :, :],
                      reads=[ht], writes=wr)

            emit_post_f(C, last, L[l]["w_out"], L[l]["w_gu"], L[l]["w_down"], L[l]["gcols"], ident, sel,
                        load_ab, load_h, store_h, store_nx, end_lblock)
        P.emit()
        print("fused stats", P.stats)
    return nc


def kernel(**inp):
    inp = {k: np.asarray(v) for k, v in inp.items()}
    x = inp["x"]
    B = x.shape[0]
    depth = inp["w_in"].shape[0]
    T = T_PAD
    Th = T // 2
    cores = [(b, p) for b in range(B) for p in range(2)]
    cs, tri, ident = _const_tables(T)
    h = np.zeros((B, T, D_MODEL), np.float32)
    h[:, :N_META] = inp["meta_tokens"][None]
    h[:, N_META:N_META + SEQ] = x
    nc = build_fused(depth)
    f = np.float32
    shared = {}
    for l in range(depth):
        last = l == depth - 1
        gnext = inp["norm_mix_g"][l + 1] if not last else np.ones(D_MODEL, f)
        shared["gcols%d" % l] = np.ascontiguousarray(np.concatenate(
            [_colform(inp["rec_norm_g"][l], 4), _colform(inp["norm_ffn_g"][l], 8), _colform(gnext, 8)], axis=1))
        wo = inp["w_out"][l].astype(f)
        shared["w_out%d" % l] = np.ascontiguousarray(np.concatenate([wo[0:256], wo[512:768], wo[256:512], wo[768:1024]], 0))
        shared["w_gu%d" % l] = np.ascontiguousarray(inp["w_gu"][l].astype(f))
        shared["w_down%d" % l] = np.ascontiguousarray(inp["w_down"][l].astype(f))
    gm0 = _colform(inp["norm_mix_g"][0], 8)
    maps = []
    for b, p in cores:
        d = dict(shared)
        d["x_in"] = np.ascontiguousarray(h[b, p * Th:(p + 1) * Th])
        selv = np.zeros((128, 2), f)
        selv[:, p] = 1.0
        d["sel"] = selv
        d["gmix0"] = gm0
        d["cs"] = cs
        d["tri"] = tri
        d["ident"] = ident
        for l in range(depth):
            m = _mix_inputs(inp, l, p, cs, tri, ident)
            for k in ("w_in", "gqk", "lamv", "gsub", "pc", "wrg", "wig"):
                d["%s%d" % (k, l)] = m[k]
        maps.append(d)
    res = run_bass_kernel_spmd(nc, maps, core_ids=list(range(len(cores)))).results
    out = np.zeros((B, T, D_MODEL), np.float32)
    for i, (b, p) in enumerate(cores):
        out[b, p * Th:(p + 1) * Th] = res[i]["h_out"]
    return np.ascontiguousarray(out[:, N_META:N_META + SEQ])
```

```python
import contextlib
import math
import numpy as np
import ml_dtypes
import concourse.bass as bass
import concourse.mybir as mybir
from concourse.bass_utils import run_bass_kernel_spmd

F32 = mybir.dt.float32
BF16 = mybir.dt.bfloat16
AF = mybir.ActivationFunctionType
ALU = mybir.AluOpType
AX = mybir.AxisListType

D_MODEL = 1024
N_META = 16
SEQ = 8192
T_REAL = SEQ + N_META
BLK = 384
T_PAD = 8448
D_FF = 2816
EPS = 1e-6
ROPE_THETA = 500000.0
GELU_C = 1.5957691216057308

COMPUTE = ("pe", "act", "dve", "pool")
QUEUES = ("sp", "act", "pool")


class Buf:
    __slots__ = ("w", "r", "name")

    def __init__(self, name=""):
        self.w = []
        self.r = []
        self.name = name


class Tile(Buf):
    __slots__ = ("t",)

    def __init__(self, t, name=""):
        Buf.__init__(self, name)
        self.t = t

    def __getitem__(self, idx):
        return self.t[idx]


class Op:
    __slots__ = ("eng", "fn", "deps", "seq", "is_dma", "ring", "cnt", "clock",
                 "waits", "signal", "sigidx", "inc")

    def __init__(self, eng, fn, is_dma):
        self.eng = eng
        self.fn = fn
        self.deps = []
        self.is_dma = is_dma
        self.ring = None
        self.cnt = 0
        self.waits = []
        self.signal = False
        self.sigidx = 0
        self.clock = None
        self.inc = 16


class Prog:
    def __init__(self, nc, ring_sizes=None):
        self.nc = nc
        self.ops = {e: [] for e in ("pe", "act", "dve", "pool", "sp")}
        self.all = []
        self.ring_sizes = ring_sizes or {"sp": 16, "act": 8, "pool": 12}
        self.dma_n = {q: 0 for q in QUEUES}
        self.ring_last = {q: [None] * self.ring_sizes[q] for q in QUEUES}
        self.ring_cnt = {q: [0] * self.ring_sizes[q] for q in QUEUES}
        self.cc_last = None
        self.cc_n = 0

    def _deps(self, op, reads, writes, par=False):
        deps = []
        for t in reads:
            deps.extend(t.w)
        for t in writes:
            if par and not t.r:
                continue
            deps.extend(t.w)
            deps.extend(t.r)
        for t in reads:
            t.r.append(op)
        for t in writes:
            if par and not t.r:
                t.w = t.w + [op]
            else:
                t.w = [op]
                t.r = []
        seen = set()
        out = []
        for d in deps:
            if id(d) not in seen and d is not op:
                seen.add(id(d))
                out.append(d)
        op.deps = out

    def _add(self, o):
        o.seq = len(self.ops[o.eng])
        self.ops[o.eng].append(o)
        self.all.append(o)
        return o

    def op(self, eng, fn, reads=(), writes=()):
        o = Op(eng, fn, False)
        self._deps(o, reads, writes)
        return self._add(o)

    def dma(self, q, out_ap, in_ap, reads=(), writes=(), par=False, **kw):
        def fn(eng):
            return eng.dma_start(out=out_ap, in_=in_ap, **kw)
        return self.dma_like(q, fn, reads, writes, par)

    def dma_like(self, q, fn, reads=(), writes=(), par=False):
        o = Op(q, fn, True)
        self._deps(o, reads, writes, par)
        n = self.dma_n[q]
        self.dma_n[q] = n + 1
        slot = n % self.ring_sizes[q]
        prev = self.ring_last[q][slot]
        if prev is not None:
            o.deps.append(prev)
        self.ring_last[q][slot] = o
        self.ring_cnt[q][slot] += 16
        o.ring = (q, slot)
        o.cnt = self.ring_cnt[q][slot]
        return self._add(o)

    def cc(self, fn, reads=(), writes=()):
        o = Op("pool", fn, True)
        self._deps(o, reads, writes)
        if self.cc_last is not None:
            o.deps.append(self.cc_last)
        self.cc_last = o
        self.cc_n += 1
        o.ring = ("cc", 0)
        o.cnt = self.cc_n
        o.inc = 1
        return self._add(o)

    def barrier(self):
        lasts = []
        for e in self.ops:
            if self.ops[e]:
                lasts.append(self.ops[e][-1])
        for q in QUEUES:
            for o in self.ring_last[q]:
                if o is not None:
                    lasts.append(o)
        if self.cc_last is not None:
            lasts.append(self.cc_last)
        for e in ("pe", "act", "dve", "pool", "sp"):
            o = Op(e, None, False)
            o.deps = [d for d in lasts]
            self._add(o)

    def emit(self, final_wait_eng="sp"):
        nc = self.nc
        fin = Op(final_wait_eng, None, False)
        for e in self.ops:
            if self.ops[e]:
                fin.deps.append(self.ops[e][-1])
        for q in QUEUES:
            for o in self.ring_last[q]:
                if o is not None:
                    fin.deps.append(o)
        if self.cc_last is not None:
            fin.deps.append(self.cc_last)
        self._add(fin)

        cur = {e: {} for e in self.ops}
        for o in self.all:
            ck = cur[o.eng]
            waits = {}
            for d in o.deps:
                if d.is_dma:
                    key, val = d.ring, d.cnt
                else:
                    if d.eng == "pe" and o.eng == "pe":
                        continue
                    key, val = d.eng, d.seq + 1
                if ck.get(key, 0) >= val:
                    continue
                if waits.get(key, (0, None))[0] < val:
                    waits[key] = (val, d)
            for key, (val, d) in waits.items():
                for k2, v2 in d.clock.items():
                    if ck.get(k2, 0) < v2:
                        ck[k2] = v2
                if ck.get(key, 0) < val:
                    ck[key] = val
                if not d.is_dma:
                    d.signal = True
            o.waits = [(key, val, d) for key, (val, d) in waits.items()]
            o.clock = dict(ck)
        for e in self.ops:
            n = 0
            for o in self.ops[e]:
                if o.signal and not o.is_dma:
                    n += 1
                    o.sigidx = n
        with contextlib.ExitStack() as st:
            sems = {}
            for e in self.ops:
                sems[e] = st.enter_context(nc.semaphore("p_" + e))
            for q in QUEUES:
                for s in range(self.ring_sizes[q]):
                    sems[(q, s)] = st.enter_context(nc.semaphore("r_%s%d" % (q, s)))
            sems[("cc", 0)] = st.enter_context(nc.semaphore("cc_sem"))
            block = st.enter_context(nc.Block())
            stats = {"waits": 0, "ops": 0}

            def run(ename):
                def body(eng):
                    for o in self.ops[ename]:
                        for key, val, d in o.waits:
                            if d.is_dma:
                                eng.wait_ge(sems[key], val)
                            else:
                                eng.wait_ge(sems[key], d.sigidx)
                            stats["waits"] += 1
                        if o.fn is None:
                            if o.signal:
                                eng.nop().then_inc(sems[ename], 1)
                            continue
                        ins = o.fn(eng)
                        stats["ops"] += 1
                        if o.is_dma:
                            ins.then_inc(sems[o.ring], o.inc)
                        elif o.signal:
                            ins.then_inc(sems[ename], 1)
                return body

            block.tensor(run("pe"))
            block.scalar(run("act"))
            block.vector(run("dve"))
            block.gpsimd(run("pool"))
            block.sync(run("sp"))
            self.stats = stats
        return self


class Ctx:
    def __init__(self, nc, st):
        self.nc = nc
        self.st = st
        self.P = Prog(nc)
        ps = st.enter_context(nc.psum_tensor("psum_all", [128, 8, 512], F32))
        self.ps = ps
        self.bank = [Buf("bank%d" % i) for i in range(8)]

    def sb(self, name, shape, dt):
        if getattr(self, "arena", None) is None:
            return Tile(self.st.enter_context(self.nc.sbuf_tensor(name, shape, dt)), name)
        esz = 4 if dt == F32 else 2
        n = 1
        for d in shape[1:]:
            n *= d
        words = (n * esz + 3) // 4
        words = (words + 7) // 8 * 8
        a = self.arena_off
        assert a + words <= self.arena_words, "arena overflow %s need %d have %d" % (name, words, self.arena_words - a)
        self.arena_off = a + words
        ap = self.arena[:, a:a + (n * esz + 3) // 4]
        if dt != F32:
            ap = ap.bitcast(dt)
            if ap.shape[-1] != n:
                ap = ap[:, 0:n]
        if len(shape) > 2:
            names = " ".join("d%d" % i for i in range(1, len(shape)))
            kw = {"d%d" % i: shape[i] for i in range(1, len(shape))}
            ap = ap.rearrange("p (%s) -> p %s" % (names, names), **kw)
        return Tile(ap, name)

    def use_arena(self, words):
        self.arena = self.st.enter_context(self.nc.sbuf_tensor("arena", [128, words], F32))
        self.arena_words = words
        self.arena_off = 0

    def arena_reset(self, keep=0):
        self.arena_off = keep

    def dram(self, name, shape, dt, kind):
        return self.nc.dram_tensor(name, shape, dt, kind=kind).ap()

    def pf(self, b, n=512, off=0):
        return self.ps[:, b, off:off + n]

    def pb(self, b):
        return self.ps[:, b, :].bitcast(BF16)

    def mm(self, out, lhsT, rhs, start, stop, R, W, **kw):
        self.P.op("pe", lambda e: e.matmul(out, lhsT, rhs, start=start, stop=stop, **kw), R, W)

    def tr(self, out, in_, ident, R, W):
        self.P.op("pe", lambda e: e.transpose(out, in_, ident[:]), list(R) + [ident], W)

    def act(self, out, in_, func, R, W, **kw):
        self.P.op("act", lambda e: e.activation(out=out, in_=in_, func=func, **kw), R, W)

    def tt(self, eng, out, in0, in1, op, R, W):
        self.P.op(eng, lambda e: e.tensor_tensor(out=out, in0=in0, in1=in1, op=op), R, W)

    def ts(self, eng, out, in0, s1, s2, op0, op1, R, W, **kw):
        if s2 is None:
            self.P.op(eng, lambda e: e.tensor_scalar(out=out, in0=in0, scalar1=s1, scalar2=None, op0=op0, **kw), R, W)
        else:
            self.P.op(eng, lambda e: e.tensor_scalar(out=out, in0=in0, scalar1=s1, scalar2=s2, op0=op0, op1=op1, **kw), R, W)

    def stt(self, out, in0, scalar, in1, op0, op1, R, W):
        self.P.op("dve", lambda e: e.scalar_tensor_tensor(out=out, in0=in0, scalar=scalar, in1=in1, op0=op0, op1=op1), R, W)

    def cp(self, eng, out, in_, R, W):
        if eng == "act":
            self.P.op("act", lambda e: e.activation(out=out, in_=in_, func=AF.Copy), R, W)
        else:
            self.P.op(eng, lambda e: e.tensor_copy(out=out, in_=in_), R, W)

    def red(self, out, in_, R, W):
        self.P.op("dve", lambda e: e.tensor_reduce(out=out, in_=in_, axis=AX.X, op=ALU.add), R, W)

    def scan(self, out, d0, d1, init, R, W):
        self.P.op("dve", lambda e: e.tensor_tensor_scan(out=out, data0=d0, data1=d1, initial=init,
                                                        op0=ALU.mult, op1=ALU.add), R, W)

    def recip(self, out, in_, R, W):
        self.P.op("dve", lambda e: e.reciprocal(out=out, in_=in_), R, W)

    def memset(self, eng, ap, val, W):
        self.P.op(eng, lambda e: e.memset(ap, val), [], W)

    def rstd(self, out, in_, n, R, W):
        self.act(out, in_, AF.Ln, R, W, scale=1.0 / n, bias=self.eps_col[:, 0:1])
        self.act(out, out, AF.Exp, W, W, scale=-0.5)

    def consts(self):
        self.eps_col = self.sb("eps_col", [128, 1], F32)
        self.memset("dve", self.eps_col[:], EPS, [self.eps_col])


def emit_norm_T(C, h_t, nt, gcol, outT, ident, pbank0, scratch, ss, hn):
    for t in range(nt):
        C.act(hn[:, t, :], h_t[:, t, :], AF.Square, [h_t], [hn, ss], accum_out=ss[:, t:t + 1])
    C.rstd(ss[:, 0:nt], ss[:, 0:nt], float(D_MODEL), [ss], [ss])
    for t in range(nt):
        C.ts("dve", hn[:, t, :], h_t[:, t, :], ss[:, t:t + 1], None, ALU.mult, None, [h_t, ss], [hn])
    w = nt * 128
    for c in range(8):
        b = pbank0 + (c % 2)
        pv = C.pb(b)[:, 0:w]
        for t in range(nt):
            C.tr(pv[:, t * 128:(t + 1) * 128], hn[:, t, c * 128:(c + 1) * 128], ident, [hn], [C.bank[b]])
        eng = "dve" if c % 2 == 0 else "pool"
        if eng == "pool":
            C.act(outT[:, c, 0:w], pv, AF.Copy, [C.bank[b], gcol], [outT], scale=gcol[:, c:c + 1])
        else:
            C.ts("dve", outT[:, c, 0:w], pv, gcol[:, c:c + 1], None, ALU.mult, None, [C.bank[b], gcol], [outT])


def load_cast_weight(C, dst, src_ap, nchunk, ncol, colstep=512):
    v = src_ap.rearrange("(c p) n -> p c n", p=128)
    for c in range(nchunk):
        for n0 in range(0, ncol, colstep):
            n1 = min(ncol, n0 + colstep)
            C.P.dma("pool", dst[:, c, n0:n1], v[:, c, n0:n1], writes=[dst], par=True)


def build_pre(Th):
    nb = Th // BLK
    nc = bass.Bass("TRN2", target_bir_lowering=False)
    with contextlib.ExitStack() as st:
        C = Ctx(nc, st)
        h_in = C.dram("h_in", [Th, D_MODEL], F32, "ExternalInput")
        gmix = C.dram("gmix", [128, 8], F32, "ExternalInput")
        ident_d = C.dram("ident", [128, 128], BF16, "ExternalInput")
        hnT_o = C.dram("hnT_o", [D_MODEL, Th], BF16, "ExternalOutput")
        C.consts()
        ident = C.sb("ident_sb", [128, 128], BF16)
        gcol = C.sb("gcol", [128, 8], F32)
        C.P.dma("sp", ident[:], ident_d, writes=[ident])
        C.P.dma("sp", gcol[:], gmix, writes=[gcol])
        hv = h_in.rearrange("(t p) d -> p t d", p=128)
        ov = hnT_o.rearrange("(c p) t -> p c t", p=128)
        hts = [C.sb("h%d" % i, [128, 3, D_MODEL], F32) for i in range(2)]
        outs = [C.sb("o%d" % i, [128, 8, BLK], BF16) for i in range(2)]
        scratch = None
        ss = C.sb("ss", [128, 4], F32)
        hn = C.sb("hn", [128, 3, D_MODEL], BF16)
        for j in range(nb):
            ht = hts[j % 2]
            oT = outs[j % 2]
            C.P.dma("sp", ht[:, :, :], hv[:, 3 * j:3 * j + 3, :], writes=[ht])
            emit_norm_T(C, ht, 3, gcol, oT, ident, 0, scratch, ss, hn)
            C.P.dma("sp", ov[:, :, j * BLK:(j + 1) * BLK], oT[:, :, :], reads=[oT])
        C.P.emit()
    return nc


def build_post(Th, last):
    nb = Th // BLK
    NF = D_FF // 128
    nc = bass.Bass("TRN2", target_bir_lowering=False)
    with contextlib.ExitStack() as st:
        C = Ctx(nc, st)
        h_in = C.dram("h_in", [Th, D_MODEL], F32, "ExternalInput")
        oattnT = C.dram("oattnT", [512, Th], BF16, "ExternalInput")
        orecT = C.dram("orecT", [512, Th], BF16, "ExternalInput")
        w_out = C.dram("w_out", [1024, D_MODEL], F32, "ExternalInput")
        w_gu = C.dram("w_gu", [D_MODEL, 2 * D_FF], F32, "ExternalInput")
        w_down = C.dram("w_down", [D_FF, D_MODEL], F32, "ExternalInput")
        gcols = C.dram("gcols", [128, 20], F32, "ExternalInput")
        ident_d = C.dram("ident", [128, 128], BF16, "ExternalInput")
        h_out = C.dram("h_out", [Th, D_MODEL], F32, "ExternalOutput")
        if not last:
            hnT_o = C.dram("hnT_o", [D_MODEL, Th], BF16, "ExternalOutput")
        C.consts()
        P = C.P
        ident = C.sb("ident_sb", [128, 128], BF16)
        gc = C.sb("gc", [128, 20], F32)
        P.dma("sp", ident[:], ident_d, writes=[ident])
        P.dma("sp", gc[:], gcols, writes=[gc])
        ones = C.sb("ones", [128, 128], BF16)
        C.memset("dve", ones[:], 1.0 / 512.0, [ones])
        Wout = C.sb("Wout", [128, 8, D_MODEL], BF16)
        Wgu = C.sb("Wgu", [128, 8, 2 * D_FF], BF16)
        Wdn = C.sb("Wdn", [128, NF, D_MODEL], BF16)
        load_cast_weight(C, Wout, w_out, 8, D_MODEL)
        load_cast_weight(C, Wgu, w_gu, 8, 2 * D_FF)
        load_cast_weight(C, Wdn, w_down, NF, D_MODEL)

        hv = h_in.rearrange("(t p) d -> p t d", p=128)
        hov = h_out.rearrange("(t p) d -> p t d", p=128)
        av = oattnT.rearrange("(c p) t -> p c t", p=128)
        rv = orecT.rearrange("(c p) t -> p c t", p=128)
        if not last:
            ov = hnT_o.rearrange("(c p) t -> p c t", p=128)

        mixTs = [C.sb("mixT%d" % i, [128, 8, BLK], BF16) for i in range(1)]
        ht = C.sb("ht", [128, 3, D_MODEL], F32)
        rstd_r = C.sb("rstd_r", [128, BLK], F32)
        mixn = C.sb("mixn", [128, 4, BLK], BF16)
        sq = mixn
        scratch = None
        ss = C.sb("ss", [128, 4], F32)
        hn = C.sb("hn", [128, 3, D_MODEL], BF16)
        hnT = C.sb("hnT", [128, 8, BLK], BF16)
        sg = C.sb("sg", [128, 2, BLK], F32)
        actT = C.sb("actT", [128, NF, BLK], BF16)
        nxT = hnT

        for j in range(nb):
            mixT = mixTs[0]
            P.dma("sp", mixT[:, 0:4, :], av[:, :, j * BLK:(j + 1) * BLK], writes=[mixT])
            P.dma("sp", mixT[:, 4:8, :], rv[:, :, j * BLK:(j + 1) * BLK], writes=[mixT])
            P.dma("sp", ht[:, :, :], hv[:, 3 * j:3 * j + 3, :], writes=[ht])
            C.act(sq[:, :, :], mixT[:, 4:8, :], AF.Square, [mixT], [sq])
            for c in range(4):
                C.mm(C.pf(6, BLK), ones[:, :], sq[:, c, :], c == 0, c == 3, [ones, sq], [C.bank[6]])
            C.rstd(rstd_r[:, :], C.pf(6, BLK), 1.0, [C.bank[6]], [rstd_r])
            for c in range(4):
                C.stt(mixn[:, c, :], mixT[:, 4 + c, :], gc[:, c:c + 1], rstd_r[:, :], ALU.mult, ALU.mult,
                      [mixT, gc, rstd_r], [mixn])
            for t in range(3):
                for n in range(2):
                    for c in range(8):
                        lhsT = mixT[:, c, t * 128:(t + 1) * 128] if c < 4 else mixn[:, c - 4, t * 128:(t + 1) * 128]
                        C.mm(C.pf(n), lhsT, Wout[:, c, n * 512:(n + 1) * 512], c == 0, c == 7,
                             [mixT, mixn, Wout], [C.bank[n]])
                for n in range(2):
                    C.tt("dve", ht[:, t, n * 512:(n + 1) * 512], ht[:, t, n * 512:(n + 1) * 512], C.pf(n), ALU.add,
                         [ht, C.bank[n]], [ht])
            emit_norm_T(C, ht, 3, _ColView(gc, 4), hnT, ident, 6, scratch, ss, hn)
            for f in range(NF):
                bg = 2 + 2 * (f % 2)
                bu = bg + 1
                for c in range(8):
                    C.mm(C.pf(bg, BLK), Wgu[:, c, f * 128:(f + 1) * 128], hnT[:, c, :], c == 0, c == 7,
                         [Wgu, hnT], [C.bank[bg]])
                for c in range(8):
                    C.mm(C.pf(bu, BLK), Wgu[:, c, D_FF + f * 128:D_FF + (f + 1) * 128], hnT[:, c, :], c == 0, c == 7,
                         [Wgu, hnT], [C.bank[bu]])
                C.act(sg[:, f % 2, :], C.pf(bg, BLK), AF.Silu, [C.bank[bg]], [sg])
                C.tt("dve", actT[:, f, :], sg[:, f % 2, :], C.pf(bu, BLK), ALU.mult, [sg, C.bank[bu]], [actT])
            for t in range(3):
                for n in range(2):
                    for f in range(NF):
                        C.mm(C.pf(n), actT[:, f, t * 128:(t + 1) * 128], Wdn[:, f, n * 512:(n + 1) * 512],
                             f == 0, f == NF - 1, [actT, Wdn], [C.bank[n]])
                for n in range(2):
                    C.tt("dve", ht[:, t, n * 512:(n + 1) * 512], ht[:, t, n * 512:(n + 1) * 512], C.pf(n), ALU.add,
                         [ht, C.bank[n]], [ht])
            P.dma("sp", hov[:, 3 * j:3 * j + 3, :], ht[:, :, :], reads=[ht])
            if not last:
                emit_norm_T(C, ht, 3, _ColView(gc, 12), nxT, ident, 6, scratch, ss, hn)
                P.dma("sp", ov[:, :, j * BLK:(j + 1) * BLK], nxT[:, :, :], reads=[nxT])
        P.emit()
        print("post stats", P.stats)
    return nc


class _ColView:
    def __init__(self, base, off):
        self.base = base
        self.off = off

    @property
    def w(self):
        return self.base.w

    @w.setter
    def w(self, v):
        self.base.w = v

    @property
    def r(self):
        return self.base.r

    @r.setter
    def r(self, v):
        self.base.r = v

    def __getitem__(self, idx):
        p, c = idx
        start = (c.start or 0) + self.off
        stop = c.stop + self.off
        return self.base.t[p, start:stop]


def build_mix(T, layer):
    NB = T // BLK
    NT = T // 128
    lam_init = 0.8 - 0.6 * math.exp(-0.3 * layer)
    nc = bass.Bass("TRN2", target_bir_lowering=False)
    with contextlib.ExitStack() as st:
        C = Ctx(nc, st)
        P = C.P
        hnT_d = C.dram("hnT", [D_MODEL, T], BF16, "ExternalInput")
        w_in = C.dram("w_in", [D_MODEL, 1280], F32, "ExternalInput")
        gqk_d = C.dram("gqk", [128, 512], F32, "ExternalInput")
        lamv_d = C.dram("lamv", [128, 4, 64], F32, "ExternalInput")
        gsub_d = C.dram("gsub", [128, 384], F32, "ExternalInput")
        pc_d = C.dram("pc", [128, 2, 8], F32, "ExternalInput")
        wrg_d = C.dram("wrg", [4, 64, 64], F32, "ExternalInput")
        wig_d = C.dram("wig", [4, 64, 64], F32, "ExternalInput")
        cs_d = C.dram("cs", [128, NT, 16], F32, "ExternalInput")
        tri_d = C.dram("tri", [128, 128], BF16, "ExternalInput")
        ident_d = C.dram("ident", [128, 128], BF16, "ExternalInput")
        oattnT_o = C.dram("oattnT_o", [256, T], BF16, "ExternalOutput")
        orecT_o = C.dram("orecT_o", [256, T], BF16, "ExternalOutput")
        C.consts()
        emit_mix(C, T, lam_init, hnT_d, w_in, gqk_d, lamv_d, gsub_d, pc_d, wrg_d, wig_d, cs_d, tri_d, ident_d,
                 oattnT_o, orecT_o)
        P.emit()
        print("mix stats", P.stats)
    return nc


def emit_mix(C, T, lam_init, hnT_d, w_in, gqk_d, lamv_d, gsub_d, pc_d, wrg_d, wig_d, cs_d, tri_d, ident_d,
             oattnT_o, orecT_o, hooks=None):
    P = C.P
    NB = T // BLK
    NT = T // 128
    ident = C.sb("ident_sb", [128, 128], BF16)
    tri = C.sb("tri_sb", [128, 128], BF16)
    Gqk = C.sb("Gqk", [128, 512], F32)
    Gsub = C.sb("Gsub", [128, 3, 128], F32)
    lamt = C.sb("lamt", [128, 4, 64], F32)
    pcs = C.sb("pcs", [128, 2, 8], F32)
    CS = C.sb("CS", [128, NT, 16], F32)
    P.dma("sp", ident[:], ident_d, writes=[ident])
    P.dma("sp", tri[:], tri_d, writes=[tri])
    P.dma("sp", Gqk[:], gqk_d, writes=[Gqk])
    P.dma("sp", Gsub[:, :, :], gsub_d.rearrange("p (q e) -> p q e", e=128), writes=[Gsub])
    P.dma("sp", lamt[:], lamv_d, writes=[lamt])
    P.dma("sp", pcs[:], pc_d, writes=[pcs])
    P.dma("sp", CS[:], cs_d, writes=[CS])
    one_col = C.sb("one_col", [128, 1], F32)
    C.memset("dve", one_col[:], 1.0, [one_col])
    Win = C.sb("Win", [128, 8, 1280], BF16)
    load_cast_weight(C, Win, w_in, 8, 1280)
    WRG = C.sb("WRG", [128, 2, 128], BF16)
    WIG = C.sb("WIG", [128, 2, 128], BF16)
    C.memset("pool", WRG[:], 0.0, [WRG])
    C.memset("pool", WIG[:], 0.0, [WIG])
    for ct in range(2):
        for bl in range(2):
            P.dma("pool", WRG[bl * 64:(bl + 1) * 64, ct, bl * 64:(bl + 1) * 64], wrg_d[2 * ct + bl], writes=[WRG])
            P.dma("pool", WIG[bl * 64:(bl + 1) * 64, ct, bl * 64:(bl + 1) * 64], wig_d[2 * ct + bl], writes=[WIG])
    der = C.sb("der", [128, 2, 2], F32)
    lsl = C.sb("lsl", [128, 2], F32)
    C.act(lsl[:, :], pcs[:, :, 7], AF.Sigmoid, [pcs], [lsl])
    C.act(lsl[:, :], lsl[:, :], AF.Ln, [lsl], [lsl])
    C.ts("dve", der[:, :, 0], lsl[:, :], 8.0, None, ALU.mult, None, [lsl], [der])
    C.ts("dve", der[:, :, 1], lsl[:, :], 16.0, None, ALU.mult, None, [lsl], [der])
    lprod = C.sb("lprod", [128, 2, 64], F32)
    lsum = C.sb("lsum", [128, 2], F32)
    lam_col = C.sb("lam_col", [128, 1], F32)
    C.tt("dve", lprod[:, :, :], lamt[:, 0:2, :], lamt[:, 2:4, :], ALU.mult, [lamt], [lprod])
    C.red(lsum[:, :], lprod[:, :, :], [lprod], [lsum])
    C.act(lsum[:, :], lsum[:, :], AF.Exp, [lsum], [lsum])
    C.tt("dve", lam_col[:, :], lsum[:, 0:1], lsum[:, 1:2], ALU.subtract, [lsum], [lam_col])
    C.ts("dve", lam_col[:, :], lam_col[:, :], lam_init, None, ALU.add, None, [lam_col], [lam_col])
    C.ts("dve", Gsub[:, :, :], Gsub[:, :, :], 1.0 - lam_init, None, ALU.mult, None, [Gsub], [Gsub])

    KT = C.sb("KT", [128, 2, T], BF16)
    Vr = C.sb("Vr", [128, NT, 2, 130], BF16)
    KTb = [Buf("KT%d" % j) for j in range(NB)]
    Vrb = [Buf("Vr%d" % j) for j in range(NB)]
    for j in range(NB):
        C.memset("pool", Vr[:, 3 * j:3 * j + 3, :, 128:130], 1.0, [Vrb[j]])

    xTs = [C.sb("xT%d" % i, [128, 8, BLK], BF16) for i in range(2)]
    qkf = C.sb("qkf", [128, 3, 512], F32)
    sqq = C.sb("sqq", [128, 3, 512], F32)
    ssq = C.sb("ssq", [128, 24], F32)
    rt = C.sb("rt", [128, 4, 192], F32)
    qkb = C.sb("qkb", [128, 3, 512], BF16)
    QTs = [C.sb("QT%d" % i, [128, 2, BLK], BF16) for i in range(2)]
    Xs = [[C.sb("X%d_%d" % (ct, i), [128, BLK + 3], F32) for i in range(2)] for ct in range(2)]
    Hs = [[C.sb("H%d_%d" % (ct, i), [128, BLK], F32) for i in range(2)] for ct in range(2)]

    def rt_tiles(ct, names):
        return {n: C.sb("%s%d" % (n, ct), [128, BLK], F32) for n in names}
    RT = [rt_tiles(ct, ["xc", "sqg", "ug", "gl", "r", "a", "m", "ig", "u"]) for ct in range(2)]
    xcb = [C.sb("xcb%d" % ct, [128, BLK], BF16) for ct in range(2)]
    yb = [[C.sb("yb%d_%d" % (ct, i), [128, BLK], BF16) for i in range(2)] for ct in range(2)]
    Pt = [C.sb("Pt%d" % k, [128, 2, BLK], BF16) for k in range(3)]
    lt = C.sb("lt", [128, 2, 3], F32)
    t1 = C.sb("t1", [128, 3, 128], F32)
    Dt = C.sb("Dt", [128, 3, 128], F32)
    sqd = C.sb("sqd", [128, 3, 128], F32)
    ssd = C.sb("ssd", [128, 3], F32)
    ob = C.sb("ob", [128, 3, 128], BF16)
    oTs = [C.sb("oT%d" % i, [128, BLK], BF16) for i in range(2)]

    bank = C.bank
    if hooks is None:
        hv = hnT_d.rearrange("(c p) t -> p c t", p=128)

        def load_x(j):
            xT = xTs[j % 2]
            P.dma("sp", xT[:, :, :], hv[:, :, j * BLK:(j + 1) * BLK], writes=[xT])

        def store_orec(j, ct, Y):
            P.dma("sp", orecT_o[ct * 128:(ct + 1) * 128, j * BLK:(j + 1) * BLK], Y[:, :], reads=[Y])

        def store_oattn(j, hl, oT):
            P.dma("sp", oattnT_o[hl * 128:(hl + 1) * 128, j * BLK:(j + 1) * BLK], oT[:, :], reads=[oT])

        def end_block(j):
            pass
    else:
        def load_x(j):
            hooks["load_x"](j, xTs[j % 2])
        store_orec = hooks["store_orec"]
        store_oattn = hooks["store_oattn"]
        end_block = hooks["end_block"]

    load_x(0)
    ocount = [0]
    for j in range(NB):
        xT = xTs[j % 2]
        QT = QTs[j % 2]
        if j + 1 < NB:
            load_x(j + 1)
        for t in range(3):
            b0 = 0 if t % 2 == 0 else 2
            b1 = b0 + 1
            for c in range(8):
                C.mm(C.pf(b0), xT[:, c, t * 128:(t + 1) * 128], Win[:, c, 0:512], c == 0, c == 7, [xT, Win], [bank[b0]])
            for c in range(8):
                C.mm(C.pf(b1, 256), xT[:, c, t * 128:(t + 1) * 128], Win[:, c, 512:768], c == 0, c == 7,
                     [xT, Win], [bank[b1]])
            C.cp("act", qkf[:, t, :], C.pf(b0), [bank[b0]], [qkf])
            C.cp("dve", Vr[:, 3 * j + t, :, 0:128], C.pf(b1, 256).rearrange("p (h e) -> p h e", e=128),
                 [bank[b1]], [Vrb[j]])
        C.act(sqq[:, :, :], qkf[:, :, :], AF.Square, [qkf], [sqq])
        C.red(ssq[:, :], sqq[:, :, :].rearrange("p t (g d) -> p (t g) d", d=64), [sqq], [ssq])
        C.rstd(ssq[:, :], ssq[:, :], 64.0, [ssq], [ssq])
        qg = qkf[:, :, :].rearrange("p t (g d) -> p (t g) d", d=64)
        C.tt("dve", qg, qg, ssq[:, :].rearrange("p (g o) -> p g o", o=1).to_broadcast([128, 24, 64]), ALU.mult,
             [qkf, ssq], [qkf])
        C.tt("dve", qkf[:, :, :], qkf[:, :, :], Gqk[:, :].rearrange("p (o n) -> p o n", o=1).to_broadcast([128, 3, 512]),
             ALU.mult, [qkf, Gqk], [qkf])
        qv = qkf[:, :, :].rearrange("p t (g d) -> p t g d", d=64)
        qbv = qkb[:, :, :].rearrange("p t (g d) -> p t g d", d=64)
        x1 = qv[:, :, :, 0:8]
        x2 = qv[:, :, :, 8:16]
        cosb = CS[:, 3 * j:3 * j + 3, 0:8].rearrange("p t (o d) -> p t o d", o=1).to_broadcast([128, 3, 8, 8])
        sinb = CS[:, 3 * j:3 * j + 3, 8:16].rearrange("p t (o d) -> p t o d", o=1).to_broadcast([128, 3, 8, 8])

        def rtv(k):
            return rt[:, k, :].rearrange("p (t g d) -> p t g d", t=3, g=8)
        C.tt("pool", rtv(0), x1, cosb, ALU.mult, [qkf, CS], [rt])
        C.tt("pool", rtv(1), x2, sinb, ALU.mult, [qkf, CS], [rt])
        C.tt("pool", rtv(2), x2, cosb, ALU.mult, [qkf, CS], [rt])
        C.tt("pool", rtv(3), x1, sinb, ALU.mult, [qkf, CS], [rt])
        C.tt("pool", qbv[:, :, :, 0:8], rtv(0), rtv(1), ALU.subtract, [rt], [qkb])
        C.tt("pool", qbv[:, :, :, 8:16], rtv(2), rtv(3), ALU.add, [rt], [qkb])
        C.cp("act", qbv[:, :, :, 16:64], qv[:, :, :, 16:64], [qkf], [qkb])
        pq = C.pb(6)
        pk = C.pb(7)
        for t in range(3):
            for g4 in range(4):
                dstb = pq if g4 < 2 else pk
                hl = g4 % 2
                C.tr(dstb[:, hl * BLK + t * 128: hl * BLK + (t + 1) * 128], qkb[:, t, g4 * 128:(g4 + 1) * 128], ident,
                     [qkb], [bank[6 if g4 < 2 else 7]])
        C.cp("act", QT[:, :, :], pq[:, 0:2 * BLK].rearrange("p (h t) -> p h t", h=2), [bank[6]], [QT])
        C.cp("dve", KT[:, :, j * BLK:(j + 1) * BLK], pk[:, 0:2 * BLK].rearrange("p (h t) -> p h t", h=2),
             [bank[7]], [KTb[j]])
        for ct in range(2):
            R = RT[ct]
            X = Xs[ct][j % 2]
            Xp = Xs[ct][(j + 1) % 2]
            H = Hs[ct][j % 2]
            Hp = Hs[ct][(j + 1) % 2]
            Y = yb[ct][j % 2]
            bx, bg, br, bi = 4 + ct, 0 + ct, 2 + ct, 6 + ct
            for c in range(8):
                C.mm(C.pf(bx, BLK), Win[:, c, 768 + ct * 128:768 + (ct + 1) * 128], xT[:, c, :], c == 0, c == 7,
                     [Win, xT], [bank[bx]])
            for c in range(8):
                C.mm(C.pf(bg, BLK), Win[:, c, 1024 + ct * 128:1024 + (ct + 1) * 128], xT[:, c, :], c == 0, c == 7,
                     [Win, xT], [bank[bg]])
            if j == 0:
                C.memset("pool", X[:, 0:3], 0.0, [X])
            else:
                C.cp("pool", X[:, 0:3], Xp[:, BLK:BLK + 3], [Xp], [X])
            C.cp("act", X[:, 3:BLK + 3], C.pf(bx, BLK), [bank[bx]], [X])
            xc = R["xc"]
            C.ts("dve", xc[:, :], X[:, 3:BLK + 3], pcs[:, ct, 3:4], pcs[:, ct, 4:5], ALU.mult, ALU.add, [X, pcs], [xc])
            for i in (2, 1, 0):
                C.stt(xc[:, :], X[:, i:i + BLK], pcs[:, ct, i:i + 1], xc[:, :], ALU.mult, ALU.add, [X, pcs, xc], [xc])
            C.cp("pool", xcb[ct][:, :], xc[:, :], [xc], [xcb[ct]])
            gps = C.pf(bg, BLK)
            C.act(R["sqg"][:, :], gps, AF.Square, [bank[bg]], [R["sqg"]])
            C.ts("pool", R["sqg"][:, :], R["sqg"][:, :], 0.044715, 1.0, ALU.mult, ALU.add, [R["sqg"]], [R["sqg"]])
            C.tt("dve", R["ug"][:, :], R["sqg"][:, :], gps, ALU.mult, [R["sqg"], bank[bg]], [R["ug"]])
            C.act(R["ug"][:, :], R["ug"][:, :], AF.Sigmoid, [R["ug"]], [R["ug"]], scale=GELU_C)
            C.tt("dve", R["gl"][:, :], R["ug"][:, :], gps, ALU.mult, [R["ug"], bank[bg]], [R["gl"]])
            C.mm(C.pf(br, BLK), WRG[:, ct, :], xcb[ct][:, :], True, True, [WRG, xcb[ct]], [bank[br]])
            C.mm(C.pf(bi, BLK), WIG[:, ct, :], xcb[ct][:, :], True, True, [WIG, xcb[ct]], [bank[bi]])
            C.act(R["r"][:, :], C.pf(br, BLK), AF.Sigmoid, [bank[br], pcs], [R["r"]], bias=pcs[:, ct, 5:6])
            C.act(R["ig"][:, :], C.pf(bi, BLK), AF.Sigmoid, [bank[bi], pcs], [R["ig"]], bias=pcs[:, ct, 6:7])
            C.act(R["a"][:, :], R["r"][:, :], AF.Exp, [R["r"], der], [R["a"]], scale=der[:, ct, 0:1])
            C.act(R["m"][:, :], R["r"][:, :], AF.Exp, [R["r"], der], [R["m"]], scale=der[:, ct, 1:2])
            C.act(R["m"][:, :], R["m"][:, :], AF.Sqrt, [R["m"], one_col], [R["m"]], scale=-1.0, bias=one_col[:, 0:1])
            if j == 0:
                C.memset("pool", R["m"][:, 0:1], 1.0, [R["m"]])
            C.tt("pool", R["u"][:, :], R["ig"][:, :], xc[:, :], ALU.mult, [R["ig"], xc], [R["u"]])
            C.tt("pool", R["u"][:, :], R["u"][:, :], R["m"][:, :], ALU.mult, [R["u"], R["m"]], [R["u"]])
            a_t, u_t = R["a"], R["u"]
            if j == 0:
                C.scan(H[:, :], a_t[:, :], u_t[:, :], 0.0, [a_t, u_t], [H])
            else:
                C.scan(H[:, :], a_t[:, :], u_t[:, :], Hp[:, BLK - 1:BLK], [a_t, u_t, Hp], [H])
            C.tt("pool", Y[:, :], H[:, :], R["gl"][:, :], ALU.mult, [H, R["gl"]], [Y])
            store_orec(j, ct, Y)
        ntile = 3 * j + 3
        for hl in range(2):
            def q0_of(i):
                return max(0, i - 3 * j) * 128

            def emit_qk(i):
                s = i % 2
                q0 = q0_of(i)
                for c in range(2):
                    b = 2 * s + c
                    C.mm(C.pf(b, BLK)[:, q0:BLK], KT[64 * c:64 * c + 64, hl, i * 128:(i + 1) * 128],
                         QT[64 * c:64 * c + 64, hl, q0:BLK], True, True, [KTb[i // 3], QT], [bank[b]])

            def emit_exp_av(i):
                s = i % 2
                q0 = q0_of(i)
                pt = Pt[i % 3]
                C.act(pt[:, :, q0:BLK], C.ps[:, 2 * s:2 * s + 2, q0:BLK], AF.Exp, [bank[2 * s], bank[2 * s + 1]], [pt],
                      scale=0.125)
                if i >= 3 * j:
                    C.tt("pool", pt[:, :, q0:q0 + 128], pt[:, :, q0:q0 + 128],
                         tri[:, :].rearrange("p (o n) -> p o n", o=1).to_broadcast([128, 2, 128]), ALU.mult,
                         [pt, tri], [pt])
                for c in range(2):
                    ov = C.pf(4 + c, 390).rearrange("p (q e) -> p q e", e=130)
                    for qc in range(q0 // 128, 3):
                        first = (i == 0 and qc == 0)
                        lastk = (i == 3 * j + qc)
                        C.mm(ov[:, qc, 0:129], pt[:, c, qc * 128:(qc + 1) * 128], Vr[:, i, hl, 0:129], first, lastk,
                             [pt, Vrb[i // 3]], [bank[4 + c]], skip_group_check=True)

            emit_qk(0)
            for i in range(ntile):
                if i + 1 < ntile:
                    emit_qk(i + 1)
                emit_exp_av(i)
            ov0 = C.pf(4, 390).rearrange("p (q e) -> p q e", e=130)
            ov1 = C.pf(5, 390).rearrange("p (q e) -> p q e", e=130)
            C.cp("dve", lt[:, 0, :], ov0[:, :, 128], [bank[4]], [lt])
            C.cp("dve", lt[:, 1, :], ov1[:, :, 128], [bank[5]], [lt])
            C.recip(lt[:, :, :], lt[:, :, :], [lt], [lt])
            C.ts("dve", lt[:, 1, :], lt[:, 1, :], lam_col[:, 0:1], None, ALU.mult, None, [lt, lam_col], [lt])
            C.tt("dve", t1[:, :, :], ov1[:, :, 0:128],
                 lt[:, 1, :].rearrange("p (q o) -> p q o", o=1).to_broadcast([128, 3, 128]), ALU.mult,
                 [bank[5], lt], [t1])
            C.tt("dve", Dt[:, :, :], ov0[:, :, 0:128],
                 lt[:, 0, :].rearrange("p (q o) -> p q o", o=1).to_broadcast([128, 3, 128]), ALU.mult,
                 [bank[4], lt], [Dt])
            C.tt("dve", Dt[:, :, :], Dt[:, :, :], t1[:, :, :], ALU.subtract, [Dt, t1], [Dt])
            C.act(sqd[:, :, :], Dt[:, :, :], AF.Square, [Dt], [sqd])
            C.red(ssd[:, :], sqd[:, :, :], [sqd], [ssd])
            C.rstd(ssd[:, :], ssd[:, :], 128.0, [ssd], [ssd])
            C.tt("dve", Dt[:, :, :], Dt[:, :, :],
                 ssd[:, :].rearrange("p (q o) -> p q o", o=1).to_broadcast([128, 3, 128]), ALU.mult, [Dt, ssd], [Dt])
            C.tt("dve", ob[:, :, :], Dt[:, :, :], Gsub[:, :, :], ALU.mult, [Dt, Gsub], [ob])
            po = C.pb(6)
            for qc in range(3):
                C.tr(po[:, qc * 128:(qc + 1) * 128], ob[:, qc, :], ident, [ob], [bank[6]])
            oT = oTs[ocount[0] % 2]
            ocount[0] += 1
            C.cp("act", oT[:, :], po[:, 0:BLK], [bank[6]], [oT])
            store_oattn(j, hl, oT)
        end_block(j)


def _colform(v, n):
    return np.ascontiguousarray(np.asarray(v, np.float32).reshape(n, 128).T)


def _const_tables(T):
    NT = T // 128
    inv = (ROPE_THETA ** (-np.arange(0, 16, 2, dtype=np.float32) / 16.0)).astype(np.float32)
    ang = np.arange(T, dtype=np.float32)[:, None] * inv[None, :]
    cs = np.concatenate([np.cos(ang), np.sin(ang)], 1).astype(np.float32)
    cs = np.ascontiguousarray(cs.reshape(NT, 128, 16).transpose(1, 0, 2))
    tri = (np.arange(128)[None, :] >= np.arange(128)[:, None]).astype(np.float32).astype(ml_dtypes.bfloat16)
    ident = np.eye(128, dtype=np.float32).astype(ml_dtypes.bfloat16)
    return cs, tri, ident


def _mix_inputs(inp, l, p, cs, tri, ident):
    f = np.float32
    sl = slice(256 * p, 256 * p + 256)
    w = inp["w_in"][l]
    w_core = np.ascontiguousarray(np.concatenate(
        [w[:, 0:512][:, sl], w[:, 512:1024][:, sl], w[:, 1024:1536][:, sl], w[:, 1536:2048][:, sl],
         w[:, 2048:2560][:, sl]], axis=1).astype(f))
    gq, gk = inp["q_norm_g"][l], inp["k_norm_g"][l]
    gqk = np.ascontiguousarray(np.broadcast_to(np.concatenate([np.tile(gq, 4), np.tile(gk, 4)])[None, :], (128, 512)).astype(f))
    lamv = np.ascontiguousarray(np.broadcast_to(
        np.stack([inp["lambda_q1"][l], inp["lambda_q2"][l], inp["lambda_k1"][l], inp["lambda_k2"][l]])[None],
        (128, 4, 64)).astype(f))
    gs = np.ascontiguousarray(np.broadcast_to(np.tile(inp["subln_g"][l], 3)[None, :], (128, 384)).astype(f))
    pc = np.zeros((128, 2, 8), f)
    for ct in range(2):
        c0 = 256 * p + ct * 128
        pc[:, ct, 0:4] = inp["conv_w"][l][:, c0:c0 + 128].T
        pc[:, ct, 4] = inp["conv_b"][l][c0:c0 + 128]
        pc[:, ct, 5] = inp["b_rg"][l][c0:c0 + 128]
        pc[:, ct, 6] = inp["b_ig"][l][c0:c0 + 128]
        pc[:, ct, 7] = inp["lru_L"][l][c0:c0 + 128]
    return dict(w_in=w_core, gqk=gqk, lamv=lamv, gsub=gs, pc=pc,
                wrg=np.ascontiguousarray(inp["w_rg"][l][4 * p:4 * p + 4].astype(f)),
                wig=np.ascontiguousarray(inp["w_ig"][l][4 * p:4 * p + 4].astype(f)),
                cs=cs, tri=tri, ident=ident)


def kernel_unfused(**inp):
    inp = {k: np.asarray(v) for k, v in inp.items()}
    x = inp["x"]
    B = x.shape[0]
    depth = inp["w_in"].shape[0]
    T = T_PAD
    Th = T // 2
    cores = [(b, p) for b in range(B) for p in range(2)]
    ncores = len(cores)
    cs, tri, ident = _const_tables(T)
    h = np.zeros((B, T, D_MODEL), np.float32)
    h[:, :N_META] = inp["meta_tokens"][None]
    h[:, N_META:N_META + SEQ] = x

    nc = build_pre(Th)
    gm = _colform(inp["norm_mix_g"][0], 8)
    maps = [dict(h_in=np.ascontiguousarray(h[b, p * Th:(p + 1) * Th]), gmix=gm, ident=ident) for b, p in cores]
    res = run_bass_kernel_spmd(nc, maps, core_ids=list(range(ncores))).results
    hnT = [np.concatenate([res[2 * b]["hnT_o"], res[2 * b + 1]["hnT_o"]], axis=1) for b in range(B)]
    for l in range(depth):
        last = l == depth - 1
        nc = build_mix(T, l)
        maps = []
        for b, p in cores:
            d = _mix_inputs(inp, l, p, cs, tri, ident)
            d["hnT"] = np.ascontiguousarray(hnT[b])
            maps.append(d)
        res = run_bass_kernel_spmd(nc, maps, core_ids=list(range(ncores))).results
        oattnT = [np.concatenate([res[2 * b]["oattnT_o"], res[2 * b + 1]["oattnT_o"]], axis=0) for b in range(B)]
        orecT = [np.concatenate([res[2 * b]["orecT_o"], res[2 * b + 1]["orecT_o"]], axis=0) for b in range(B)]
        nc = build_post(Th, last)
        gnext = inp["norm_mix_g"][l + 1] if not last else np.ones(D_MODEL, np.float32)
        gcols = np.ascontiguousarray(np.concatenate(
            [_colform(inp["rec_norm_g"][l], 4), _colform(inp["norm_ffn_g"][l], 8), _colform(gnext, 8)], axis=1))
        maps = []
        for b, p in cores:
            tsl = slice(p * Th, (p + 1) * Th)
            maps.append(dict(h_in=np.ascontiguousarray(h[b, tsl]),
                             oattnT=np.ascontiguousarray(oattnT[b][:, tsl]),
                             orecT=np.ascontiguousarray(orecT[b][:, tsl]),
                             w_out=np.ascontiguousarray(inp["w_out"][l].astype(np.float32)),
                             w_gu=np.ascontiguousarray(inp["w_gu"][l].astype(np.float32)),
                             w_down=np.ascontiguousarray(inp["w_down"][l].astype(np.float32)),
                             gcols=gcols, ident=ident))
        res = run_bass_kernel_spmd(nc, maps, core_ids=list(range(ncores))).results
        for i, (b, p) in enumerate(cores):
            h[b, p * Th:(p + 1) * Th] = res[i]["h_out"]
        if not last:
            hnT = [np.concatenate([res[2 * b]["hnT_o"], res[2 * b + 1]["hnT_o"]], axis=1) for b in range(B)]
    return np.ascontiguousarray(h[:, N_META:N_META + SEQ]).astype(np.float32)


NLB = (T_PAD // 2) // BLK
X1_CH = [(0, 2), (2, 2), (4, 2), (6, 2), (8, 2), (10, 1)]
PAIRS = [[0, 1], [2, 3], [4, 5], [6, 7]]
REC_POS = [2, 3, 6, 7]
ATT_POS = [0, 1, 4, 5]


def emit_pre_f(C, x_in, gmix_d, ident, store_nx, end_lblock):
    P = C.P
    gcol = C.sb("gcol", [128, 8], F32)
    P.dma("sp", gcol[:], gmix_d, writes=[gcol])
    hv = x_in.rearrange("(t p) d -> p t d", p=128)
    hts = [C.sb("h%d" % i, [128, 3, D_MODEL], F32) for i in range(2)]
    outs = [C.sb("o%d" % i, [128, 8, BLK], BF16) for i in range(2)]
    ss = C.sb("ss", [128, 4], F32)
    hn = C.sb("hn", [128, 3, D_MODEL], BF16)
    for j in range(NLB):
        ht = hts[j % 2]
        oT = outs[j % 2]
        P.dma("sp", ht[:, :, :], hv[:, 3 * j:3 * j + 3, :], writes=[ht])
        emit_norm_T(C, ht, 3, gcol, oT, ident, 0, None, ss, hn)
        store_nx(j, oT)
        end_lblock(j)


def emit_post_f(C, last, w_out, w_gu, w_down, gcols, ident, sel, load_ab, load_h, store_h, store_nx, end_lblock,
                wbufs=None):
    P = C.P
    NF = D_FF // 128
    gc = C.sb("gc", [128, 20], F32)
    P.dma("sp", gc[:], gcols, writes=[gc])
    ones = C.sb("ones", [128, 128], BF16)
    C.memset("dve", ones[:], 1.0 / 512.0, [ones])
    Wout = C.sb("Wout", [128, 8, D_MODEL], BF16)
    Wgu = C.sb("Wgu", [128, 8, 2 * D_FF], BF16)
    Wdn = C.sb("Wdn", [128, NF, D_MODEL], BF16)
    GW = 512
    ngg = (D_FF + GW - 1) // GW
    Wout_b = Buf("Wout_b")
    Wgu_b = [Buf("Wgu_b%d" % g) for g in range(2 * ngg)]
    Wdn_b = [Buf("Wdn_b%d" % k) for k in range(NF // 2)]
    vo = w_out.rearrange("(c p) n -> p c n", p=128)
    vg = w_gu.rearrange("(c p) n -> p c n", p=128)
    vd = w_down.rearrange("(c p) n -> p c n", p=128)
    for c in range(8):
        for n0 in (0, 512):
            P.dma("pool", Wout[:, c, n0:n0 + 512], vo[:, c, n0:n0 + 512], writes=[Wout_b], par=True)
    for g in range(ngg):
        for half in range(2):
            n0 = half * D_FF + g * GW
            n1 = half * D_FF + min(D_FF, (g + 1) * GW)
            P.dma("pool", Wgu[:, :, n0:n1], vg[:, :, n0:n1], writes=[Wgu_b[half * ngg + g]])
    for k in range(NF // 2):
        for n0 in (0, 512):
            P.dma("pool", Wdn[:, 2 * k:2 * k + 2, n0:n0 + 512], vd[:, 2 * k:2 * k + 2, n0:n0 + 512],
                  writes=[Wdn_b[k]], par=True)
    mixT = C.sb("mixT", [128, 8, BLK], BF16)
    mixA = mixT
    mixB = C.sb("mixB", [128, 8, BLK], BF16)
    ht = C.sb("ht", [128, 3, D_MODEL], F32)
    rstd_r = C.sb("rstd_r", [128, BLK], F32)
    mixn = C.sb("mixn", [128, 4, BLK], BF16)
    ss = C.sb("ss", [128, 4], F32)
    hn = C.sb("hn", [128, 3, D_MODEL], BF16)
    hnT = C.sb("hnT", [128, 8, BLK], BF16)
    sg = C.sb("sg", [128, 2, BLK], BF16)
    actT = C.sb("actT", [128, NF, BLK], BF16)
    nxT = hnT
    for j in range(NLB):
        load_ab(j, mixA, mixB)
        load_h(j, ht)
        C.ts("pool", mixT[:, :, :], mixA[:, :, :], sel[:, 0:1], None, ALU.mult, None, [mixA, sel], [mixT])
        C.stt(mixT[:, :, :], mixB[:, :, :], sel[:, 1:2], mixT[:, :, :], ALU.mult, ALU.add, [mixB, sel, mixT], [mixT])
        for k, c in enumerate(REC_POS):
            C.act(mixn[:, k, :], mixT[:, c, :], AF.Square, [mixT], [mixn])
        for k in range(4):
            C.mm(C.pf(6, BLK), ones[:, :], mixn[:, k, :], k == 0, k == 3, [ones, mixn], [C.bank[6]])
        C.rstd(rstd_r[:, :], C.pf(6, BLK), 1.0, [C.bank[6]], [rstd_r])
        for k, c in enumerate(REC_POS):
            C.stt(mixn[:, k, :], mixT[:, c, :], gc[:, k:k + 1], rstd_r[:, :], ALU.mult, ALU.mult,
                  [mixT, gc, rstd_r], [mixn])
        for t in range(3):
            for n in range(2):
                for c in range(8):
                    if c in REC_POS:
                        lhsT = mixn[:, REC_POS.index(c), t * 128:(t + 1) * 128]
                    else:
                        lhsT = mixT[:, c, t * 128:(t + 1) * 128]
                    C.mm(C.pf(n), lhsT, Wout[:, c, n * 512:(n + 1) * 512], c == 0, c == 7,
                         [mixT, mixn, Wout_b], [C.bank[n]])
            for n in range(2):
                C.tt("dve", ht[:, t, n * 512:(n + 1) * 512], ht[:, t, n * 512:(n + 1) * 512], C.pf(n), ALU.add,
                     [ht, C.bank[n]], [ht])
        emit_norm_T(C, ht, 3, _ColView(gc, 4), hnT, ident, 6, None, ss, hn)
        for f in range(NF):
            bg = 2 + 2 * (f % 2)
            bu = bg + 1
            for c in range(8):
                C.mm(C.pf(bg, BLK), Wgu[:, c, f * 128:(f + 1) * 128], hnT[:, c, :], c == 0, c == 7,
                     [Wgu_b[(f * 128) // GW], hnT], [C.bank[bg]])
            for c in range(8):
                C.mm(C.pf(bu, BLK), Wgu[:, c, D_FF + f * 128:D_FF + (f + 1) * 128], hnT[:, c, :], c == 0, c == 7,
                     [Wgu_b[ngg + (f * 128) // GW], hnT], [C.bank[bu]])
            C.act(sg[:, f % 2, :], C.pf(bg, BLK), AF.Silu, [C.bank[bg]], [sg])
            C.tt("dve", actT[:, f, :], sg[:, f % 2, :], C.pf(bu, BLK), ALU.mult, [sg, C.bank[bu]], [actT])
        for t in range(3):
            for n in range(2):
                for f in range(NF):
                    C.mm(C.pf(n), actT[:, f, t * 128:(t + 1) * 128], Wdn[:, f, n * 512:(n + 1) * 512],
                         f == 0, f == NF - 1, [actT, Wdn_b[f // 2]], [C.bank[n]])
            for n in range(2):
                C.tt("dve", ht[:, t, n * 512:(n + 1) * 512], ht[:, t, n * 512:(n + 1) * 512], C.pf(n), ALU.add,
                     [ht, C.bank[n]], [ht])
        store_h(j, ht)
        if not last:
            emit_norm_T(C, ht, 3, _ColView(gc, 12), nxT, ident, 6, None, ss, hn)
            store_nx(j, nxT)
            end_lblock(j)


def build_fused(depth=2):
    T = T_PAD
    Th = T // 2
    NT = T // 128
    nc = bass.Bass("TRN2", target_bir_lowering=False)
    with contextlib.ExitStack() as st:
        C = Ctx(nc, st)
        P = C.P
        IN = "ExternalInput"
        x_in = C.dram("x_in", [Th, D_MODEL], F32, IN)
        sel_d = C.dram("sel", [128, 2], F32, IN)
        gmix0 = C.dram("gmix0", [128, 8], F32, IN)
        cs_d = C.dram("cs", [128, NT, 16], F32, IN)
        tri_d = C.dram("tri", [128, 128], BF16, IN)
        ident_d = C.dram("ident", [128, 128], BF16, IN)
        L = []
        for l in range(depth):
            L.append(dict(
                w_in=C.dram("w_in%d" % l, [D_MODEL, 1280], F32, IN),
                gqk=C.dram("gqk%d" % l, [128, 512], F32, IN),
                lamv=C.dram("lamv%d" % l, [128, 4, 64], F32, IN),
                gsub=C.dram("gsub%d" % l, [128, 384], F32, IN),
                pc=C.dram("pc%d" % l, [128, 2, 8], F32, IN),
                wrg=C.dram("wrg%d" % l, [4, 64, 64], F32, IN),
                wig=C.dram("wig%d" % l, [4, 64, 64], F32, IN),
                w_out=C.dram("w_out%d" % l, [1024, D_MODEL], F32, IN),
                w_gu=C.dram("w_gu%d" % l, [D_MODEL, 2 * D_FF], F32, IN),
                w_down=C.dram("w_down%d" % l, [D_FF, D_MODEL], F32, IN),
                gcols=C.dram("gcols%d" % l, [128, 20], F32, IN)))
        h_out = C.dram("h_out", [Th, D_MODEL], F32, "ExternalOutput")
        hres = nc.dram_tensor("hres", [Th, D_MODEL], F32).ap()
        hres_b = [Buf("hres%d" % j) for j in range(NLB)]
        x1s = [nc.dram_tensor("x1s%d" % i, [D_MODEL, n * BLK], BF16).ap() for i, (b0, n) in enumerate(X1_CH)]
        x1d = [nc.dram_tensor("x1d%d" % i, [2 * D_MODEL, n * BLK], BF16).ap() for i, (b0, n) in enumerate(X1_CH)]
        x1s_b = [Buf("x1s%d" % i) for i in range(len(X1_CH))]
        x1d_b = [Buf("x1d%d" % i) for i in range(len(X1_CH))]
        NX2 = (T // BLK) // 2
        x2s = [nc.dram_tensor("x2s%d" % i, [512, 2 * BLK], BF16).ap() for i in range(NX2)]
        x2d = [nc.dram_tensor("x2d%d" % i, [1024, 2 * BLK], BF16).ap() for i in range(NX2)]
        x2s_b = [Buf("x2s%d" % i) for i in range(NX2)]
        x2d_b = [Buf("x2d%d" % i) for i in range(NX2)]

        WB = []
        conv = []
        for l in range(0):
            d = {}
            for key, shp in (("w_out", [1024, D_MODEL]), ("w_gu", [D_MODEL, 2 * D_FF]), ("w_down", [D_FF, D_MODEL])):
                dst = nc.dram_tensor("%s_b%d" % (key, l), shp, BF16).ap()
                bufw = Buf("%s_b%d" % (key, l))
                d[key] = (dst, bufw)
                src = L[l][key]
                for r0 in range(0, shp[0], 128):
                    for c0 in range(0, shp[1], 2048):
                        c1 = min(shp[1], c0 + 2048)
                        conv.append((dst[r0:r0 + 128, c0:c1], src[r0:r0 + 128, c0:c1], bufw))
            WB.append(d)
        conv.reverse()

        def pump_conv(n):
            for _ in range(n):
                if not conv:
                    return
                o, i_, bufw = conv.pop()
                P.dma("pool", o, i_, writes=[bufw], par=True)

        C.consts()
        ident = C.sb("ident_sb", [128, 128], BF16)
        sel = C.sb("sel_sb", [128, 2], F32)
        P.dma("sp", ident[:], ident_d, writes=[ident])
        P.dma("sp", sel[:], sel_d, writes=[sel])
        C.use_arena(53100)

        def x1_chunk_of(jl):
            for i, (b0, n) in enumerate(X1_CH):
                if b0 <= jl < b0 + n:
                    return i, jl - b0, n
            raise AssertionError

        def store_nx(jl, oT):
            i, k, n = x1_chunk_of(jl)
            P.dma("sp", x1s[i].rearrange("(c p) t -> p c t", p=128)[:, :, k * BLK:(k + 1) * BLK], oT[:, :, :],
                  reads=[oT], writes=[x1s_b[i]])

        def end_lblock(jl):
            i, k, n = x1_chunk_of(jl)
            if k == n - 1:
                src, dst = x1s[i], x1d[i]
                P.cc(lambda e: e.collective_compute("AllGather", ALU.bypass, replica_groups=PAIRS,
                                                    ins=[src.opt()], outs=[dst.opt()]),
                     reads=[x1s_b[i]], writes=[x1d_b[i]])

        def mix_load_x(j, xT):
            half, jl = j // NLB, j % NLB
            i, k, n = x1_chunk_of(jl)
            v = x1d[i].rearrange("(r c p) t -> r p c t", r=2, p=128)
            P.dma("sp", xT[:, :, :], v[half][:, :, k * BLK:(k + 1) * BLK], reads=[x1d_b[i]], writes=[xT])

        def store_orec(j, ct, Y):
            c, k = j // 2, j % 2
            P.dma("sp", x2s[c][256 + ct * 128:256 + (ct + 1) * 128, k * BLK:(k + 1) * BLK], Y[:, :],
                  reads=[Y], writes=[x2s_b[c]])

        def store_oattn(j, hl, oT):
            c, k = j // 2, j % 2
            P.dma("sp", x2s[c][hl * 128:(hl + 1) * 128, k * BLK:(k + 1) * BLK], oT[:, :],
                  reads=[oT], writes=[x2s_b[c]])

        def mix_end_block(j):
            if j % 2 == 1:
                c = j // 2
                src, dst = x2s[c], x2d[c]
                P.cc(lambda e: e.collective_compute("AllGather", ALU.bypass, replica_groups=PAIRS,
                                                    ins=[src.opt()], outs=[dst.opt()]),
                     reads=[x2s_b[c]], writes=[x2d_b[c]])

        def load_ab(jl, mixA, mixB):
            for tile_, j in ((mixA, jl), (mixB, NLB + jl)):
                c, k = j // 2, j % 2
                v = x2d[c].rearrange("(q p) t -> p q t", p=128)
                P.dma("sp", tile_[:, :, :], v[:, :, k * BLK:(k + 1) * BLK], reads=[x2d_b[c]], writes=[tile_])

        keep = C.arena_off
        emit_pre_f(C, x_in, gmix0, ident, store_nx, end_lblock)
        for l in range(depth):
            last = l == depth - 1
            lam_init = 0.8 - 0.6 * math.exp(-0.3 * l)
            P.barrier()
            C.arena_reset(keep)
            emit_mix(C, T, lam_init, None, L[l]["w_in"], L[l]["gqk"], L[l]["lamv"], L[l]["gsub"], L[l]["pc"],
                     L[l]["wrg"], L[l]["wig"], cs_d, tri_d, ident_d, None, None,
                     hooks=dict(load_x=mix_load_x, store_orec=store_orec, store_oattn=store_oattn,
                                end_block=mix_end_block))
            P.barrier()
            C.arena_reset(keep)
            src_h = x_in if l == 0 else hres
            dst_h = h_out if last else hres

            def load_h(jl, ht, src_h=src_h, l=l):
                rd = [] if l == 0 else [hres_b[jl]]
                P.dma("sp", ht[:, :, :], src_h.rearrange("(t p) d -> p t d", p=128)[:, 3 * jl:3 * jl + 3, :],
                      reads=rd, writes=[ht])

            def store_h(jl, ht, dst_h=dst_h, last=last):
                wr = [] if last else [hres_b[jl]]
                P.dma("sp", dst_h.rearrange("(t p) d -> p t d", p=128)[:, 3 * jl:3 * jl + 3, :], ht[:, :, :],
                      reads=[ht], writes=wr)

            emit_post_f(C, last, L[l]["w_out"], L[l]["w_gu"], L[l]["w_down"], L[l]["gcols"], ident, sel,
                        load_ab, load_h, store_h, store_nx, end_lblock)
        P.emit()
        print("fused stats", P.stats)
    return nc


def kernel(**inp):
    inp = {k: np.asarray(v) for k, v in inp.items()}
    x = inp["x"]
    B = x.shape[0]
    depth = inp["w_in"].shape[0]
    T = T_PAD
    Th = T // 2
    cores = [(b, p) for b in range(B) for p in range(2)]
    cs, tri, ident = _const_tables(T)
    h = np.zeros((B, T, D_MODEL), np.float32)
    h[:, :N_META] = inp["meta_tokens"][None]
    h[:, N_META:N_META + SEQ] = x
    nc = build_fused(depth)
    f = np.float32
    shared = {}
    for l in range(depth):
        last = l == depth - 1
        gnext = inp["norm_mix_g"][l + 1] if not last else np.ones(D_MODEL, f)
        shared["gcols%d" % l] = np.ascontiguousarray(np.concatenate(
            [_colform(inp["rec_norm_g"][l], 4), _colform(inp["norm_ffn_g"][l], 8), _colform(gnext, 8)], axis=1))
        wo = inp["w_out"][l].astype(f)
        shared["w_out%d" % l] = np.ascontiguousarray(np.concatenate([wo[0:256], wo[512:768], wo[256:512], wo[768:1024]], 0))
        shared["w_gu%d" % l] = np.ascontiguousarray(inp["w_gu"][l].astype(f))
        shared["w_down%d" % l] = np.ascontiguousarray(inp["w_down"][l].astype(f))
    gm0 = _colform(inp["norm_mix_g"][0], 8)
    maps = []
    for b, p in cores:
        d = dict(shared)
        d["x_in"] = np.ascontiguousarray(h[b, p * Th:(p + 1) * Th])
        selv = np.zeros((128, 2), f)
        selv[:, p] = 1.0
        d["sel"] = selv
        d["gmix0"] = gm0
        d["cs"] = cs
        d["tri"] = tri
        d["ident"] = ident
        for l in range(depth):
            m = _mix_inputs(inp, l, p, cs, tri, ident)
            for k in ("w_in", "gqk", "lamv", "gsub", "pc", "wrg", "wig"):
                d["%s%d" % (k, l)] = m[k]
        maps.append(d)
    res = run_bass_kernel_spmd(nc, maps, core_ids=list(range(len(cores)))).results
    out = np.zeros((B, T, D_MODEL), np.float32)
    for i, (b, p) in enumerate(cores):
        out[b, p * Th:(p + 1) * Th] = res[i]["h_out"]
    return np.ascontiguousarray(out[:, N_META:N_META + SEQ])
```

```python
import contextlib
import math
import numpy as np
import ml_dtypes
import concourse.bass as bass
import concourse.mybir as mybir
from concourse.bass_utils import run_bass_kernel_spmd

F32 = mybir.dt.float32
BF16 = mybir.dt.bfloat16
AF = mybir.ActivationFunctionType
ALU = mybir.AluOpType
AX = mybir.AxisListType

D_MODEL = 1024
N_META = 16
SEQ = 8192
T_REAL = SEQ + N_META
BLK = 384
T_PAD = 8448
D_FF = 2816
EPS = 1e-6
ROPE_THETA = 500000.0
GELU_C = 1.5957691216057308

COMPUTE = ("pe", "act", "dve", "pool")
QUEUES = ("sp", "act", "pool")


class Buf:
    __slots__ = ("w", "r", "name")

    def __init__(self, name=""):
        self.w = []
        self.r = []
        self.name = name


class Tile(Buf):
    __slots__ = ("t",)

    def __init__(self, t, name=""):
        Buf.__init__(self, name)
        self.t = t

    def __getitem__(self, idx):
        return self.t[idx]


class Op:
    __slots__ = ("eng", "fn", "deps", "seq", "is_dma", "ring", "cnt", "clock",
                 "waits", "signal", "sigidx", "inc")

    def __init__(self, eng, fn, is_dma):
        self.eng = eng
        self.fn = fn
        self.deps = []
        self.is_dma = is_dma
        self.ring = None
        self.cnt = 0
        self.waits = []
        self.signal = False
        self.sigidx = 0
        self.clock = None
        self.inc = 16


class Prog:
    def __init__(self, nc, ring_sizes=None):
        self.nc = nc
        self.ops = {e: [] for e in ("pe", "act", "dve", "pool", "sp")}
        self.all = []
        self.ring_sizes = ring_sizes or {"sp": 16, "act": 8, "pool": 12}
        self.dma_n = {q: 0 for q in QUEUES}
        self.ring_last = {q: [None] * self.ring_sizes[q] for q in QUEUES}
        self.ring_cnt = {q: [0] * self.ring_sizes[q] for q in QUEUES}
        self.cc_last = None
        self.cc_n = 0

    def _deps(self, op, reads, writes, par=False):
        deps = []
        for t in reads:
            deps.extend(t.w)
        for t in writes:
            if par and not t.r:
                continue
            deps.extend(t.w)
            deps.extend(t.r)
        for t in reads:
            t.r.append(op)
        for t in writes:
            if par and not t.r:
                t.w = t.w + [op]
            else:
                t.w = [op]
                t.r = []
        seen = set()
        out = []
        for d in deps:
            if id(d) not in seen and d is not op:
                seen.add(id(d))
                out.append(d)
        op.deps = out

    def _add(self, o):
        o.seq = len(self.ops[o.eng])
        self.ops[o.eng].append(o)
        self.all.append(o)
        return o

    def op(self, eng, fn, reads=(), writes=()):
        o = Op(eng, fn, False)
        self._deps(o, reads, writes)
        return self._add(o)

    def dma(self, q, out_ap, in_ap, reads=(), writes=(), par=False, **kw):
        def fn(eng):
            return eng.dma_start(out=out_ap, in_=in_ap, **kw)
        return self.dma_like(q, fn, reads, writes, par)

    def dma_like(self, q, fn, reads=(), writes=(), par=False):
        o = Op(q, fn, True)
        self._deps(o, reads, writes, par)
        n = self.dma_n[q]
        self.dma_n[q] = n + 1
        slot = n % self.ring_sizes[q]
        prev = self.ring_last[q][slot]
        if prev is not None:
            o.deps.append(prev)
        self.ring_last[q][slot] = o
        self.ring_cnt[q][slot] += 16
        o.ring = (q, slot)
        o.cnt = self.ring_cnt[q][slot]
        return self._add(o)

    def cc(self, fn, reads=(), writes=()):
        o = Op("pool", fn, True)
        self._deps(o, reads, writes)
        if self.cc_last is not None:
            o.deps.append(self.cc_last)
        self.cc_last = o
        self.cc_n += 1
        o.ring = ("cc", 0)
        o.cnt = self.cc_n
        o.inc = 1
        return self._add(o)

    def barrier(self):
        lasts = []
        for e in self.ops:
            if self.ops[e]:
                lasts.append(self.ops[e][-1])
        for q in QUEUES:
            for o in self.ring_last[q]:
                if o is not None:
                    lasts.append(o)
        if self.cc_last is not None:
            lasts.append(self.cc_last)
        for e in ("pe", "act", "dve", "pool", "sp"):
            o = Op(e, None, False)
            o.deps = [d for d in lasts]
            self._add(o)

    def emit(self, final_wait_eng="sp"):
        nc = self.nc
        fin = Op(final_wait_eng, None, False)
        for e in self.ops:
            if self.ops[e]:
                fin.deps.append(self.ops[e][-1])
        for q in QUEUES:
            for o in self.ring_last[q]:
                if o is not None:
                    fin.deps.append(o)
        if self.cc_last is not None:
            fin.deps.append(self.cc_last)
        self._add(fin)

        cur = {e: {} for e in self.ops}
        for o in self.all:
            ck = cur[o.eng]
            waits = {}
            for d in o.deps:
                if d.is_dma:
                    key, val = d.ring, d.cnt
                else:
                    if d.eng == "pe" and o.eng == "pe":
                        continue
                    key, val = d.eng, d.seq + 1
                if ck.get(key, 0) >= val:
                    continue
                if waits.get(key, (0, None))[0] < val:
                    waits[key] = (val, d)
            for key, (val, d) in waits.items():
                for k2, v2 in d.clock.items():
                    if ck.get(k2, 0) < v2:
                        ck[k2] = v2
                if ck.get(key, 0) < val:
                    ck[key] = val
                if not d.is_dma:
                    d.signal = True
            o.waits = [(key, val, d) for key, (val, d) in waits.items()]
            o.clock = dict(ck)
        for e in self.ops:
            n = 0
            for o in self.ops[e]:
                if o.signal and not o.is_dma:
                    n += 1
                    o.sigidx = n
        with contextlib.ExitStack() as st:
            sems = {}
            for e in self.ops:
                sems[e] = st.enter_context(nc.semaphore("p_" + e))
            for q in QUEUES:
                for s in range(self.ring_sizes[q]):
                    sems[(q, s)] = st.enter_context(nc.semaphore("r_%s%d" % (q, s)))
            sems[("cc", 0)] = st.enter_context(nc.semaphore("cc_sem"))
            block = st.enter_context(nc.Block())
            stats = {"waits": 0, "ops": 0}

            def run(ename):
                def body(eng):
                    for o in self.ops[ename]:
                        for key, val, d in o.waits:
                            if d.is_dma:
                                eng.wait_ge(sems[key], val)
                            else:
                                eng.wait_ge(sems[key], d.sigidx)
                            stats["waits"] += 1
                        if o.fn is None:
                            if o.signal:
                                eng.nop().then_inc(sems[ename], 1)
                            continue
                        ins = o.fn(eng)
                        stats["ops"] += 1
                        if o.is_dma:
                            ins.then_inc(sems[o.ring], o.inc)
                        elif o.signal:
                            ins.then_inc(sems[ename], 1)
                return body

            block.tensor(run("pe"))
            block.scalar(run("act"))
            block.vector(run("dve"))
            block.gpsimd(run("pool"))
            block.sync(run("sp"))
            self.stats = stats
        return self


class Ctx:
    def __init__(self, nc, st):
        self.nc = nc
        self.st = st
        self.P = Prog(nc)
        ps = st.enter_context(nc.psum_tensor("psum_all", [128, 8, 512], F32))
        self.ps = ps
        self.bank = [Buf("bank%d" % i) for i in range(8)]

    def sb(self, name, shape, dt):
        if getattr(self, "arena", None) is None:
            return Tile(self.st.enter_context(self.nc.sbuf_tensor(name, shape, dt)), name)
        esz = 4 if dt == F32 else 2
        n = 1
        for d in shape[1:]:
            n *= d
        words = (n * esz + 3) // 4
        words = (words + 7) // 8 * 8
        a = self.arena_off
        assert a + words <= self.arena_words, "arena overflow %s need %d have %d" % (name, words, self.arena_words - a)
        self.arena_off = a + words
        ap = self.arena[:, a:a + (n * esz + 3) // 4]
        if dt != F32:
            ap = ap.bitcast(dt)
            if ap.shape[-1] != n:
                ap = ap[:, 0:n]
        if len(shape) > 2:
            names = " ".join("d%d" % i for i in range(1, len(shape)))
            kw = {"d%d" % i: shape[i] for i in range(1, len(shape))}
            ap = ap.rearrange("p (%s) -> p %s" % (names, names), **kw)
        return Tile(ap, name)

    def use_arena(self, words):
        self.arena = self.st.enter_context(self.nc.sbuf_tensor("arena", [128, words], F32))
        self.arena_words = words
        self.arena_off = 0

    def arena_reset(self, keep=0):
        self.arena_off = keep

    def dram(self, name, shape, dt, kind):
        return self.nc.dram_tensor(name, shape, dt, kind=kind).ap()

    def pf(self, b, n=512, off=0):
        return self.ps[:, b, off:off + n]

    def pb(self, b):
        return self.ps[:, b, :].bitcast(BF16)

    def mm(self, out, lhsT, rhs, start, stop, R, W, **kw):
        self.P.op("pe", lambda e: e.matmul(out, lhsT, rhs, start=start, stop=stop, **kw), R, W)

    def tr(self, out, in_, ident, R, W):
        self.P.op("pe", lambda e: e.transpose(out, in_, ident[:]), list(R) + [ident], W)

    def act(self, out, in_, func, R, W, **kw):
        self.P.op("act", lambda e: e.activation(out=out, in_=in_, func=func, **kw), R, W)

    def tt(self, eng, out, in0, in1, op, R, W):
        self.P.op(eng, lambda e: e.tensor_tensor(out=out, in0=in0, in1=in1, op=op), R, W)

    def ts(self, eng, out, in0, s1, s2, op0, op1, R, W, **kw):
        if s2 is None:
            self.P.op(eng, lambda e: e.tensor_scalar(out=out, in0=in0, scalar1=s1, scalar2=None, op0=op0, **kw), R, W)
        else:
            self.P.op(eng, lambda e: e.tensor_scalar(out=out, in0=in0, scalar1=s1, scalar2=s2, op0=op0, op1=op1, **kw), R, W)

    def stt(self, out, in0, scalar, in1, op0, op1, R, W):
        self.P.op("dve", lambda e: e.scalar_tensor_tensor(out=out, in0=in0, scalar=scalar, in1=in1, op0=op0, op1=op1), R, W)

    def cp(self, eng, out, in_, R, W):
        if eng == "act":
            self.P.op("act", lambda e: e.activation(out=out, in_=in_, func=AF.Copy), R, W)
        else:
            self.P.op(eng, lambda e: e.tensor_copy(out=out, in_=in_), R, W)

    def red(self, out, in_, R, W):
        self.P.op("dve", lambda e: e.tensor_reduce(out=out, in_=in_, axis=AX.X, op=ALU.add), R, W)

    def scan(self, out, d0, d1, init, R, W):
        self.P.op("dve", lambda e: e.tensor_tensor_scan(out=out, data0=d0, data1=d1, initial=init,
                                                        op0=ALU.mult, op1=ALU.add), R, W)

    def recip(self, out, in_, R, W):
        self.P.op("dve", lambda e: e.reciprocal(out=out, in_=in_), R, W)

    def memset(self, eng, ap, val, W):
        self.P.op(eng, lambda e: e.memset(ap, val), [], W)

    def rstd(self, out, in_, n, R, W):
        self.act(out, in_, AF.Ln, R, W, scale=1.0 / n, bias=self.eps_col[:, 0:1])
        self.act(out, out, AF.Exp, W, W, scale=-0.5)

    def consts(self):
        self.eps_col = self.sb("eps_col", [128, 1], F32)
        self.memset("dve", self.eps_col[:], EPS, [self.eps_col])


def emit_norm_T(C, h_t, nt, gcol, outT, ident, pbank0, scratch, ss, hn):
    for t in range(nt):
        C.act(hn[:, t, :], h_t[:, t, :], AF.Square, [h_t], [hn, ss], accum_out=ss[:, t:t + 1])
    C.rstd(ss[:, 0:nt], ss[:, 0:nt], float(D_MODEL), [ss], [ss])
    for t in range(nt):
        C.ts("dve", hn[:, t, :], h_t[:, t, :], ss[:, t:t + 1], None, ALU.mult, None, [h_t, ss], [hn])
    w = nt * 128
    for c in range(8):
        b = pbank0 + (c % 2)
        pv = C.pb(b)[:, 0:w]
        for t in range(nt):
            C.tr(pv[:, t * 128:(t + 1) * 128], hn[:, t, c * 128:(c + 1) * 128], ident, [hn], [C.bank[b]])
        eng = "dve" if c % 2 == 0 else "pool"
        if eng == "pool":
            C.act(outT[:, c, 0:w], pv, AF.Copy, [C.bank[b], gcol], [outT], scale=gcol[:, c:c + 1])
        else:
            C.ts("dve", outT[:, c, 0:w], pv, gcol[:, c:c + 1], None, ALU.mult, None, [C.bank[b], gcol], [outT])


def load_cast_weight(C, dst, src_ap, nchunk, ncol, colstep=512):
    v = src_ap.rearrange("(c p) n -> p c n", p=128)
    for c in range(nchunk):
        for n0 in range(0, ncol, colstep):
            n1 = min(ncol, n0 + colstep)
            C.P.dma("pool", dst[:, c, n0:n1], v[:, c, n0:n1], writes=[dst], par=True)


def build_pre(Th):
    nb = Th // BLK
    nc = bass.Bass("TRN2", target_bir_lowering=False)
    with contextlib.ExitStack() as st:
        C = Ctx(nc, st)
        h_in = C.dram("h_in", [Th, D_MODEL], F32, "ExternalInput")
        gmix = C.dram("gmix", [128, 8], F32, "ExternalInput")
        ident_d = C.dram("ident", [128, 128], BF16, "ExternalInput")
        hnT_o = C.dram("hnT_o", [D_MODEL, Th], BF16, "ExternalOutput")
        C.consts()
        ident = C.sb("ident_sb", [128, 128], BF16)
        gcol = C.sb("gcol", [128, 8], F32)
        C.P.dma("sp", ident[:], ident_d, writes=[ident])
        C.P.dma("sp", gcol[:], gmix, writes=[gcol])
        hv = h_in.rearrange("(t p) d -> p t d", p=128)
        ov = hnT_o.rearrange("(c p) t -> p c t", p=128)
        hts = [C.sb("h%d" % i, [128, 3, D_MODEL], F32) for i in range(2)]
        outs = [C.sb("o%d" % i, [128, 8, BLK], BF16) for i in range(2)]
        scratch = None
        ss = C.sb("ss", [128, 4], F32)
        hn = C.sb("hn", [128, 3, D_MODEL], BF16)
        for j in range(nb):
            ht = hts[j % 2]
            oT = outs[j % 2]
            C.P.dma("sp", ht[:, :, :], hv[:, 3 * j:3 * j + 3, :], writes=[ht])
            emit_norm_T(C, ht, 3, gcol, oT, ident, 0, scratch, ss, hn)
            C.P.dma("sp", ov[:, :, j * BLK:(j + 1) * BLK], oT[:, :, :], reads=[oT])
        C.P.emit()
    return nc


def build_post(Th, last):
    nb = Th // BLK
    NF = D_FF // 128
    nc = bass.Bass("TRN2", target_bir_lowering=False)
    with contextlib.ExitStack() as st:
        C = Ctx(nc, st)
        h_in = C.dram("h_in", [Th, D_MODEL], F32, "ExternalInput")
        oattnT = C.dram("oattnT", [512, Th], BF16, "ExternalInput")
        orecT = C.dram("orecT", [512, Th], BF16, "ExternalInput")
        w_out = C.dram("w_out", [1024, D_MODEL], F32, "ExternalInput")
        w_gu = C.dram("w_gu", [D_MODEL, 2 * D_FF], F32, "ExternalInput")
        w_down = C.dram("w_down", [D_FF, D_MODEL], F32, "ExternalInput")
        gcols = C.dram("gcols", [128, 20], F32, "ExternalInput")
        ident_d = C.dram("ident", [128, 128], BF16, "ExternalInput")
        h_out = C.dram("h_out", [Th, D_MODEL], F32, "ExternalOutput")
        if not last:
            hnT_o = C.dram("hnT_o", [D_MODEL, Th], BF16, "ExternalOutput")
        C.consts()
        P = C.P
        ident = C.sb("ident_sb", [128, 128], BF16)
        gc = C.sb("gc", [128, 20], F32)
        P.dma("sp", ident[:], ident_d, writes=[ident])
        P.dma("sp", gc[:], gcols, writes=[gc])
        ones = C.sb("ones", [128, 128], BF16)
        C.memset("dve", ones[:], 1.0 / 512.0, [ones])
        Wout = C.sb("Wout", [128, 8, D_MODEL], BF16)
        Wgu = C.sb("Wgu", [128, 8, 2 * D_FF], BF16)
        Wdn = C.sb("Wdn", [128, NF, D_MODEL], BF16)
        load_cast_weight(C, Wout, w_out, 8, D_MODEL)
        load_cast_weight(C, Wgu, w_gu, 8, 2 * D_FF)
        load_cast_weight(C, Wdn, w_down, NF, D_MODEL)

        hv = h_in.rearrange("(t p) d -> p t d", p=128)
        hov = h_out.rearrange("(t p) d -> p t d", p=128)
        av = oattnT.rearrange("(c p) t -> p c t", p=128)
        rv = orecT.rearrange("(c p) t -> p c t", p=128)
        if not last:
            ov = hnT_o.rearrange("(c p) t -> p c t", p=128)

        mixTs = [C.sb("mixT%d" % i, [128, 8, BLK], BF16) for i in range(1)]
        ht = C.sb("ht", [128, 3, D_MODEL], F32)
        rstd_r = C.sb("rstd_r", [128, BLK], F32)
        mixn = C.sb("mixn", [128, 4, BLK], BF16)
        sq = mixn
        scratch = None
        ss = C.sb("ss", [128, 4], F32)
        hn = C.sb("hn", [128, 3, D_MODEL], BF16)
        hnT = C.sb("hnT", [128, 8, BLK], BF16)
        sg = C.sb("sg", [128, 2, BLK], F32)
        actT = C.sb("actT", [128, NF, BLK], BF16)
        nxT = hnT

        for j in range(nb):
            mixT = mixTs[0]
            P.dma("sp", mixT[:, 0:4, :], av[:, :, j * BLK:(j + 1) * BLK], writes=[mixT])
            P.dma("sp", mixT[:, 4:8, :], rv[:, :, j * BLK:(j + 1) * BLK], writes=[mixT])
            P.dma("sp", ht[:, :, :], hv[:, 3 * j:3 * j + 3, :], writes=[ht])
            C.act(sq[:, :, :], mixT[:, 4:8, :], AF.Square, [mixT], [sq])
            for c in range(4):
                C.mm(C.pf(6, BLK), ones[:, :], sq[:, c, :], c == 0, c == 3, [ones, sq], [C.bank[6]])
            C.rstd(rstd_r[:, :], C.pf(6, BLK), 1.0, [C.bank[6]], [rstd_r])
            for c in range(4):
                C.stt(mixn[:, c, :], mixT[:, 4 + c, :], gc[:, c:c + 1], rstd_r[:, :], ALU.mult, ALU.mult,
                      [mixT, gc, rstd_r], [mixn])
            for t in range(3):
                for n in range(2):
                    for c in range(8):
                        lhsT = mixT[:, c, t * 128:(t + 1) * 128] if c < 4 else mixn[:, c - 4, t * 128:(t + 1) * 128]
                        C.mm(C.pf(n), lhsT, Wout[:, c, n * 512:(n + 1) * 512], c == 0, c == 7,
                             [mixT, mixn, Wout], [C.bank[n]])
                for n in range(2):
                    C.tt("dve", ht[:, t, n * 512:(n + 1) * 512], ht[:, t, n * 512:(n + 1) * 512], C.pf(n), ALU.add,
                         [ht, C.bank[n]], [ht])
            emit_norm_T(C, ht, 3, _ColView(gc, 4), hnT, ident, 6, scratch, ss, hn)
            for f in range(NF):
                bg = 2 + 2 * (f % 2)
                bu = bg + 1
                for c in range(8):
                    C.mm(C.pf(bg, BLK), Wgu[:, c, f * 128:(f + 1) * 128], hnT[:, c, :], c == 0, c == 7,
                         [Wgu, hnT], [C.bank[bg]])
                for c in range(8):
                    C.mm(C.pf(bu, BLK), Wgu[:, c, D_FF + f * 128:D_FF + (f + 1) * 128], hnT[:, c, :], c == 0, c == 7,
                         [Wgu, hnT], [C.bank[bu]])
                C.act(sg[:, f % 2, :], C.pf(bg, BLK), AF.Silu, [C.bank[bg]], [sg])
                C.tt("dve", actT[:, f, :], sg[:, f % 2, :], C.pf(bu, BLK), ALU.mult, [sg, C.bank[bu]], [actT])
            for t in range(3):
                for n in range(2):
                    for f in range(NF):
                        C.mm(C.pf(n), actT[:, f, t * 128:(t + 1) * 128], Wdn[:, f, n * 512:(n + 1) * 512],
                             f == 0, f == NF - 1, [actT, Wdn], [C.bank[n]])
                for n in range(2):
                    C.tt("dve", ht[:, t, n * 512:(n + 1) * 512], ht[:, t, n * 512:(n + 1) * 512], C.pf(n), ALU.add,
                         [ht, C.bank[n]], [ht])
            P.dma("sp", hov[:, 3 * j:3 * j + 3, :], ht[:, :, :], reads=[ht])
            if not last:
                emit_norm_T(C, ht, 3, _ColView(gc, 12), nxT, ident, 6, scratch, ss, hn)
                P.dma("sp", ov[:, :, j * BLK:(j + 1) * BLK], nxT[:, :, :], reads=[nxT])
        P.emit()
        print("post stats", P.stats)
    return nc


class _ColView:
    def __init__(self, base, off):
        self.base = base
        self.off = off

    @property
    def w(self):
        return self.base.w

    @w.setter
    def w(self, v):
        self.base.w = v

    @property
    def r(self):
        return self.base.r

    @r.setter
    def r(self, v):
        self.base.r = v

    def __getitem__(self, idx):
        p, c = idx
        start = (c.start or 0) + self.off
        stop = c.stop + self.off
        return self.base.t[p, start:stop]


def build_mix(T, layer):
    NB = T // BLK
    NT = T // 128
    lam_init = 0.8 - 0.6 * math.exp(-0.3 * layer)
    nc = bass.Bass("TRN2", target_bir_lowering=False)
    with contextlib.ExitStack() as st:
        C = Ctx(nc, st)
        P = C.P
        hnT_d = C.dram("hnT", [D_MODEL, T], BF16, "ExternalInput")
        w_in = C.dram("w_in", [D_MODEL, 1280], F32, "ExternalInput")
        gqk_d = C.dram("gqk", [128, 512], F32, "ExternalInput")
        lamv_d = C.dram("lamv", [128, 4, 64], F32, "ExternalInput")
        gsub_d = C.dram("gsub", [128, 384], F32, "ExternalInput")
        pc_d = C.dram("pc", [128, 2, 8], F32, "ExternalInput")
        wrg_d = C.dram("wrg", [4, 64, 64], F32, "ExternalInput")
        wig_d = C.dram("wig", [4, 64, 64], F32, "ExternalInput")
        cs_d = C.dram("cs", [128, NT, 16], F32, "ExternalInput")
        tri_d = C.dram("tri", [128, 128], BF16, "ExternalInput")
        ident_d = C.dram("ident", [128, 128], BF16, "ExternalInput")
        oattnT_o = C.dram("oattnT_o", [256, T], BF16, "ExternalOutput")
        orecT_o = C.dram("orecT_o", [256, T], BF16, "ExternalOutput")
        C.consts()
        emit_mix(C, T, lam_init, hnT_d, w_in, gqk_d, lamv_d, gsub_d, pc_d, wrg_d, wig_d, cs_d, tri_d, ident_d,
                 oattnT_o, orecT_o)
        P.emit()
        print("mix stats", P.stats)
    return nc


def emit_mix(C, T, lam_init, hnT_d, w_in, gqk_d, lamv_d, gsub_d, pc_d, wrg_d, wig_d, cs_d, tri_d, ident_d,
             oattnT_o, orecT_o, hooks=None):
    P = C.P
    NB = T // BLK
    NT = T // 128
    ident = C.sb("ident_sb", [128, 128], BF16)
    tri = C.sb("tri_sb", [128, 128], BF16)
    Gqk = C.sb("Gqk", [128, 512], F32)
    Gsub = C.sb("Gsub", [128, 3, 128], F32)
    lamt = C.sb("lamt", [128, 4, 64], F32)
    pcs = C.sb("pcs", [128, 2, 8], F32)
    CS = C.sb("CS", [128, NT, 16], F32)
    P.dma("sp", ident[:], ident_d, writes=[ident])
    P.dma("sp", tri[:], tri_d, writes=[tri])
    P.dma("sp", Gqk[:], gqk_d, writes=[Gqk])
    P.dma("sp", Gsub[:, :, :], gsub_d.rearrange("p (q e) -> p q e", e=128), writes=[Gsub])
    P.dma("sp", lamt[:], lamv_d, writes=[lamt])
    P.dma("sp", pcs[:], pc_d, writes=[pcs])
    P.dma("sp", CS[:], cs_d, writes=[CS])
    one_col = C.sb("one_col", [128, 1], F32)
    C.memset("dve", one_col[:], 1.0, [one_col])
    Win = C.sb("Win", [128, 8, 1280], BF16)
    load_cast_weight(C, Win, w_in, 8, 1280)
    WRG = C.sb("WRG", [128, 2, 128], BF16)
    WIG = C.sb("WIG", [128, 2, 128], BF16)
    C.memset("dve", WRG[:], 0.0, [WRG])
    C.memset("dve", WIG[:], 0.0, [WIG])
    for ct in range(2):
        for bl in range(2):
            P.dma("pool", WRG[bl * 64:(bl + 1) * 64, ct, bl * 64:(bl + 1) * 64], wrg_d[2 * ct + bl], writes=[WRG])
            P.dma("pool", WIG[bl * 64:(bl + 1) * 64, ct, bl * 64:(bl + 1) * 64], wig_d[2 * ct + bl], writes=[WIG])
    der = C.sb("der", [128, 2, 2], F32)
    lsl = C.sb("lsl", [128, 2], F32)
    C.act(lsl[:, :], pcs[:, :, 7], AF.Sigmoid, [pcs], [lsl])
    C.act(lsl[:, :], lsl[:, :], AF.Ln, [lsl], [lsl])
    C.ts("dve", der[:, :, 0], lsl[:, :], 8.0, None, ALU.mult, None, [lsl], [der])
    C.ts("dve", der[:, :, 1], lsl[:, :], 16.0, None, ALU.mult, None, [lsl], [der])
    lprod = C.sb("lprod", [128, 2, 64], F32)
    lsum = C.sb("lsum", [128, 2], F32)
    lam_col = C.sb("lam_col", [128, 1], F32)
    C.tt("dve", lprod[:, :, :], lamt[:, 0:2, :], lamt[:, 2:4, :], ALU.mult, [lamt], [lprod])
    C.red(lsum[:, :], lprod[:, :, :], [lprod], [lsum])
    C.act(lsum[:, :], lsum[:, :], AF.Exp, [lsum], [lsum])
    C.tt("dve", lam_col[:, :], lsum[:, 0:1], lsum[:, 1:2], ALU.subtract, [lsum], [lam_col])
    C.ts("dve", lam_col[:, :], lam_col[:, :], lam_init, None, ALU.add, None, [lam_col], [lam_col])
    C.ts("dve", Gsub[:, :, :], Gsub[:, :, :], 1.0 - lam_init, None, ALU.mult, None, [Gsub], [Gsub])

    KT = C.sb("KT", [128, 2, T], BF16)
    Vr = C.sb("Vr", [128, NT, 2, 130], BF16)
    KTb = [Buf("KT%d" % j) for j in range(NB)]
    Vrb = [Buf("Vr%d" % j) for j in range(NB)]
    for j in range(NB):
        C.memset("dve", Vr[:, 3 * j:3 * j + 3, :, 128:130], 1.0, [Vrb[j]])

    xTs = [C.sb("xT%d" % i, [128, 8, BLK], BF16) for i in range(2)]
    qkf = C.sb("qkf", [128, 3, 512], F32)
    sqq = C.sb("sqq", [128, 3, 512], F32)
    ssq = C.sb("ssq", [128, 24], F32)
    rt = C.sb("rt", [128, 4, 192], F32)
    qkb = C.sb("qkb", [128, 3, 512], BF16)
    QTs = [C.sb("QT%d" % i, [128, 2, BLK], BF16) for i in range(2)]
    Xs = [[C.sb("X%d_%d" % (ct, i), [128, BLK + 3], F32) for i in range(2)] for ct in range(2)]
    Hs = [[C.sb("H%d_%d" % (ct, i), [128, BLK], F32) for i in range(2)] for ct in range(2)]

    def rt_tiles(ct, names):
        return {n: C.sb("%s%d" % (n, ct), [128, BLK], F32) for n in names}
    RT = [rt_tiles(ct, ["xc", "sqg", "ug", "gl", "r", "a", "m", "ig", "u"]) for ct in range(2)]
    xcb = [C.sb("xcb%d" % ct, [128, BLK], BF16) for ct in range(2)]
    yb = [[C.sb("yb%d_%d" % (ct, i), [128, BLK], BF16) for i in range(2)] for ct in range(2)]
    Pt = [C.sb("Pt%d" % k, [128, 2, BLK], BF16) for k in range(3)]
    lt = C.sb("lt", [128, 2, 3], F32)
    t1 = C.sb("t1", [128, 3, 128], F32)
    Dt = C.sb("Dt", [128, 3, 128], F32)
    sqd = C.sb("sqd", [128, 3, 128], F32)
    ssd = C.sb("ssd", [128, 3], F32)
    ob = C.sb("ob", [128, 3, 128], BF16)
    oTs = [C.sb("oT%d" % i, [128, BLK], BF16) for i in range(2)]

    bank = C.bank
    if hooks is None:
        hv = hnT_d.rearrange("(c p) t -> p c t", p=128)

        def load_x(j):
            xT = xTs[j % 2]
            P.dma("sp", xT[:, :, :], hv[:, :, j * BLK:(j + 1) * BLK], writes=[xT])

        def store_orec(j, ct, Y):
            P.dma("sp", orecT_o[ct * 128:(ct + 1) * 128, j * BLK:(j + 1) * BLK], Y[:, :], reads=[Y])

        def store_oattn(j, hl, oT):
            P.dma("sp", oattnT_o[hl * 128:(hl + 1) * 128, j * BLK:(j + 1) * BLK], oT[:, :], reads=[oT])

        def end_block(j):
            pass
    else:
        def load_x(j):
            hooks["load_x"](j, xTs[j % 2])
        store_orec = hooks["store_orec"]
        store_oattn = hooks["store_oattn"]
        end_block = hooks["end_block"]

    load_x(0)
    ocount = [0]
    for j in range(NB):
        xT = xTs[j % 2]
        QT = QTs[j % 2]
        if j + 1 < NB:
            load_x(j + 1)
        for t in range(3):
            b0 = 0 if t % 2 == 0 else 2
            b1 = b0 + 1
            for c in range(8):
                C.mm(C.pf(b0), xT[:, c, t * 128:(t + 1) * 128], Win[:, c, 0:512], c == 0, c == 7, [xT, Win], [bank[b0]])
            for c in range(8):
                C.mm(C.pf(b1, 256), xT[:, c, t * 128:(t + 1) * 128], Win[:, c, 512:768], c == 0, c == 7,
                     [xT, Win], [bank[b1]])
            C.cp("act", qkf[:, t, :], C.pf(b0), [bank[b0]], [qkf])
            C.cp("dve", Vr[:, 3 * j + t, :, 0:128], C.pf(b1, 256).rearrange("p (h e) -> p h e", e=128),
                 [bank[b1]], [Vrb[j]])
        C.act(sqq[:, :, :], qkf[:, :, :], AF.Square, [qkf], [sqq])
        C.red(ssq[:, :], sqq[:, :, :].rearrange("p t (g d) -> p (t g) d", d=64), [sqq], [ssq])
        C.rstd(ssq[:, :], ssq[:, :], 64.0, [ssq], [ssq])
        qg = qkf[:, :, :].rearrange("p t (g d) -> p (t g) d", d=64)
        C.tt("dve", qg, qg, ssq[:, :].rearrange("p (g o) -> p g o", o=1).to_broadcast([128, 24, 64]), ALU.mult,
             [qkf, ssq], [qkf])
        C.tt("dve", qkf[:, :, :], qkf[:, :, :], Gqk[:, :].rearrange("p (o n) -> p o n", o=1).to_broadcast([128, 3, 512]),
             ALU.mult, [qkf, Gqk], [qkf])
        qv = qkf[:, :, :].rearrange("p t (g d) -> p t g d", d=64)
        qbv = qkb[:, :, :].rearrange("p t (g d) -> p t g d", d=64)
        x1 = qv[:, :, :, 0:8]
        x2 = qv[:, :, :, 8:16]
        cosb = CS[:, 3 * j:3 * j + 3, 0:8].rearrange("p t (o d) -> p t o d", o=1).to_broadcast([128, 3, 8, 8])
        sinb = CS[:, 3 * j:3 * j + 3, 8:16].rearrange("p t (o d) -> p t o d", o=1).to_broadcast([128, 3, 8, 8])

        def rtv(k):
            return rt[:, k, :].rearrange("p (t g d) -> p t g d", t=3, g=8)
        C.tt("dve", rtv(0), x1, cosb, ALU.mult, [qkf, CS], [rt])
        C.tt("dve", rtv(1), x2, sinb, ALU.mult, [qkf, CS], [rt])
        C.tt("dve", rtv(2), x2, cosb, ALU.mult, [qkf, CS], [rt])
        C.tt("dve", rtv(3), x1, sinb, ALU.mult, [qkf, CS], [rt])
        C.tt("dve", qbv[:, :, :, 0:8], rtv(0), rtv(1), ALU.subtract, [rt], [qkb])
        C.tt("dve", qbv[:, :, :, 8:16], rtv(2), rtv(3), ALU.add, [rt], [qkb])
        C.cp("act", qbv[:, :, :, 16:64], qv[:, :, :, 16:64], [qkf], [qkb])
        pq = C.pb(6)
        pk = C.pb(7)
        for t in range(3):
            for g4 in range(4):
                dstb = pq if g4 < 2 else pk
                hl = g4 % 2
                C.tr(dstb[:, hl * BLK + t * 128: hl * BLK + (t + 1) * 128], qkb[:, t, g4 * 128:(g4 + 1) * 128], ident,
                     [qkb], [bank[6 if g4 < 2 else 7]])
        C.cp("act", QT[:, :, :], pq[:, 0:2 * BLK].rearrange("p (h t) -> p h t", h=2), [bank[6]], [QT])
        C.cp("dve", KT[:, :, j * BLK:(j + 1) * BLK], pk[:, 0:2 * BLK].rearrange("p (h t) -> p h t", h=2),
             [bank[7]], [KTb[j]])
        for ct in range(2):
            R = RT[ct]
            X = Xs[ct][j % 2]
            Xp = Xs[ct][(j + 1) % 2]
            H = Hs[ct][j % 2]
            Hp = Hs[ct][(j + 1) % 2]
            Y = yb[ct][j % 2]
            bx, bg, br, bi = 4 + ct, 0 + ct, 2 + ct, 6 + ct
            for c in range(8):
                C.mm(C.pf(bx, BLK), Win[:, c, 768 + ct * 128:768 + (ct + 1) * 128], xT[:, c, :], c == 0, c == 7,
                     [Win, xT], [bank[bx]])
            for c in range(8):
                C.mm(C.pf(bg, BLK), Win[:, c, 1024 + ct * 128:1024 + (ct + 1) * 128], xT[:, c, :], c == 0, c == 7,
                     [Win, xT], [bank[bg]])
            if j == 0:
                C.memset("dve", X[:, 0:3], 0.0, [X])
            else:
                C.cp("dve", X[:, 0:3], Xp[:, BLK:BLK + 3], [Xp], [X])
            C.cp("act", X[:, 3:BLK + 3], C.pf(bx, BLK), [bank[bx]], [X])
            xc = R["xc"]
            C.ts("dve", xc[:, :], X[:, 3:BLK + 3], pcs[:, ct, 3:4], pcs[:, ct, 4:5], ALU.mult, ALU.add, [X, pcs], [xc])
            for i in (2, 1, 0):
                C.stt(xc[:, :], X[:, i:i + BLK], pcs[:, ct, i:i + 1], xc[:, :], ALU.mult, ALU.add, [X, pcs, xc], [xc])
            C.cp("dve", xcb[ct][:, :], xc[:, :], [xc], [xcb[ct]])
            gps = C.pf(bg, BLK)
            C.act(R["sqg"][:, :], gps, AF.Square, [bank[bg]], [R["sqg"]])
            C.ts("dve", R["sqg"][:, :], R["sqg"][:, :], 0.044715, 1.0, ALU.mult, ALU.add, [R["sqg"]], [R["sqg"]])
            C.tt("dve", R["ug"][:, :], R["sqg"][:, :], gps, ALU.mult, [R["sqg"], bank[bg]], [R["ug"]])
            C.act(R["ug"][:, :], R["ug"][:, :], AF.Sigmoid, [R["ug"]], [R["ug"]], scale=GELU_C)
            C.tt("dve", R["gl"][:, :], R["ug"][:, :], gps, ALU.mult, [R["ug"], bank[bg]], [R["gl"]])
            C.mm(C.pf(br, BLK), WRG[:, ct, :], xcb[ct][:, :], True, True, [WRG, xcb[ct]], [bank[br]])
            C.mm(C.pf(bi, BLK), WIG[:, ct, :], xcb[ct][:, :], True, True, [WIG, xcb[ct]], [bank[bi]])
            C.act(R["r"][:, :], C.pf(br, BLK), AF.Sigmoid, [bank[br], pcs], [R["r"]], bias=pcs[:, ct, 5:6])
            C.act(R["ig"][:, :], C.pf(bi, BLK), AF.Sigmoid, [bank[bi], pcs], [R["ig"]], bias=pcs[:, ct, 6:7])
            C.act(R["a"][:, :], R["r"][:, :], AF.Exp, [R["r"], der], [R["a"]], scale=der[:, ct, 0:1])
            C.act(R["m"][:, :], R["r"][:, :], AF.Exp, [R["r"], der], [R["m"]], scale=der[:, ct, 1:2])
            C.act(R["m"][:, :], R["m"][:, :], AF.Sqrt, [R["m"], one_col], [R["m"]], scale=-1.0, bias=one_col[:, 0:1])
            if j == 0:
                C.memset("dve", R["m"][:, 0:1], 1.0, [R["m"]])
            C.tt("dve", R["u"][:, :], R["ig"][:, :], xc[:, :], ALU.mult, [R["ig"], xc], [R["u"]])
            C.tt("dve", R["u"][:, :], R["u"][:, :], R["m"][:, :], ALU.mult, [R["u"], R["m"]], [R["u"]])
            a_t, u_t = R["a"], R["u"]
            if j == 0:
                C.scan(H[:, :], a_t[:, :], u_t[:, :], 0.0, [a_t, u_t], [H])
            else:
                C.scan(H[:, :], a_t[:, :], u_t[:, :], Hp[:, BLK - 1:BLK], [a_t, u_t, Hp], [H])
            C.tt("dve", Y[:, :], H[:, :], R["gl"][:, :], ALU.mult, [H, R["gl"]], [Y])
            store_orec(j, ct, Y)
        ntile = 3 * j + 3
        for hl in range(2):
            def q0_of(i):
                return max(0, i - 3 * j) * 128

            def emit_qk(i):
                s = i % 2
                q0 = q0_of(i)
                for c in range(2):
                    b = 2 * s + c
                    C.mm(C.pf(b, BLK)[:, q0:BLK], KT[64 * c:64 * c + 64, hl, i * 128:(i + 1) * 128],
                         QT[64 * c:64 * c + 64, hl, q0:BLK], True, True, [KTb[i // 3], QT], [bank[b]])

            def emit_exp_av(i):
                s = i % 2
                q0 = q0_of(i)
                pt = Pt[i % 3]
                C.act(pt[:, :, q0:BLK], C.ps[:, 2 * s:2 * s + 2, q0:BLK], AF.Exp, [bank[2 * s], bank[2 * s + 1]], [pt],
                      scale=0.125)
                if i >= 3 * j:
                    C.tt("dve", pt[:, :, q0:q0 + 128], pt[:, :, q0:q0 + 128],
                         tri[:, :].rearrange("p (o n) -> p o n", o=1).to_broadcast([128, 2, 128]), ALU.mult,
                         [pt, tri], [pt])
                for c in range(2):
                    ov = C.pf(4 + c, 390).rearrange("p (q e) -> p q e", e=130)
                    for qc in range(q0 // 128, 3):
                        first = (i == 0 and qc == 0)
                        lastk = (i == 3 * j + qc)
                        C.mm(ov[:, qc, 0:129], pt[:, c, qc * 128:(qc + 1) * 128], Vr[:, i, hl, 0:129], first, lastk,
                             [pt, Vrb[i // 3]], [bank[4 + c]], skip_group_check=True)

            emit_qk(0)
            for i in range(ntile):
                if i + 1 < ntile:
                    emit_qk(i + 1)
                emit_exp_av(i)
            ov0 = C.pf(4, 390).rearrange("p (q e) -> p q e", e=130)
            ov1 = C.pf(5, 390).rearrange("p (q e) -> p q e", e=130)
            C.cp("dve", lt[:, 0, :], ov0[:, :, 128], [bank[4]], [lt])
            C.cp("dve", lt[:, 1, :], ov1[:, :, 128], [bank[5]], [lt])
            C.recip(lt[:, :, :], lt[:, :, :], [lt], [lt])
            C.ts("dve", lt[:, 1, :], lt[:, 1, :], lam_col[:, 0:1], None, ALU.mult, None, [lt, lam_col], [lt])
            C.tt("dve", t1[:, :, :], ov1[:, :, 0:128],
                 lt[:, 1, :].rearrange("p (q o) -> p q o", o=1).to_broadcast([128, 3, 128]), ALU.mult,
                 [bank[5], lt], [t1])
            C.tt("dve", Dt[:, :, :], ov0[:, :, 0:128],
                 lt[:, 0, :].rearrange("p (q o) -> p q o", o=1).to_broadcast([128, 3, 128]), ALU.mult,
                 [bank[4], lt], [Dt])
            C.tt("dve", Dt[:, :, :], Dt[:, :, :], t1[:, :, :], ALU.subtract, [Dt, t1], [Dt])
            C.act(sqd[:, :, :], Dt[:, :, :], AF.Square, [Dt], [sqd])
            C.red(ssd[:, :], sqd[:, :, :], [sqd], [ssd])
            C.rstd(ssd[:, :], ssd[:, :], 128.0, [ssd], [ssd])
            C.tt("dve", Dt[:, :, :], Dt[:, :, :],
                 ssd[:, :].rearrange("p (q o) -> p q o", o=1).to_broadcast([128, 3, 128]), ALU.mult, [Dt, ssd], [Dt])
            C.tt("dve", ob[:, :, :], Dt[:, :, :], Gsub[:, :, :], ALU.mult, [Dt, Gsub], [ob])
            po = C.pb(6)
            for qc in range(3):
                C.tr(po[:, qc * 128:(qc + 1) * 128], ob[:, qc, :], ident, [ob], [bank[6]])
            oT = oTs[ocount[0] % 2]
            ocount[0] += 1
            C.cp("act", oT[:, :], po[:, 0:BLK], [bank[6]], [oT])
            store_oattn(j, hl, oT)
        end_block(j)


def _colform(v, n):
    return np.ascontiguousarray(np.asarray(v, np.float32).reshape(n, 128).T)


def _const_tables(T):
    NT = T // 128
    inv = (ROPE_THETA ** (-np.arange(0, 16, 2, dtype=np.float32) / 16.0)).astype(np.float32)
    ang = np.arange(T, dtype=np.float32)[:, None] * inv[None, :]
    cs = np.concatenate([np.cos(ang), np.sin(ang)], 1).astype(np.float32)
    cs = np.ascontiguousarray(cs.reshape(NT, 128, 16).transpose(1, 0, 2))
    tri = (np.arange(128)[None, :] >= np.arange(128)[:, None]).astype(np.float32).astype(ml_dtypes.bfloat16)
    ident = np.eye(128, dtype=np.float32).astype(ml_dtypes.bfloat16)
    return cs, tri, ident


def _mix_inputs(inp, l, p, cs, tri, ident):
    f = np.float32
    sl = slice(256 * p, 256 * p + 256)
    w = inp["w_in"][l]
    w_core = np.ascontiguousarray(np.concatenate(
        [w[:, 0:512][:, sl], w[:, 512:1024][:, sl], w[:, 1024:1536][:, sl], w[:, 1536:2048][:, sl],
         w[:, 2048:2560][:, sl]], axis=1).astype(f))
    gq, gk = inp["q_norm_g"][l], inp["k_norm_g"][l]
    gqk = np.ascontiguousarray(np.broadcast_to(np.concatenate([np.tile(gq, 4), np.tile(gk, 4)])[None, :], (128, 512)).astype(f))
    lamv = np.ascontiguousarray(np.broadcast_to(
        np.stack([inp["lambda_q1"][l], inp["lambda_q2"][l], inp["lambda_k1"][l], inp["lambda_k2"][l]])[None],
        (128, 4, 64)).astype(f))
    gs = np.ascontiguousarray(np.broadcast_to(np.tile(inp["subln_g"][l], 3)[None, :], (128, 384)).astype(f))
    pc = np.zeros((128, 2, 8), f)
    for ct in range(2):
        c0 = 256 * p + ct * 128
        pc[:, ct, 0:4] = inp["conv_w"][l][:, c0:c0 + 128].T
        pc[:, ct, 4] = inp["conv_b"][l][c0:c0 + 128]
        pc[:, ct, 5] = inp["b_rg"][l][c0:c0 + 128]
        pc[:, ct, 6] = inp["b_ig"][l][c0:c0 + 128]
        pc[:, ct, 7] = inp["lru_L"][l][c0:c0 + 128]
    return dict(w_in=w_core, gqk=gqk, lamv=lamv, gsub=gs, pc=pc,
                wrg=np.ascontiguousarray(inp["w_rg"][l][4 * p:4 * p + 4].astype(f)),
                wig=np.ascontiguousarray(inp["w_ig"][l][4 * p:4 * p + 4].astype(f)),
                cs=cs, tri=tri, ident=ident)


def kernel_unfused(**inp):
    inp = {k: np.asarray(v) for k, v in inp.items()}
    x = inp["x"]
    B = x.shape[0]
    depth = inp["w_in"].shape[0]
    T = T_PAD
    Th = T // 2
    cores = [(b, p) for b in range(B) for p in range(2)]
    ncores = len(cores)
    cs, tri, ident = _const_tables(T)
    h = np.zeros((B, T, D_MODEL), np.float32)
    h[:, :N_META] = inp["meta_tokens"][None]
    h[:, N_META:N_META + SEQ] = x

    nc = build_pre(Th)
    gm = _colform(inp["norm_mix_g"][0], 8)
    maps = [dict(h_in=np.ascontiguousarray(h[b, p * Th:(p + 1) * Th]), gmix=gm, ident=ident) for b, p in cores]
    res = run_bass_kernel_spmd(nc, maps, core_ids=list(range(ncores))).results
    hnT = [np.concatenate([res[2 * b]["hnT_o"], res[2 * b + 1]["hnT_o"]], axis=1) for b in range(B)]
    for l in range(depth):
        last = l == depth - 1
        nc = build_mix(T, l)
        maps = []
        for b, p in cores:
            d = _mix_inputs(inp, l, p, cs, tri, ident)
            d["hnT"] = np.ascontiguousarray(hnT[b])
            maps.append(d)
        res = run_bass_kernel_spmd(nc, maps, core_ids=list(range(ncores))).results
        oattnT = [np.concatenate([res[2 * b]["oattnT_o"], res[2 * b + 1]["oattnT_o"]], axis=0) for b in range(B)]
        orecT = [np.concatenate([res[2 * b]["orecT_o"], res[2 * b + 1]["orecT_o"]], axis=0) for b in range(B)]
        nc = build_post(Th, last)
        gnext = inp["norm_mix_g"][l + 1] if not last else np.ones(D_MODEL, np.float32)
        gcols = np.ascontiguousarray(np.concatenate(
            [_colform(inp["rec_norm_g"][l], 4), _colform(inp["norm_ffn_g"][l], 8), _colform(gnext, 8)], axis=1))
        maps = []
        for b, p in cores:
            tsl = slice(p * Th, (p + 1) * Th)
            maps.append(dict(h_in=np.ascontiguousarray(h[b, tsl]),
                             oattnT=np.ascontiguousarray(oattnT[b][:, tsl]),
                             orecT=np.ascontiguousarray(orecT[b][:, tsl]),
                             w_out=np.ascontiguousarray(inp["w_out"][l].astype(np.float32)),
                             w_gu=np.ascontiguousarray(inp["w_gu"][l].astype(np.float32)),
                             w_down=np.ascontiguousarray(inp["w_down"][l].astype(np.float32)),
                             gcols=gcols, ident=ident))
        res = run_bass_kernel_spmd(nc, maps, core_ids=list(range(ncores))).results
        for i, (b, p) in enumerate(cores):
            h[b, p * Th:(p + 1) * Th] = res[i]["h_out"]
        if not last:
            hnT = [np.concatenate([res[2 * b]["hnT_o"], res[2 * b + 1]["hnT_o"]], axis=1) for b in range(B)]
    return np.ascontiguousarray(h[:, N_META:N_META + SEQ]).astype(np.float32)


NLB = (T_PAD // 2) // BLK
X1_CH = [(0, 2), (2, 2), (4, 2), (6, 2), (8, 2), (10, 1)]
PAIRS = [[0, 1], [2, 3], [4, 5], [6, 7]]
REC_POS = [2, 3, 6, 7]
ATT_POS = [0, 1, 4, 5]


def emit_pre_f(C, x_in, gmix_d, ident, store_nx, end_lblock):
    P = C.P
    gcol = C.sb("gcol", [128, 8], F32)
    P.dma("sp", gcol[:], gmix_d, writes=[gcol])
    hv = x_in.rearrange("(t p) d -> p t d", p=128)
    hts = [C.sb("h%d" % i, [128, 3, D_MODEL], F32) for i in range(2)]
    outs = [C.sb("o%d" % i, [128, 8, BLK], BF16) for i in range(2)]
    ss = C.sb("ss", [128, 4], F32)
    hn = C.sb("hn", [128, 3, D_MODEL], BF16)
    for j in range(NLB):
        ht = hts[j % 2]
        oT = outs[j % 2]
        P.dma("sp", ht[:, :, :], hv[:, 3 * j:3 * j + 3, :], writes=[ht])
        emit_norm_T(C, ht, 3, gcol, oT, ident, 0, None, ss, hn)
        store_nx(j, oT)
        end_lblock(j)


def emit_post_f(C, last, w_out, w_gu, w_down, gcols, ident, sel, load_ab, load_h, store_h, store_nx, end_lblock,
                wbufs=None):
    P = C.P
    NF = D_FF // 128
    gc = C.sb("gc", [128, 20], F32)
    P.dma("sp", gc[:], gcols, writes=[gc])
    ones = C.sb("ones", [128, 128], BF16)
    C.memset("dve", ones[:], 1.0 / 512.0, [ones])
    Wout = C.sb("Wout", [128, 8, D_MODEL], BF16)
    Wgu = C.sb("Wgu", [128, 8, 2 * D_FF], BF16)
    Wdn = C.sb("Wdn", [128, NF, D_MODEL], BF16)
    GW = 512
    ngg = (D_FF + GW - 1) // GW
    Wout_b = Buf("Wout_b")
    Wgu_b = [Buf("Wgu_b%d" % g) for g in range(2 * ngg)]
    Wdn_b = [Buf("Wdn_b%d" % k) for k in range(NF // 2)]
    vo = w_out.rearrange("(c p) n -> p c n", p=128)
    vg = w_gu.rearrange("(c p) n -> p c n", p=128)
    vd = w_down.rearrange("(c p) n -> p c n", p=128)
    for c in range(8):
        for n0 in (0, 512):
            P.dma("pool", Wout[:, c, n0:n0 + 512], vo[:, c, n0:n0 + 512], writes=[Wout_b], par=True)
    for g in range(ngg):
        for half in range(2):
            n0 = half * D_FF + g * GW
            n1 = half * D_FF + min(D_FF, (g + 1) * GW)
            P.dma("pool", Wgu[:, :, n0:n1], vg[:, :, n0:n1], writes=[Wgu_b[half * ngg + g]])
    for k in range(NF // 2):
        for n0 in (0, 512):
            P.dma("pool", Wdn[:, 2 * k:2 * k + 2, n0:n0 + 512], vd[:, 2 * k:2 * k + 2, n0:n0 + 512],
                  writes=[Wdn_b[k]], par=True)
    mixT = C.sb("mixT", [128, 8, BLK], BF16)
    mixA = mixT
    mixB = C.sb("mixB", [128, 8, BLK], BF16)
    ht = C.sb("ht", [128, 3, D_MODEL], F32)
    rstd_r = C.sb("rstd_r", [128, BLK], F32)
    mixn = C.sb("mixn", [128, 4, BLK], BF16)
    ss = C.sb("ss", [128, 4], F32)
    hn = C.sb("hn", [128, 3, D_MODEL], BF16)
    hnT = C.sb("hnT", [128, 8, BLK], BF16)
    sg = C.sb("sg", [128, 2, BLK], BF16)
    actT = C.sb("actT", [128, NF, BLK], BF16)
    nxT = hnT
    for j in range(NLB):
        load_ab(j, mixA, mixB)
        load_h(j, ht)
        C.ts("dve", mixT[:, :, :], mixA[:, :, :], sel[:, 0:1], None, ALU.mult, None, [mixA, sel], [mixT])
        C.stt(mixT[:, :, :], mixB[:, :, :], sel[:, 1:2], mixT[:, :, :], ALU.mult, ALU.add, [mixB, sel, mixT], [mixT])
        for k, c in enumerate(REC_POS):
            C.act(mixn[:, k, :], mixT[:, c, :], AF.Square, [mixT], [mixn])
        for k in range(4):
            C.mm(C.pf(6, BLK), ones[:, :], mixn[:, k, :], k == 0, k == 3, [ones, mixn], [C.bank[6]])
        C.rstd(rstd_r[:, :], C.pf(6, BLK), 1.0, [C.bank[6]], [rstd_r])
        for k, c in enumerate(REC_POS):
            C.stt(mixn[:, k, :], mixT[:, c, :], gc[:, k:k + 1], rstd_r[:, :], ALU.mult, ALU.mult,
                  [mixT, gc, rstd_r], [mixn])
        for t in range(3):
            for n in range(2):
                for c in range(8):
                    if c in REC_POS:
                        lhsT = mixn[:, REC_POS.index(c), t * 128:(t + 1) * 128]
                    else:
                        lhsT = mixT[:, c, t * 128:(t + 1) * 128]
                    C.mm(C.pf(n), lhsT, Wout[:, c, n * 512:(n + 1) * 512], c == 0, c == 7,
                         [mixT, mixn, Wout_b], [C.bank[n]])
            for n in range(2):
                C.tt("dve", ht[:, t, n * 512:(n + 1) * 512], ht[:, t, n * 512:(n + 1) * 512], C.pf(n), ALU.add,
                     [ht, C.bank[n]], [ht])
        emit_norm_T(C, ht, 3, _ColView(gc, 4), hnT, ident, 6, None, ss, hn)
        for f in range(NF):
            bg = 2 + 2 * (f % 2)
            bu = bg + 1
            for c in range(8):
                C.mm(C.pf(bg, BLK), Wgu[:, c, f * 128:(f + 1) * 128], hnT[:, c, :], c == 0, c == 7,
                     [Wgu_b[(f * 128) // GW], hnT], [C.bank[bg]])
            for c in range(8):
                C.mm(C.pf(bu, BLK), Wgu[:, c, D_FF + f * 128:D_FF + (f + 1) * 128], hnT[:, c, :], c == 0, c == 7,
                     [Wgu_b[ngg + (f * 128) // GW], hnT], [C.bank[bu]])
            C.act(sg[:, f % 2, :], C.pf(bg, BLK), AF.Silu, [C.bank[bg]], [sg])
            C.tt("dve", actT[:, f, :], sg[:, f % 2, :], C.pf(bu, BLK), ALU.mult, [sg, C.bank[bu]], [actT])
        for t in range(3):
            for n in range(2):
                for f in range(NF):
                    C.mm(C.pf(n), actT[:, f, t * 128:(t + 1) * 128], Wdn[:, f, n * 512:(n + 1) * 512],
                         f == 0, f == NF - 1, [actT, Wdn_b[f // 2]], [C.bank[n]])
            for n in range(2):
                C.tt("dve", ht[:, t, n * 512:(n + 1) * 512], ht[:, t, n * 512:(n + 1) * 512], C.pf(n), ALU.add,
                     [ht, C.bank[n]], [ht])
        store_h(j, ht)
        if not last:
            emit_norm_T(C, ht, 3, _ColView(gc, 12), nxT, ident, 6, None, ss, hn)
            store_nx(j, nxT)
            end_lblock(j)


def build_fused(depth=2):
    T = T_PAD
    Th = T // 2
    NT = T // 128
    nc = bass.Bass("TRN2", target_bir_lowering=False)
    with contextlib.ExitStack() as st:
        C = Ctx(nc, st)
        P = C.P
        IN = "ExternalInput"
        x_in = C.dram("x_in", [Th, D_MODEL], F32, IN)
        sel_d = C.dram("sel", [128, 2], F32, IN)
        gmix0 = C.dram("gmix0", [128, 8], F32, IN)
        cs_d = C.dram("cs", [128, NT, 16], F32, IN)
        tri_d = C.dram("tri", [128, 128], BF16, IN)
        ident_d = C.dram("ident", [128, 128], BF16, IN)
        L = []
        for l in range(depth):
            L.append(dict(
                w_in=C.dram("w_in%d" % l, [D_MODEL, 1280], F32, IN),
                gqk=C.dram("gqk%d" % l, [128, 512], F32, IN),
                lamv=C.dram("lamv%d" % l, [128, 4, 64], F32, IN),
                gsub=C.dram("gsub%d" % l, [128, 384], F32, IN),
                pc=C.dram("pc%d" % l, [128, 2, 8], F32, IN),
                wrg=C.dram("wrg%d" % l, [4, 64, 64], F32, IN),
                wig=C.dram("wig%d" % l, [4, 64, 64], F32, IN),
                w_out=C.dram("w_out%d" % l, [1024, D_MODEL], F32, IN),
                w_gu=C.dram("w_gu%d" % l, [D_MODEL, 2 * D_FF], F32, IN),
                w_down=C.dram("w_down%d" % l, [D_FF, D_MODEL], F32, IN),
                gcols=C.dram("gcols%d" % l, [128, 20], F32, IN)))
        h_out = C.dram("h_out", [Th, D_MODEL], F32, "ExternalOutput")
        hres = nc.dram_tensor("hres", [Th, D_MODEL], F32).ap()
        hres_b = [Buf("hres%d" % j) for j in range(NLB)]
        x1s = [nc.dram_tensor("x1s%d" % i, [D_MODEL, n * BLK], BF16).ap() for i, (b0, n) in enumerate(X1_CH)]
        x1d = [nc.dram_tensor("x1d%d" % i, [2 * D_MODEL, n * BLK], BF16).ap() for i, (b0, n) in enumerate(X1_CH)]
        x1s_b = [Buf("x1s%d" % i) for i in range(len(X1_CH))]
        x1d_b = [Buf("x1d%d" % i) for i in range(len(X1_CH))]
        NX2 = (T // BLK) // 2
        x2s = [nc.dram_tensor("x2s%d" % i, [512, 2 * BLK], BF16).ap() for i in range(NX2)]
        x2d = [nc.dram_tensor("x2d%d" % i, [1024, 2 * BLK], BF16).ap() for i in range(NX2)]
        x2s_b = [Buf("x2s%d" % i) for i in range(NX2)]
        x2d_b = [Buf("x2d%d" % i) for i in range(NX2)]

        WB = []
        conv = []
        for l in range(0):
            d = {}
            for key, shp in (("w_out", [1024, D_MODEL]), ("w_gu", [D_MODEL, 2 * D_FF]), ("w_down", [D_FF, D_MODEL])):
                dst = nc.dram_tensor("%s_b%d" % (key, l), shp, BF16).ap()
                bufw = Buf("%s_b%d" % (key, l))
                d[key] = (dst, bufw)
                src = L[l][key]
                for r0 in range(0, shp[0], 128):
                    for c0 in range(0, shp[1], 2048):
                        c1 = min(shp[1], c0 + 2048)
                        conv.append((dst[r0:r0 + 128, c0:c1], src[r0:r0 + 128, c0:c1], bufw))
            WB.append(d)
        conv.reverse()

        def pump_conv(n):
            for _ in range(n):
                if not conv:
                    return
                o, i_, bufw = conv.pop()
                P.dma("pool", o, i_, writes=[bufw], par=True)

        C.consts()
        ident = C.sb("ident_sb", [128, 128], BF16)
        sel = C.sb("sel_sb", [128, 2], F32)
        P.dma("sp", ident[:], ident_d, writes=[ident])
        P.dma("sp", sel[:], sel_d, writes=[sel])
        C.use_arena(53100)

        def x1_chunk_of(jl):
            for i, (b0, n) in enumerate(X1_CH):
                if b0 <= jl < b0 + n:
                    return i, jl - b0, n
            raise AssertionError

        def store_nx(jl, oT):
            i, k, n = x1_chunk_of(jl)
            P.dma("sp", x1s[i].rearrange("(c p) t -> p c t", p=128)[:, :, k * BLK:(k + 1) * BLK], oT[:, :, :],
                  reads=[oT], writes=[x1s_b[i]])

        def end_lblock(jl):
            i, k, n = x1_chunk_of(jl)
            if k == n - 1:
                src, dst = x1s[i], x1d[i]
                P.cc(lambda e: e.collective_compute("AllGather", ALU.bypass, replica_groups=PAIRS,
                                                    ins=[src.opt()], outs=[dst.opt()]),
                     reads=[x1s_b[i]], writes=[x1d_b[i]])

        def mix_load_x(j, xT):
            half, jl = j // NLB, j % NLB
            i, k, n = x1_chunk_of(jl)
            v = x1d[i].rearrange("(r c p) t -> r p c t", r=2, p=128)
            P.dma("sp", xT[:, :, :], v[half][:, :, k * BLK:(k + 1) * BLK], reads=[x1d_b[i]], writes=[xT])

        def store_orec(j, ct, Y):
            c, k = j // 2, j % 2
            P.dma("sp", x2s[c][256 + ct * 128:256 + (ct + 1) * 128, k * BLK:(k + 1) * BLK], Y[:, :],
                  reads=[Y], writes=[x2s_b[c]])

        def store_oattn(j, hl, oT):
            c, k = j // 2, j % 2
            P.dma("sp", x2s[c][hl * 128:(hl + 1) * 128, k * BLK:(k + 1) * BLK], oT[:, :],
                  reads=[oT], writes=[x2s_b[c]])

        def mix_end_block(j):
            if j % 2 == 1:
                c = j // 2
                src, dst = x2s[c], x2d[c]
                P.cc(lambda e: e.collective_compute("AllGather", ALU.bypass, replica_groups=PAIRS,
                                                    ins=[src.opt()], outs=[dst.opt()]),
                     reads=[x2s_b[c]], writes=[x2d_b[c]])

        def load_ab(jl, mixA, mixB):
            for tile_, j in ((mixA, jl), (mixB, NLB + jl)):
                c, k = j // 2, j % 2
                v = x2d[c].rearrange("(q p) t -> p q t", p=128)
                P.dma("sp", tile_[:, :, :], v[:, :, k * BLK:(k + 1) * BLK], reads=[x2d_b[c]], writes=[tile_])

        keep = C.arena_off
        emit_pre_f(C, x_in, gmix0, ident, store_nx, end_lblock)
        for l in range(depth):
            last = l == depth - 1
            lam_init = 0.8 - 0.6 * math.exp(-0.3 * l)
            P.barrier()
            C.arena_reset(keep)
            emit_mix(C, T, lam_init, None, L[l]["w_in"], L[l]["gqk"], L[l]["lamv"], L[l]["gsub"], L[l]["pc"],
                     L[l]["wrg"], L[l]["wig"], cs_d, tri_d, ident_d, None, None,
                     hooks=dict(load_x=mix_load_x, store_orec=store_orec, store_oattn=store_oattn,
                                end_block=mix_end_block))
            P.barrier()
            C.arena_reset(keep)
            src_h = x_in if l == 0 else hres
            dst_h = h_out if last else hres

            def load_h(jl, ht, src_h=src_h, l=l):
                rd = [] if l == 0 else [hres_b[jl]]
                P.dma("sp", ht[:, :, :], src_h.rearrange("(t p) d -> p t d", p=128)[:, 3 * jl:3 * jl + 3, :],
                      reads=rd, writes=[ht])

            def store_h(jl, ht, dst_h=dst_h, last=last):
                wr = [] if last else [hres_b[jl]]
                P.dma("sp", dst_h.rearrange("(t p) d -> p t d", p=128)[:, 3 * jl:3 * jl + 3, :], ht[:, :, :],
                      reads=[ht], writes=wr)

            emit_post_f(C, last, L[l]["w_out"], L[l]["w_gu"], L[l]["w_down"], L[l]["gcols"], ident, sel,
                        load_ab, load_h, store_h, store_nx, end_lblock)
        P.emit()
        print("fused stats", P.stats)
    return nc


def kernel(**inp):
    inp = {k: np.asarray(v) for k, v in inp.items()}
    x = inp["x"]
    B = x.shape[0]
    depth = inp["w_in"].shape[0]
    T = T_PAD
    Th = T // 2
    cores = [(b, p) for b in range(B) for p in range(2)]
    cs, tri, ident = _const_tables(T)
    h = np.zeros((B, T, D_MODEL), np.float32)
    h[:, :N_META] = inp["meta_tokens"][None]
    h[:, N_META:N_META + SEQ] = x
    nc = build_fused(depth)
    f = np.float32
    shared = {}
    for l in range(depth):
        last = l == depth - 1
        gnext = inp["norm_mix_g"][l + 1] if not last else np.ones(D_MODEL, f)
        shared["gcols%d" % l] = np.ascontiguousarray(np.concatenate(
            [_colform(inp["rec_norm_g"][l], 4), _colform(inp["norm_ffn_g"][l], 8), _colform(gnext, 8)], axis=1))
        wo = inp["w_out"][l].astype(f)
        shared["w_out%d" % l] = np.ascontiguousarray(np.concatenate([wo[0:256], wo[512:768], wo[256:512], wo[768:1024]], 0))
        shared["w_gu%d" % l] = np.ascontiguousarray(inp["w_gu"][l].astype(f))
        shared["w_down%d" % l] = np.ascontiguousarray(inp["w_down"][l].astype(f))
    gm0 = _colform(inp["norm_mix_g"][0], 8)
    maps = []
    for b, p in cores:
        d = dict(shared)
        d["x_in"] = np.ascontiguousarray(h[b, p * Th:(p + 1) * Th])
        selv = np.zeros((128, 2), f)
        selv[:, p] = 1.0
        d["sel"] = selv
        d["gmix0"] = gm0
        d["cs"] = cs
        d["tri"] = tri
        d["ident"] = ident
        for l in range(depth):
            m = _mix_inputs(inp, l, p, cs, tri, ident)
            for k in ("w_in", "gqk", "lamv", "gsub", "pc", "wrg", "wig"):
                d["%s%d" % (k, l)] = m[k]
        maps.append(d)
    res = run_bass_kernel_spmd(nc, maps, core_ids=list(range(len(cores)))).results
    out = np.zeros((B, T, D_MODEL), np.float32)
    for i, (b, p) in enumerate(cores):
        out[b, p * Th:(p + 1) * Th] = res[i]["h_out"]
    return np.ascontiguousarray(out[:, N_META:N_META + SEQ])
```

```python
import contextlib
import math
import numpy as np
import ml_dtypes
import concourse.bass as bass
import concourse.mybir as mybir
from concourse.bass_utils import run_bass_kernel_spmd

F32 = mybir.dt.float32
BF16 = mybir.dt.bfloat16
AF = mybir.ActivationFunctionType
ALU = mybir.AluOpType
AX = mybir.AxisListType

D_MODEL = 1024
N_META = 16
SEQ = 8192
T_REAL = SEQ + N_META
BLK = 384
T_PAD = 8448
D_FF = 2816
EPS = 1e-6
ROPE_THETA = 500000.0
GELU_C = 1.5957691216057308

COMPUTE = ("pe", "act", "dve", "pool")
QUEUES = ("sp", "act", "pool")


class Buf:
    __slots__ = ("w", "r", "name")

    def __init__(self, name=""):
        self.w = []
        self.r = []
        self.name = name


class Tile(Buf):
    __slots__ = ("t",)

    def __init__(self, t, name=""):
        Buf.__init__(self, name)
        self.t = t

    def __getitem__(self, idx):
        return self.t[idx]


class Op:
    __slots__ = ("eng", "fn", "deps", "seq", "is_dma", "ring", "cnt", "clock",
                 "waits", "signal", "sigidx", "inc")

    def __init__(self, eng, fn, is_dma):
        self.eng = eng
        self.fn = fn
        self.deps = []
        self.is_dma = is_dma
        self.ring = None
        self.cnt = 0
        self.waits = []
        self.signal = False
        self.sigidx = 0
        self.clock = None
        self.inc = 16


class Prog:
    def __init__(self, nc, ring_sizes=None):
        self.nc = nc
        self.ops = {e: [] for e in ("pe", "act", "dve", "pool", "sp")}
        self.all = []
        self.ring_sizes = ring_sizes or {"sp": 16, "act": 8, "pool": 12}
        self.dma_n = {q: 0 for q in QUEUES}
        self.ring_last = {q: [None] * self.ring_sizes[q] for q in QUEUES}
        self.ring_cnt = {q: [0] * self.ring_sizes[q] for q in QUEUES}
        self.cc_last = None
        self.cc_n = 0

    def _deps(self, op, reads, writes, par=False):
        deps = []
        for t in reads:
            deps.extend(t.w)
        for t in writes:
            if par and not t.r:
                continue
            deps.extend(t.w)
            deps.extend(t.r)
        for t in reads:
            t.r.append(op)
        for t in writes:
            if par and not t.r:
                t.w = t.w + [op]
            else:
                t.w = [op]
                t.r = []
        seen = set()
        out = []
        for d in deps:
            if id(d) not in seen and d is not op:
                seen.add(id(d))
                out.append(d)
        op.deps = out

    def _add(self, o):
        o.seq = len(self.ops[o.eng])
        self.ops[o.eng].append(o)
        self.all.append(o)
        return o

    def op(self, eng, fn, reads=(), writes=()):
        o = Op(eng, fn, False)
        self._deps(o, reads, writes)
        return self._add(o)

    def dma(self, q, out_ap, in_ap, reads=(), writes=(), par=False, **kw):
        def fn(eng):
            return eng.dma_start(out=out_ap, in_=in_ap, **kw)
        return self.dma_like(q, fn, reads, writes, par)

    def dma_like(self, q, fn, reads=(), writes=(), par=False):
        o = Op(q, fn, True)
        self._deps(o, reads, writes, par)
        n = self.dma_n[q]
        self.dma_n[q] = n + 1
        slot = n % self.ring_sizes[q]
        prev = self.ring_last[q][slot]
        if prev is not None:
            o.deps.append(prev)
        self.ring_last[q][slot] = o
        self.ring_cnt[q][slot] += 16
        o.ring = (q, slot)
        o.cnt = self.ring_cnt[q][slot]
        return self._add(o)

    def cc(self, fn, reads=(), writes=()):
        o = Op("pool", fn, True)
        self._deps(o, reads, writes)
        if self.cc_last is not None:
            o.deps.append(self.cc_last)
        self.cc_last = o
        self.cc_n += 1
        o.ring = ("cc", 0)
        o.cnt = self.cc_n
        o.inc = 1
        return self._add(o)

    def barrier(self):
        lasts = []
        for e in self.ops:
            if self.ops[e]:
                lasts.append(self.ops[e][-1])
        for q in QUEUES:
            for o in self.ring_last[q]:
                if o is not None:
                    lasts.append(o)
        if self.cc_last is not None:
            lasts.append(self.cc_last)
        for e in ("pe", "act", "dve", "pool", "sp"):
            o = Op(e, None, False)
            o.deps = [d for d in lasts]
            self._add(o)

    def emit(self, final_wait_eng="sp"):
        nc = self.nc
        fin = Op(final_wait_eng, None, False)
        for e in self.ops:
            if self.ops[e]:
                fin.deps.append(self.ops[e][-1])
        for q in QUEUES:
            for o in self.ring_last[q]:
                if o is not None:
                    fin.deps.append(o)
        if self.cc_last is not None:
            fin.deps.append(self.cc_last)
        self._add(fin)

        cur = {e: {} for e in self.ops}
        for o in self.all:
            ck = cur[o.eng]
            waits = {}
            for d in o.deps:
                if d.is_dma:
                    key, val = d.ring, d.cnt
                else:
                    if d.eng == "pe" and o.eng == "pe":
                        continue
                    key, val = d.eng, d.seq + 1
                if ck.get(key, 0) >= val:
                    continue
                if waits.get(key, (0, None))[0] < val:
                    waits[key] = (val, d)
            for key, (val, d) in waits.items():
                for k2, v2 in d.clock.items():
                    if ck.get(k2, 0) < v2:
                        ck[k2] = v2
                if ck.get(key, 0) < val:
                    ck[key] = val
                if not d.is_dma:
                    d.signal = True
            o.waits = [(key, val, d) for key, (val, d) in waits.items()]
            o.clock = dict(ck)
        for e in self.ops:
            n = 0
            for o in self.ops[e]:
                if o.signal and not o.is_dma:
                    n += 1
                    o.sigidx = n
        with contextlib.ExitStack() as st:
            sems = {}
            for e in self.ops:
                sems[e] = st.enter_context(nc.semaphore("p_" + e))
            for q in QUEUES:
                for s in range(self.ring_sizes[q]):
                    sems[(q, s)] = st.enter_context(nc.semaphore("r_%s%d" % (q, s)))
            sems[("cc", 0)] = st.enter_context(nc.semaphore("cc_sem"))
            block = st.enter_context(nc.Block())
            stats = {"waits": 0, "ops": 0}

            def run(ename):
                def body(eng):
                    for o in self.ops[ename]:
                        for key, val, d in o.waits:
                            if d.is_dma:
                                eng.wait_ge(sems[key], val)
                            else:
                                eng.wait_ge(sems[key], d.sigidx)
                            stats["waits"] += 1
                        if o.fn is None:
                            if o.signal:
                                eng.nop().then_inc(sems[ename], 1)
                            continue
                        ins = o.fn(eng)
                        stats["ops"] += 1
                        if o.is_dma:
                            ins.then_inc(sems[o.ring], o.inc)
                        elif o.signal:
                            ins.then_inc(sems[ename], 1)
                return body

            block.tensor(run("pe"))
            block.scalar(run("act"))
            block.vector(run("dve"))
            block.gpsimd(run("pool"))
            block.sync(run("sp"))
            self.stats = stats
        return self


class Ctx:
    def __init__(self, nc, st):
        self.nc = nc
        self.st = st
        self.P = Prog(nc)
        ps = st.enter_context(nc.psum_tensor("psum_all", [128, 8, 512], F32))
        self.ps = ps
        self.bank = [Buf("bank%d" % i) for i in range(8)]

    def sb(self, name, shape, dt):
        if getattr(self, "arena", None) is None:
            return Tile(self.st.enter_context(self.nc.sbuf_tensor(name, shape, dt)), name)
        esz = 4 if dt == F32 else 2
        n = 1
        for d in shape[1:]:
            n *= d
        words = (n * esz + 3) // 4
        words = (words + 7) // 8 * 8
        a = self.arena_off
        assert a + words <= self.arena_words, "arena overflow %s need %d have %d" % (name, words, self.arena_words - a)
        self.arena_off = a + words
        ap = self.arena[:, a:a + (n * esz + 3) // 4]
        if dt != F32:
            ap = ap.bitcast(dt)
            if ap.shape[-1] != n:
                ap = ap[:, 0:n]
        if len(shape) > 2:
            names = " ".join("d%d" % i for i in range(1, len(shape)))
            kw = {"d%d" % i: shape[i] for i in range(1, len(shape))}
            ap = ap.rearrange("p (%s) -> p %s" % (names, names), **kw)
        return Tile(ap, name)

    def use_arena(self, words):
        self.arena = self.st.enter_context(self.nc.sbuf_tensor("arena", [128, words], F32))
        self.arena_words = words
        self.arena_off = 0

    def arena_reset(self, keep=0):
        self.arena_off = keep

    def dram(self, name, shape, dt, kind):
        return self.nc.dram_tensor(name, shape, dt, kind=kind).ap()

    def pf(self, b, n=512, off=0):
        return self.ps[:, b, off:off + n]

    def pb(self, b):
        return self.ps[:, b, :].bitcast(BF16)

    def mm(self, out, lhsT, rhs, start, stop, R, W, **kw):
        self.P.op("pe", lambda e: e.matmul(out, lhsT, rhs, start=start, stop=stop, **kw), R, W)

    def tr(self, out, in_, ident, R, W):
        self.P.op("pe", lambda e: e.transpose(out, in_, ident[:]), list(R) + [ident], W)

    def act(self, out, in_, func, R, W, **kw):
        self.P.op("act", lambda e: e.activation(out=out, in_=in_, func=func, **kw), R, W)

    def tt(self, eng, out, in0, in1, op, R, W):
        self.P.op(eng, lambda e: e.tensor_tensor(out=out, in0=in0, in1=in1, op=op), R, W)

    def ts(self, eng, out, in0, s1, s2, op0, op1, R, W, **kw):
        if s2 is None:
            self.P.op(eng, lambda e: e.tensor_scalar(out=out, in0=in0, scalar1=s1, scalar2=None, op0=op0, **kw), R, W)
        else:
            self.P.op(eng, lambda e: e.tensor_scalar(out=out, in0=in0, scalar1=s1, scalar2=s2, op0=op0, op1=op1, **kw), R, W)

    def stt(self, out, in0, scalar, in1, op0, op1, R, W):
        self.P.op("dve", lambda e: e.scalar_tensor_tensor(out=out, in0=in0, scalar=scalar, in1=in1, op0=op0, op1=op1), R, W)

    def cp(self, eng, out, in_, R, W):
        if eng == "act":
            self.P.op("act", lambda e: e.activation(out=out, in_=in_, func=AF.Copy), R, W)
        else:
            self.P.op(eng, lambda e: e.tensor_copy(out=out, in_=in_), R, W)

    def red(self, out, in_, R, W):
        self.P.op("dve", lambda e: e.tensor_reduce(out=out, in_=in_, axis=AX.X, op=ALU.add), R, W)

    def scan(self, out, d0, d1, init, R, W):
        self.P.op("dve", lambda e: e.tensor_tensor_scan(out=out, data0=d0, data1=d1, initial=init,
                                                        op0=ALU.mult, op1=ALU.add), R, W)

    def recip(self, out, in_, R, W):
        self.P.op("dve", lambda e: e.reciprocal(out=out, in_=in_), R, W)

    def memset(self, eng, ap, val, W):
        self.P.op(eng, lambda e: e.memset(ap, val), [], W)

    def rstd(self, out, in_, n, R, W):
        self.act(out, in_, AF.Ln, list(R) + [self.eps_col], W, scale=1.0 / n, bias=self.eps_col[:, 0:1])
        self.act(out, out, AF.Exp, W, W, scale=-0.5)

    def consts(self):
        self.eps_col = self.sb("eps_col", [128, 1], F32)
        self.memset("dve", self.eps_col[:], EPS, [self.eps_col])


def emit_norm_T(C, h_t, nt, gcol, outT, ident, pbank0, scratch, ss, hn):
    for t in range(nt):
        C.act(hn[:, t, :], h_t[:, t, :], AF.Square, [h_t], [hn, ss], accum_out=ss[:, t:t + 1])
    C.rstd(ss[:, 0:nt], ss[:, 0:nt], float(D_MODEL), [ss], [ss])
    for t in range(nt):
        C.ts("dve", hn[:, t, :], h_t[:, t, :], ss[:, t:t + 1], None, ALU.mult, None, [h_t, ss], [hn])
    w = nt * 128
    for c in range(8):
        b = pbank0 + (c % 2)
        pv = C.pb(b)[:, 0:w]
        for t in range(nt):
            C.tr(pv[:, t * 128:(t + 1) * 128], hn[:, t, c * 128:(c + 1) * 128], ident, [hn], [C.bank[b]])
        eng = "dve" if c % 2 == 0 else "pool"
        if eng == "pool":
            C.act(outT[:, c, 0:w], pv, AF.Copy, [C.bank[b], gcol], [outT], scale=gcol[:, c:c + 1])
        else:
            C.ts("dve", outT[:, c, 0:w], pv, gcol[:, c:c + 1], None, ALU.mult, None, [C.bank[b], gcol], [outT])


def load_cast_weight(C, dst, src_ap, nchunk, ncol, colstep=512):
    v = src_ap.rearrange("(c p) n -> p c n", p=128)
    for c in range(nchunk):
        for n0 in range(0, ncol, colstep):
            n1 = min(ncol, n0 + colstep)
            C.P.dma("pool", dst[:, c, n0:n1], v[:, c, n0:n1], writes=[dst], par=True)


def build_pre(Th):
    nb = Th // BLK
    nc = bass.Bass("TRN2", target_bir_lowering=False)
    with contextlib.ExitStack() as st:
        C = Ctx(nc, st)
        h_in = C.dram("h_in", [Th, D_MODEL], F32, "ExternalInput")
        gmix = C.dram("gmix", [128, 8], F32, "ExternalInput")
        ident_d = C.dram("ident", [128, 128], BF16, "ExternalInput")
        hnT_o = C.dram("hnT_o", [D_MODEL, Th], BF16, "ExternalOutput")
        C.consts()
        ident = C.sb("ident_sb", [128, 128], BF16)
        gcol = C.sb("gcol", [128, 8], F32)
        C.P.dma("sp", ident[:], ident_d, writes=[ident])
        C.P.dma("sp", gcol[:], gmix, writes=[gcol])
        hv = h_in.rearrange("(t p) d -> p t d", p=128)
        ov = hnT_o.rearrange("(c p) t -> p c t", p=128)
        hts = [C.sb("h%d" % i, [128, 3, D_MODEL], F32) for i in range(2)]
        outs = [C.sb("o%d" % i, [128, 8, BLK], BF16) for i in range(2)]
        scratch = None
        ss = C.sb("ss", [128, 4], F32)
        hn = C.sb("hn", [128, 3, D_MODEL], BF16)
        for j in range(nb):
            ht = hts[j % 2]
            oT = outs[j % 2]
            C.P.dma("sp", ht[:, :, :], hv[:, 3 * j:3 * j + 3, :], writes=[ht])
            emit_norm_T(C, ht, 3, gcol, oT, ident, 0, scratch, ss, hn)
            C.P.dma("sp", ov[:, :, j * BLK:(j + 1) * BLK], oT[:, :, :], reads=[oT])
        C.P.emit()
    return nc


def build_post(Th, last):
    nb = Th // BLK
    NF = D_FF // 128
    nc = bass.Bass("TRN2", target_bir_lowering=False)
    with contextlib.ExitStack() as st:
        C = Ctx(nc, st)
        h_in = C.dram("h_in", [Th, D_MODEL], F32, "ExternalInput")
        oattnT = C.dram("oattnT", [512, Th], BF16, "ExternalInput")
        orecT = C.dram("orecT", [512, Th], BF16, "ExternalInput")
        w_out = C.dram("w_out", [1024, D_MODEL], F32, "ExternalInput")
        w_gu = C.dram("w_gu", [D_MODEL, 2 * D_FF], F32, "ExternalInput")
        w_down = C.dram("w_down", [D_FF, D_MODEL], F32, "ExternalInput")
        gcols = C.dram("gcols", [128, 20], F32, "ExternalInput")
        ident_d = C.dram("ident", [128, 128], BF16, "ExternalInput")
        h_out = C.dram("h_out", [Th, D_MODEL], F32, "ExternalOutput")
        if not last:
            hnT_o = C.dram("hnT_o", [D_MODEL, Th], BF16, "ExternalOutput")
        C.consts()
        P = C.P
        ident = C.sb("ident_sb", [128, 128], BF16)
        gc = C.sb("gc", [128, 20], F32)
        P.dma("sp", ident[:], ident_d, writes=[ident])
        P.dma("sp", gc[:], gcols, writes=[gc])
        ones = C.sb("ones", [128, 128], BF16)
        C.memset("dve", ones[:], 1.0 / 512.0, [ones])
        Wout = C.sb("Wout", [128, 8, D_MODEL], BF16)
        Wgu = C.sb("Wgu", [128, 8, 2 * D_FF], BF16)
        Wdn = C.sb("Wdn", [128, NF, D_MODEL], BF16)
        load_cast_weight(C, Wout, w_out, 8, D_MODEL)
        load_cast_weight(C, Wgu, w_gu, 8, 2 * D_FF)
        load_cast_weight(C, Wdn, w_down, NF, D_MODEL)

        hv = h_in.rearrange("(t p) d -> p t d", p=128)
        hov = h_out.rearrange("(t p) d -> p t d", p=128)
        av = oattnT.rearrange("(c p) t -> p c t", p=128)
        rv = orecT.rearrange("(c p) t -> p c t", p=128)
        if not last:
            ov = hnT_o.rearrange("(c p) t -> p c t", p=128)

        mixTs = [C.sb("mixT%d" % i, [128, 8, BLK], BF16) for i in range(1)]
        ht = C.sb("ht", [128, 3, D_MODEL], F32)
        rstd_r = C.sb("rstd_r", [128, BLK], F32)
        mixn = C.sb("mixn", [128, 4, BLK], BF16)
        sq = mixn
        scratch = None
        ss = C.sb("ss", [128, 4], F32)
        hn = C.sb("hn", [128, 3, D_MODEL], BF16)
        hnT = C.sb("hnT", [128, 8, BLK], BF16)
        sg = C.sb("sg", [128, 2, BLK], F32)
        actT = C.sb("actT", [128, NF, BLK], BF16)
        nxT = hnT

        for j in range(nb):
            mixT = mixTs[0]
            P.dma("sp", mixT[:, 0:4, :], av[:, :, j * BLK:(j + 1) * BLK], writes=[mixT])
            P.dma("sp", mixT[:, 4:8, :], rv[:, :, j * BLK:(j + 1) * BLK], writes=[mixT])
            P.dma("sp", ht[:, :, :], hv[:, 3 * j:3 * j + 3, :], writes=[ht])
            C.act(sq[:, :, :], mixT[:, 4:8, :], AF.Square, [mixT], [sq])
            for c in range(4):
                C.mm(C.pf(6, BLK), ones[:, :], sq[:, c, :], c == 0, c == 3, [ones, sq], [C.bank[6]])
            C.rstd(rstd_r[:, :], C.pf(6, BLK), 1.0, [C.bank[6]], [rstd_r])
            for c in range(4):
                C.stt(mixn[:, c, :], mixT[:, 4 + c, :], gc[:, c:c + 1], rstd_r[:, :], ALU.mult, ALU.mult,
                      [mixT, gc, rstd_r], [mixn])
            for t in range(3):
                for n in range(2):
                    for c in range(8):
                        lhsT = mixT[:, c, t * 128:(t + 1) * 128] if c < 4 else mixn[:, c - 4, t * 128:(t + 1) * 128]
                        C.mm(C.pf(n), lhsT, Wout[:, c, n * 512:(n + 1) * 512], c == 0, c == 7,
                             [mixT, mixn, Wout], [C.bank[n]])
                for n in range(2):
                    C.tt("dve", ht[:, t, n * 512:(n + 1) * 512], ht[:, t, n * 512:(n + 1) * 512], C.pf(n), ALU.add,
                         [ht, C.bank[n]], [ht])
            emit_norm_T(C, ht, 3, _ColView(gc, 4), hnT, ident, 6, scratch, ss, hn)
            for f in range(NF):
                bg = 2 + 2 * (f % 2)
                bu = bg + 1
                for c in range(8):
                    C.mm(C.pf(bg, BLK), Wgu[:, c, f * 128:(f + 1) * 128], hnT[:, c, :], c == 0, c == 7,
                         [Wgu, hnT], [C.bank[bg]])
                for c in range(8):
                    C.mm(C.pf(bu, BLK), Wgu[:, c, D_FF + f * 128:D_FF + (f + 1) * 128], hnT[:, c, :], c == 0, c == 7,
                         [Wgu, hnT], [C.bank[bu]])
                C.act(sg[:, f % 2, :], C.pf(bg, BLK), AF.Silu, [C.bank[bg]], [sg])
                C.tt("dve", actT[:, f, :], sg[:, f % 2, :], C.pf(bu, BLK), ALU.mult, [sg, C.bank[bu]], [actT])
            for t in range(3):
                for n in range(2):
                    for f in range(NF):
                        C.mm(C.pf(n), actT[:, f, t * 128:(t + 1) * 128], Wdn[:, f, n * 512:(n + 1) * 512],
                             f == 0, f == NF - 1, [actT, Wdn], [C.bank[n]])
                for n in range(2):
                    C.tt("dve", ht[:, t, n * 512:(n + 1) * 512], ht[:, t, n * 512:(n + 1) * 512], C.pf(n), ALU.add,
                         [ht, C.bank[n]], [ht])
            P.dma("sp", hov[:, 3 * j:3 * j + 3, :], ht[:, :, :], reads=[ht])
            if not last:
                emit_norm_T(C, ht, 3, _ColView(gc, 12), nxT, ident, 6, scratch, ss, hn)
                P.dma("sp", ov[:, :, j * BLK:(j + 1) * BLK], nxT[:, :, :], reads=[nxT])
        P.emit()
        print("post stats", P.stats)
    return nc


class _ColView:
    def __init__(self, base, off):
        self.base = base
        self.off = off

    @property
    def w(self):
        return self.base.w

    @w.setter
    def w(self, v):
        self.base.w = v

    @property
    def r(self):
        return self.base.r

    @r.setter
    def r(self, v):
        self.base.r = v

    def __getitem__(self, idx):
        p, c = idx
        start = (c.start or 0) + self.off
        stop = c.stop + self.off
        return self.base.t[p, start:stop]


def build_mix(T, layer):
    NB = T // BLK
    NT = T // 128
    lam_init = 0.8 - 0.6 * math.exp(-0.3 * layer)
    nc = bass.Bass("TRN2", target_bir_lowering=False)
    with contextlib.ExitStack() as st:
        C = Ctx(nc, st)
        P = C.P
        hnT_d = C.dram("hnT", [D_MODEL, T], BF16, "ExternalInput")
        w_in = C.dram("w_in", [D_MODEL, 1280], F32, "ExternalInput")
        gqk_d = C.dram("gqk", [128, 512], F32, "ExternalInput")
        lamv_d = C.dram("lamv", [128, 4, 64], F32, "ExternalInput")
        gsub_d = C.dram("gsub", [128, 384], F32, "ExternalInput")
        pc_d = C.dram("pc", [128, 2, 8], F32, "ExternalInput")
        wrg_d = C.dram("wrg", [4, 64, 64], F32, "ExternalInput")
        wig_d = C.dram("wig", [4, 64, 64], F32, "ExternalInput")
        cs_d = C.dram("cs", [128, NT, 16], F32, "ExternalInput")
        tri_d = C.dram("tri", [128, 128], BF16, "ExternalInput")
        ident_d = C.dram("ident", [128, 128], BF16, "ExternalInput")
        oattnT_o = C.dram("oattnT_o", [256, T], BF16, "ExternalOutput")
        orecT_o = C.dram("orecT_o", [256, T], BF16, "ExternalOutput")
        C.consts()
        emit_mix(C, T, lam_init, hnT_d, w_in, gqk_d, lamv_d, gsub_d, pc_d, wrg_d, wig_d, cs_d, tri_d, ident_d,
                 oattnT_o, orecT_o)
        P.emit()
        print("mix stats", P.stats)
    return nc


def emit_mix(C, T, lam_init, hnT_d, w_in, gqk_d, lamv_d, gsub_d, pc_d, wrg_d, wig_d, cs_d, tri_d, ident_d,
             oattnT_o, orecT_o, hooks=None):
    P = C.P
    NB = T // BLK
    NT = T // 128
    ident = C.sb("ident_sb", [128, 128], BF16)
    tri = C.sb("tri_sb", [128, 128], BF16)
    Gqk = C.sb("Gqk", [128, 512], F32)
    Gsub = C.sb("Gsub", [128, 3, 128], F32)
    lamt = C.sb("lamt", [128, 4, 64], F32)
    pcs = C.sb("pcs", [128, 2, 8], F32)
    CS = C.sb("CS", [128, NT, 16], F32)
    P.dma("sp", ident[:], ident_d, writes=[ident])
    P.dma("sp", tri[:], tri_d, writes=[tri])
    P.dma("sp", Gqk[:], gqk_d, writes=[Gqk])
    P.dma("sp", Gsub[:, :, :], gsub_d.rearrange("p (q e) -> p q e", e=128), writes=[Gsub])
    P.dma("sp", lamt[:], lamv_d, writes=[lamt])
    P.dma("sp", pcs[:], pc_d, writes=[pcs])
    P.dma("sp", CS[:], cs_d, writes=[CS])
    one_col = C.sb("one_col", [128, 1], F32)
    C.memset("dve", one_col[:], 1.0, [one_col])
    Win = C.sb("Win", [128, 8, 1280], BF16)
    load_cast_weight(C, Win, w_in, 8, 1280)
    WRG = C.sb("WRG", [128, 2, 128], BF16)
    WIG = C.sb("WIG", [128, 2, 128], BF16)
    C.memset("dve", WRG[:], 0.0, [WRG])
    C.memset("dve", WIG[:], 0.0, [WIG])
    for ct in range(2):
        for bl in range(2):
            P.dma("pool", WRG[bl * 64:(bl + 1) * 64, ct, bl * 64:(bl + 1) * 64], wrg_d[2 * ct + bl], writes=[WRG])
            P.dma("pool", WIG[bl * 64:(bl + 1) * 64, ct, bl * 64:(bl + 1) * 64], wig_d[2 * ct + bl], writes=[WIG])
    der = C.sb("der", [128, 2, 2], F32)
    lsl = C.sb("lsl", [128, 2], F32)
    C.act(lsl[:, :], pcs[:, :, 7], AF.Sigmoid, [pcs], [lsl])
    C.act(lsl[:, :], lsl[:, :], AF.Ln, [lsl], [lsl])
    C.ts("dve", der[:, :, 0], lsl[:, :], 8.0, None, ALU.mult, None, [lsl], [der])
    C.ts("dve", der[:, :, 1], lsl[:, :], 16.0, None, ALU.mult, None, [lsl], [der])
    lprod = C.sb("lprod", [128, 2, 64], F32)
    lsum = C.sb("lsum", [128, 2], F32)
    lam_col = C.sb("lam_col", [128, 1], F32)
    C.tt("dve", lprod[:, :, :], lamt[:, 0:2, :], lamt[:, 2:4, :], ALU.mult, [lamt], [lprod])
    C.red(lsum[:, :], lprod[:, :, :], [lprod], [lsum])
    C.act(lsum[:, :], lsum[:, :], AF.Exp, [lsum], [lsum])
    C.tt("dve", lam_col[:, :], lsum[:, 0:1], lsum[:, 1:2], ALU.subtract, [lsum], [lam_col])
    C.ts("dve", lam_col[:, :], lam_col[:, :], lam_init, None, ALU.add, None, [lam_col], [lam_col])
    C.ts("dve", Gsub[:, :, :], Gsub[:, :, :], 1.0 - lam_init, None, ALU.mult, None, [Gsub], [Gsub])

    KT = C.sb("KT", [128, 2, T], BF16)
    Vr = C.sb("Vr", [128, NT, 2, 130], BF16)
    KTb = [Buf("KT%d" % j) for j in range(NB)]
    Vrb = [Buf("Vr%d" % j) for j in range(NB)]
    for j in range(NB):
        C.memset("dve", Vr[:, 3 * j:3 * j + 3, :, 128:130], 1.0, [Vrb[j]])

    xTs = [C.sb("xT%d" % i, [128, 8, BLK], BF16) for i in range(2)]
    qkf = C.sb("qkf", [128, 3, 512], F32)
    sqq = C.sb("sqq", [128, 3, 512], F32)
    ssq = C.sb("ssq", [128, 24], F32)
    rt = C.sb("rt", [128, 4, 192], F32)
    qkb = C.sb("qkb", [128, 3, 512], BF16)
    QTs = [C.sb("QT%d" % i, [128, 2, BLK], BF16) for i in range(2)]
    Xs = [[C.sb("X%d_%d" % (ct, i), [128, BLK + 3], F32) for i in range(2)] for ct in range(2)]
    Hs = [[C.sb("H%d_%d" % (ct, i), [128, BLK], F32) for i in range(2)] for ct in range(2)]

    def rt_tiles(ct, names):
        return {n: C.sb("%s%d" % (n, ct), [128, BLK], F32) for n in names}
    RT = [rt_tiles(ct, ["xc", "sqg", "ug", "gl", "r", "a", "m", "ig", "u"]) for ct in range(2)]
    xcb = [C.sb("xcb%d" % ct, [128, BLK], BF16) for ct in range(2)]
    yb = [[C.sb("yb%d_%d" % (ct, i), [128, BLK], BF16) for i in range(2)] for ct in range(2)]
    Pt = [C.sb("Pt%d" % k, [128, 2, BLK], BF16) for k in range(3)]
    lt = C.sb("lt", [128, 2, 3], F32)
    t1 = C.sb("t1", [128, 3, 128], F32)
    Dt = C.sb("Dt", [128, 3, 128], F32)
    sqd = C.sb("sqd", [128, 3, 128], F32)
    ssd = C.sb("ssd", [128, 3], F32)
    ob = C.sb("ob", [128, 3, 128], BF16)
    oTs = [C.sb("oT%d" % i, [128, BLK], BF16) for i in range(2)]

    bank = C.bank
    if hooks is None:
        hv = hnT_d.rearrange("(c p) t -> p c t", p=128)

        def load_x(j):
            xT = xTs[j % 2]
            P.dma("sp", xT[:, :, :], hv[:, :, j * BLK:(j + 1) * BLK], writes=[xT])

        def store_orec(j, ct, Y):
            P.dma("sp", orecT_o[ct * 128:(ct + 1) * 128, j * BLK:(j + 1) * BLK], Y[:, :], reads=[Y])

        def store_oattn(j, hl, oT):
            P.dma("sp", oattnT_o[hl * 128:(hl + 1) * 128, j * BLK:(j + 1) * BLK], oT[:, :], reads=[oT])

        def end_block(j):
            pass
    else:
        def load_x(j):
            hooks["load_x"](j, xTs[j % 2])
        store_orec = hooks["store_orec"]
        store_oattn = hooks["store_oattn"]
        end_block = hooks["end_block"]

    load_x(0)
    ocount = [0]
    for j in range(NB):
        xT = xTs[j % 2]
        QT = QTs[j % 2]
        if j + 1 < NB:
            load_x(j + 1)
        for t in range(3):
            b0 = 0 if t % 2 == 0 else 2
            b1 = b0 + 1
            for c in range(8):
                C.mm(C.pf(b0), xT[:, c, t * 128:(t + 1) * 128], Win[:, c, 0:512], c == 0, c == 7, [xT, Win], [bank[b0]])
            for c in range(8):
                C.mm(C.pf(b1, 256), xT[:, c, t * 128:(t + 1) * 128], Win[:, c, 512:768], c == 0, c == 7,
                     [xT, Win], [bank[b1]])
            C.cp("act", qkf[:, t, :], C.pf(b0), [bank[b0]], [qkf])
            C.cp("dve", Vr[:, 3 * j + t, :, 0:128], C.pf(b1, 256).rearrange("p (h e) -> p h e", e=128),
                 [bank[b1]], [Vrb[j]])
        C.act(sqq[:, :, :], qkf[:, :, :], AF.Square, [qkf], [sqq])
        C.red(ssq[:, :], sqq[:, :, :].rearrange("p t (g d) -> p (t g) d", d=64), [sqq], [ssq])
        C.rstd(ssq[:, :], ssq[:, :], 64.0, [ssq], [ssq])
        qg = qkf[:, :, :].rearrange("p t (g d) -> p (t g) d", d=64)
        C.tt("dve", qg, qg, ssq[:, :].rearrange("p (g o) -> p g o", o=1).to_broadcast([128, 24, 64]), ALU.mult,
             [qkf, ssq], [qkf])
        C.tt("dve", qkf[:, :, :], qkf[:, :, :], Gqk[:, :].rearrange("p (o n) -> p o n", o=1).to_broadcast([128, 3, 512]),
             ALU.mult, [qkf, Gqk], [qkf])
        qv = qkf[:, :, :].rearrange("p t (g d) -> p t g d", d=64)
        qbv = qkb[:, :, :].rearrange("p t (g d) -> p t g d", d=64)
        x1 = qv[:, :, :, 0:8]
        x2 = qv[:, :, :, 8:16]
        cosb = CS[:, 3 * j:3 * j + 3, 0:8].rearrange("p t (o d) -> p t o d", o=1).to_broadcast([128, 3, 8, 8])
        sinb = CS[:, 3 * j:3 * j + 3, 8:16].rearrange("p t (o d) -> p t o d", o=1).to_broadcast([128, 3, 8, 8])

        def rtv(k):
            return rt[:, k, :].rearrange("p (t g d) -> p t g d", t=3, g=8)
        C.tt("dve", rtv(0), x1, cosb, ALU.mult, [qkf, CS], [rt])
        C.tt("dve", rtv(1), x2, sinb, ALU.mult, [qkf, CS], [rt])
        C.tt("dve", rtv(2), x2, cosb, ALU.mult, [qkf, CS], [rt])
        C.tt("dve", rtv(3), x1, sinb, ALU.mult, [qkf, CS], [rt])
        C.tt("dve", qbv[:, :, :, 0:8], rtv(0), rtv(1), ALU.subtract, [rt], [qkb])
        C.tt("dve", qbv[:, :, :, 8:16], rtv(2), rtv(3), ALU.add, [rt], [qkb])
        C.cp("act", qbv[:, :, :, 16:64], qv[:, :, :, 16:64], [qkf], [qkb])
        pq = C.pb(6)
        pk = C.pb(7)
        for t in range(3):
            for g4 in range(4):
                dstb = pq if g4 < 2 else pk
                hl = g4 % 2
                C.tr(dstb[:, hl * BLK + t * 128: hl * BLK + (t + 1) * 128], qkb[:, t, g4 * 128:(g4 + 1) * 128], ident,
                     [qkb], [bank[6 if g4 < 2 else 7]])
        C.cp("act", QT[:, :, :], pq[:, 0:2 * BLK].rearrange("p (h t) -> p h t", h=2), [bank[6]], [QT])
        C.cp("dve", KT[:, :, j * BLK:(j + 1) * BLK], pk[:, 0:2 * BLK].rearrange("p (h t) -> p h t", h=2),
             [bank[7]], [KTb[j]])
        for ct in range(2):
            R = RT[ct]
            X = Xs[ct][j % 2]
            Xp = Xs[ct][(j + 1) % 2]
            H = Hs[ct][j % 2]
            Hp = Hs[ct][(j + 1) % 2]
            Y = yb[ct][j % 2]
            bx, bg, br, bi = 4 + ct, 0 + ct, 2 + ct, 6 + ct
            for c in range(8):
                C.mm(C.pf(bx, BLK), Win[:, c, 768 + ct * 128:768 + (ct + 1) * 128], xT[:, c, :], c == 0, c == 7,
                     [Win, xT], [bank[bx]])
            for c in range(8):
                C.mm(C.pf(bg, BLK), Win[:, c, 1024 + ct * 128:1024 + (ct + 1) * 128], xT[:, c, :], c == 0, c == 7,
                     [Win, xT], [bank[bg]])
            if j == 0:
                C.memset("dve", X[:, 0:3], 0.0, [X])
            else:
                C.cp("dve", X[:, 0:3], Xp[:, BLK:BLK + 3], [Xp], [X])
            C.cp("act", X[:, 3:BLK + 3], C.pf(bx, BLK), [bank[bx]], [X])
            xc = R["xc"]
            C.ts("dve", xc[:, :], X[:, 3:BLK + 3], pcs[:, ct, 3:4], pcs[:, ct, 4:5], ALU.mult, ALU.add, [X, pcs], [xc])
            for i in (2, 1, 0):
                C.stt(xc[:, :], X[:, i:i + BLK], pcs[:, ct, i:i + 1], xc[:, :], ALU.mult, ALU.add, [X, pcs, xc], [xc])
            C.cp("dve", xcb[ct][:, :], xc[:, :], [xc], [xcb[ct]])
            gps = C.pf(bg, BLK)
            C.act(R["sqg"][:, :], gps, AF.Square, [bank[bg]], [R["sqg"]])
            C.ts("dve", R["sqg"][:, :], R["sqg"][:, :], 0.044715, 1.0, ALU.mult, ALU.add, [R["sqg"]], [R["sqg"]])
            C.tt("dve", R["ug"][:, :], R["sqg"][:, :], gps, ALU.mult, [R["sqg"], bank[bg]], [R["ug"]])
            C.act(R["ug"][:, :], R["ug"][:, :], AF.Sigmoid, [R["ug"]], [R["ug"]], scale=GELU_C)
            C.tt("dve", R["gl"][:, :], R["ug"][:, :], gps, ALU.mult, [R["ug"], bank[bg]], [R["gl"]])
            C.mm(C.pf(br, BLK), WRG[:, ct, :], xcb[ct][:, :], True, True, [WRG, xcb[ct]], [bank[br]])
            C.mm(C.pf(bi, BLK), WIG[:, ct, :], xcb[ct][:, :], True, True, [WIG, xcb[ct]], [bank[bi]])
            C.act(R["r"][:, :], C.pf(br, BLK), AF.Sigmoid, [bank[br], pcs], [R["r"]], bias=pcs[:, ct, 5:6])
            C.act(R["ig"][:, :], C.pf(bi, BLK), AF.Sigmoid, [bank[bi], pcs], [R["ig"]], bias=pcs[:, ct, 6:7])
            C.act(R["a"][:, :], R["r"][:, :], AF.Exp, [R["r"], der], [R["a"]], scale=der[:, ct, 0:1])
            C.act(R["m"][:, :], R["r"][:, :], AF.Exp, [R["r"], der], [R["m"]], scale=der[:, ct, 1:2])
            C.act(R["m"][:, :], R["m"][:, :], AF.Sqrt, [R["m"], one_col], [R["m"]], scale=-1.0, bias=one_col[:, 0:1])
            if j == 0:
                C.memset("dve", R["m"][:, 0:1], 1.0, [R["m"]])
            C.tt("dve", R["u"][:, :], R["ig"][:, :], xc[:, :], ALU.mult, [R["ig"], xc], [R["u"]])
            C.tt("dve", R["u"][:, :], R["u"][:, :], R["m"][:, :], ALU.mult, [R["u"], R["m"]], [R["u"]])
            a_t, u_t = R["a"], R["u"]
            if j == 0:
                C.scan(H[:, :], a_t[:, :], u_t[:, :], 0.0, [a_t, u_t], [H])
            else:
                C.scan(H[:, :], a_t[:, :], u_t[:, :], Hp[:, BLK - 1:BLK], [a_t, u_t, Hp], [H])
            C.tt("dve", Y[:, :], H[:, :], R["gl"][:, :], ALU.mult, [H, R["gl"]], [Y])
            store_orec(j, ct, Y)
        ntile = 3 * j + 3
        for hl in range(2):
            def q0_of(i):
                return max(0, i - 3 * j) * 128

            def emit_qk(i):
                s = i % 2
                q0 = q0_of(i)
                for c in range(2):
                    b = 2 * s + c
                    C.mm(C.pf(b, BLK)[:, q0:BLK], KT[64 * c:64 * c + 64, hl, i * 128:(i + 1) * 128],
                         QT[64 * c:64 * c + 64, hl, q0:BLK], True, True, [KTb[i // 3], QT], [bank[b]])

            def emit_exp_av(i):
                s = i % 2
                q0 = q0_of(i)
                pt = Pt[i % 3]
                C.act(pt[:, :, q0:BLK], C.ps[:, 2 * s:2 * s + 2, q0:BLK], AF.Exp, [bank[2 * s], bank[2 * s + 1]], [pt],
                      scale=0.125)
                if i >= 3 * j:
                    C.tt("dve", pt[:, :, q0:q0 + 128], pt[:, :, q0:q0 + 128],
                         tri[:, :].rearrange("p (o n) -> p o n", o=1).to_broadcast([128, 2, 128]), ALU.mult,
                         [pt, tri], [pt])
                for c in range(2):
                    ov = C.pf(4 + c, 390).rearrange("p (q e) -> p q e", e=130)
                    for qc in range(q0 // 128, 3):
                        first = (i == 0 and qc == 0)
                        lastk = (i == 3 * j + qc)
                        C.mm(ov[:, qc, 0:129], pt[:, c, qc * 128:(qc + 1) * 128], Vr[:, i, hl, 0:129], first, lastk,
                             [pt, Vrb[i // 3]], [bank[4 + c]], skip_group_check=True)

            emit_qk(0)
            for i in range(ntile):
                if i + 1 < ntile:
                    emit_qk(i + 1)
                emit_exp_av(i)
            ov0 = C.pf(4, 390).rearrange("p (q e) -> p q e", e=130)
            ov1 = C.pf(5, 390).rearrange("p (q e) -> p q e", e=130)
            C.cp("dve", lt[:, 0, :], ov0[:, :, 128], [bank[4]], [lt])
            C.cp("dve", lt[:, 1, :], ov1[:, :, 128], [bank[5]], [lt])
            C.recip(lt[:, :, :], lt[:, :, :], [lt], [lt])
            C.ts("dve", lt[:, 1, :], lt[:, 1, :], lam_col[:, 0:1], None, ALU.mult, None, [lt, lam_col], [lt])
            C.tt("dve", t1[:, :, :], ov1[:, :, 0:128],
                 lt[:, 1, :].rearrange("p (q o) -> p q o", o=1).to_broadcast([128, 3, 128]), ALU.mult,
                 [bank[5], lt], [t1])
            C.tt("dve", Dt[:, :, :], ov0[:, :, 0:128],
                 lt[:, 0, :].rearrange("p (q o) -> p q o", o=1).to_broadcast([128, 3, 128]), ALU.mult,
                 [bank[4], lt], [Dt])
            C.tt("dve", Dt[:, :, :], Dt[:, :, :], t1[:, :, :], ALU.subtract, [Dt, t1], [Dt])
            C.act(sqd[:, :, :], Dt[:, :, :], AF.Square, [Dt], [sqd])
            C.red(ssd[:, :], sqd[:, :, :], [sqd], [ssd])
            C.rstd(ssd[:, :], ssd[:, :], 128.0, [ssd], [ssd])
            C.tt("dve", Dt[:, :, :], Dt[:, :, :],
                 ssd[:, :].rearrange("p (q o) -> p q o", o=1).to_broadcast([128, 3, 128]), ALU.mult, [Dt, ssd], [Dt])
            C.tt("dve", ob[:, :, :], Dt[:, :, :], Gsub[:, :, :], ALU.mult, [Dt, Gsub], [ob])
            po = C.pb(6)
            for qc in range(3):
                C.tr(po[:, qc * 128:(qc + 1) * 128], ob[:, qc, :], ident, [ob], [bank[6]])
            oT = oTs[ocount[0] % 2]
            ocount[0] += 1
            C.cp("act", oT[:, :], po[:, 0:BLK], [bank[6]], [oT])
            store_oattn(j, hl, oT)
        end_block(j)


def _colform(v, n):
    return np.ascontiguousarray(np.asarray(v, np.float32).reshape(n, 128).T)


def _const_tables(T):
    NT = T // 128
    inv = (ROPE_THETA ** (-np.arange(0, 16, 2, dtype=np.float32) / 16.0)).astype(np.float32)
    ang = np.arange(T, dtype=np.float32)[:, None] * inv[None, :]
    cs = np.concatenate([np.cos(ang), np.sin(ang)], 1).astype(np.float32)
    cs = np.ascontiguousarray(cs.reshape(NT, 128, 16).transpose(1, 0, 2))
    tri = (np.arange(128)[None, :] >= np.arange(128)[:, None]).astype(np.float32).astype(ml_dtypes.bfloat16)
    ident = np.eye(128, dtype=np.float32).astype(ml_dtypes.bfloat16)
    return cs, tri, ident


def _mix_inputs(inp, l, p, cs, tri, ident):
    f = np.float32
    sl = slice(256 * p, 256 * p + 256)
    w = inp["w_in"][l]
    w_core = np.ascontiguousarray(np.concatenate(
        [w[:, 0:512][:, sl], w[:, 512:1024][:, sl], w[:, 1024:1536][:, sl], w[:, 1536:2048][:, sl],
         w[:, 2048:2560][:, sl]], axis=1).astype(f))
    gq, gk = inp["q_norm_g"][l], inp["k_norm_g"][l]
    gqk = np.ascontiguousarray(np.broadcast_to(np.concatenate([np.tile(gq, 4), np.tile(gk, 4)])[None, :], (128, 512)).astype(f))
    lamv = np.ascontiguousarray(np.broadcast_to(
        np.stack([inp["lambda_q1"][l], inp["lambda_q2"][l], inp["lambda_k1"][l], inp["lambda_k2"][l]])[None],
        (128, 4, 64)).astype(f))
    gs = np.ascontiguousarray(np.broadcast_to(np.tile(inp["subln_g"][l], 3)[None, :], (128, 384)).astype(f))
    pc = np.zeros((128, 2, 8), f)
    for ct in range(2):
        c0 = 256 * p + ct * 128
        pc[:, ct, 0:4] = inp["conv_w"][l][:, c0:c0 + 128].T
        pc[:, ct, 4] = inp["conv_b"][l][c0:c0 + 128]
        pc[:, ct, 5] = inp["b_rg"][l][c0:c0 + 128]
        pc[:, ct, 6] = inp["b_ig"][l][c0:c0 + 128]
        pc[:, ct, 7] = inp["lru_L"][l][c0:c0 + 128]
    return dict(w_in=w_core, gqk=gqk, lamv=lamv, gsub=gs, pc=pc,
                wrg=np.ascontiguousarray(inp["w_rg"][l][4 * p:4 * p + 4].astype(f)),
                wig=np.ascontiguousarray(inp["w_ig"][l][4 * p:4 * p + 4].astype(f)),
                cs=cs, tri=tri, ident=ident)


def kernel_unfused(**inp):
    inp = {k: np.asarray(v) for k, v in inp.items()}
    x = inp["x"]
    B = x.shape[0]
    depth = inp["w_in"].shape[0]
    T = T_PAD
    Th = T // 2
    cores = [(b, p) for b in range(B) for p in range(2)]
    ncores = len(cores)
    cs, tri, ident = _const_tables(T)
    h = np.zeros((B, T, D_MODEL), np.float32)
    h[:, :N_META] = inp["meta_tokens"][None]
    h[:, N_META:N_META + SEQ] = x

    nc = build_pre(Th)
    gm = _colform(inp["norm_mix_g"][0], 8)
    maps = [dict(h_in=np.ascontiguousarray(h[b, p * Th:(p + 1) * Th]), gmix=gm, ident=ident) for b, p in cores]
    res = run_bass_kernel_spmd(nc, maps, core_ids=list(range(ncores))).results
    hnT = [np.concatenate([res[2 * b]["hnT_o"], res[2 * b + 1]["hnT_o"]], axis=1) for b in range(B)]
    for l in range(depth):
        last = l == depth - 1
        nc = build_mix(T, l)
        maps = []
        for b, p in cores:
            d = _mix_inputs(inp, l, p, cs, tri, ident)
            d["hnT"] = np.ascontiguousarray(hnT[b])
            maps.append(d)
        res = run_bass_kernel_spmd(nc, maps, core_ids=list(range(ncores))).results
        oattnT = [np.concatenate([res[2 * b]["oattnT_o"], res[2 * b + 1]["oattnT_o"]], axis=0) for b in range(B)]
        orecT = [np.concatenate([res[2 * b]["orecT_o"], res[2 * b + 1]["orecT_o"]], axis=0) for b in range(B)]
        nc = build_post(Th, last)
        gnext = inp["norm_mix_g"][l + 1] if not last else np.ones(D_MODEL, np.float32)
        gcols = np.ascontiguousarray(np.concatenate(
            [_colform(inp["rec_norm_g"][l], 4), _colform(inp["norm_ffn_g"][l], 8), _colform(gnext, 8)], axis=1))
        maps = []
        for b, p in cores:
            tsl = slice(p * Th, (p + 1) * Th)
            maps.append(dict(h_in=np.ascontiguousarray(h[b, tsl]),
                             oattnT=np.ascontiguousarray(oattnT[b][:, tsl]),
                             orecT=np.ascontiguousarray(orecT[b][:, tsl]),
                             w_out=np.ascontiguousarray(inp["w_out"][l].astype(np.float32)),
                             w_gu=np.ascontiguousarray(inp["w_gu"][l].astype(np.float32)),
                             w_down=np.ascontiguousarray(inp["w_down"][l].astype(np.float32)),
                             gcols=gcols, ident=ident))
        res = run_bass_kernel_spmd(nc, maps, core_ids=list(range(ncores))).results
        for i, (b, p) in enumerate(cores):
            h[b, p * Th:(p + 1) * Th] = res[i]["h_out"]
        if not last:
            hnT = [np.concatenate([res[2 * b]["hnT_o"], res[2 * b + 1]["hnT_o"]], axis=1) for b in range(B)]
    return np.ascontiguousarray(h[:, N_META:N_META + SEQ]).astype(np.float32)


NLB = (T_PAD // 2) // BLK
X1_CH = [(0, 2), (2, 2), (4, 2), (6, 2), (8, 2), (10, 1)]
PAIRS = [[0, 1], [2, 3], [4, 5], [6, 7]]
REC_POS = [2, 3, 6, 7]
ATT_POS = [0, 1, 4, 5]


def emit_pre_f(C, x_in, gmix_d, ident, store_nx, end_lblock):
    P = C.P
    gcol = C.sb("gcol", [128, 8], F32)
    P.dma("sp", gcol[:], gmix_d, writes=[gcol])
    hv = x_in.rearrange("(t p) d -> p t d", p=128)
    hts = [C.sb("h%d" % i, [128, 3, D_MODEL], F32) for i in range(2)]
    outs = [C.sb("o%d" % i, [128, 8, BLK], BF16) for i in range(2)]
    ss = C.sb("ss", [128, 4], F32)
    hn = C.sb("hn", [128, 3, D_MODEL], BF16)
    for j in range(NLB):
        ht = hts[j % 2]
        oT = outs[j % 2]
        P.dma("sp", ht[:, :, :], hv[:, 3 * j:3 * j + 3, :], writes=[ht])
        emit_norm_T(C, ht, 3, gcol, oT, ident, 0, None, ss, hn)
        store_nx(j, oT)
        end_lblock(j)


def emit_post_f(C, last, w_out, w_gu, w_down, gcols, ident, sel, load_ab, load_h, store_h, store_nx, end_lblock,
                wbufs=None):
    P = C.P
    NF = D_FF // 128
    gc = C.sb("gc", [128, 20], F32)
    P.dma("sp", gc[:], gcols, writes=[gc])
    ones = C.sb("ones", [128, 128], BF16)
    C.memset("dve", ones[:], 1.0 / 512.0, [ones])
    Wout = C.sb("Wout", [128, 8, D_MODEL], BF16)
    Wgu = C.sb("Wgu", [128, 8, 2 * D_FF], BF16)
    Wdn = C.sb("Wdn", [128, NF, D_MODEL], BF16)
    GW = 512
    ngg = (D_FF + GW - 1) // GW
    Wout_b = Buf("Wout_b")
    Wgu_b = [Buf("Wgu_b%d" % g) for g in range(2 * ngg)]
    Wdn_b = [Buf("Wdn_b%d" % k) for k in range(NF // 2)]
    vo = w_out.rearrange("(c p) n -> p c n", p=128)
    vg = w_gu.rearrange("(c p) n -> p c n", p=128)
    vd = w_down.rearrange("(c p) n -> p c n", p=128)
    for c in range(8):
        for n0 in (0, 512):
            P.dma("pool", Wout[:, c, n0:n0 + 512], vo[:, c, n0:n0 + 512], writes=[Wout_b], par=True)
    for g in range(ngg):
        for half in range(2):
            n0 = half * D_FF + g * GW
            n1 = half * D_FF + min(D_FF, (g + 1) * GW)
            P.dma("pool", Wgu[:, :, n0:n1], vg[:, :, n0:n1], writes=[Wgu_b[half * ngg + g]])
    for k in range(NF // 2):
        for n0 in (0, 512):
            P.dma("pool", Wdn[:, 2 * k:2 * k + 2, n0:n0 + 512], vd[:, 2 * k:2 * k + 2, n0:n0 + 512],
                  writes=[Wdn_b[k]], par=True)
    mixT = C.sb("mixT", [128, 8, BLK], BF16)
    mixA = mixT
    mixB = C.sb("mixB", [128, 8, BLK], BF16)
    ht = C.sb("ht", [128, 3, D_MODEL], F32)
    rstd_r = C.sb("rstd_r", [128, BLK], F32)
    mixn = C.sb("mixn", [128, 4, BLK], BF16)
    ss = C.sb("ss", [128, 4], F32)
    hn = C.sb("hn", [128, 3, D_MODEL], BF16)
    hnT = C.sb("hnT", [128, 8, BLK], BF16)
    sg = C.sb("sg", [128, 2, BLK], BF16)
    actT = C.sb("actT", [128, NF, BLK], BF16)
    nxT = hnT
    for j in range(NLB):
        load_ab(j, mixA, mixB)
        load_h(j, ht)
        C.ts("dve", mixT[:, :, :], mixA[:, :, :], sel[:, 0:1], None, ALU.mult, None, [mixA, sel], [mixT])
        C.stt(mixT[:, :, :], mixB[:, :, :], sel[:, 1:2], mixT[:, :, :], ALU.mult, ALU.add, [mixB, sel, mixT], [mixT])
        for k, c in enumerate(REC_POS):
            C.act(mixn[:, k, :], mixT[:, c, :], AF.Square, [mixT], [mixn])
        for k in range(4):
            C.mm(C.pf(6, BLK), ones[:, :], mixn[:, k, :], k == 0, k == 3, [ones, mixn], [C.bank[6]])
        C.rstd(rstd_r[:, :], C.pf(6, BLK), 1.0, [C.bank[6]], [rstd_r])
        for k, c in enumerate(REC_POS):
            C.stt(mixn[:, k, :], mixT[:, c, :], gc[:, k:k + 1], rstd_r[:, :], ALU.mult, ALU.mult,
                  [mixT, gc, rstd_r], [mixn])
        for t in range(3):
            for n in range(2):
                for c in range(8):
                    if c in REC_POS:
                        lhsT = mixn[:, REC_POS.index(c), t * 128:(t + 1) * 128]
                    else:
                        lhsT = mixT[:, c, t * 128:(t + 1) * 128]
                    C.mm(C.pf(n), lhsT, Wout[:, c, n * 512:(n + 1) * 512], c == 0, c == 7,
                         [mixT, mixn, Wout_b], [C.bank[n]])
            for n in range(2):
                C.tt("dve", ht[:, t, n * 512:(n + 1) * 512], ht[:, t, n * 512:(n + 1) * 512], C.pf(n), ALU.add,
                     [ht, C.bank[n]], [ht])
        emit_norm_T(C, ht, 3, _ColView(gc, 4), hnT, ident, 6, None, ss, hn)
        for f in range(NF):
            bg = 2 + 2 * (f % 2)
            bu = bg + 1
            for c in range(8):
                C.mm(C.pf(bg, BLK), Wgu[:, c, f * 128:(f + 1) * 128], hnT[:, c, :], c == 0, c == 7,
                     [Wgu_b[(f * 128) // GW], hnT], [C.bank[bg]])
            for c in range(8):
                C.mm(C.pf(bu, BLK), Wgu[:, c, D_FF + f * 128:D_FF + (f + 1) * 128], hnT[:, c, :], c == 0, c == 7,
                     [Wgu_b[ngg + (f * 128) // GW], hnT], [C.bank[bu]])
            C.act(sg[:, f % 2, :], C.pf(bg, BLK), AF.Silu, [C.bank[bg]], [sg])
            C.tt("dve", actT[:, f, :], sg[:, f % 2, :], C.pf(bu, BLK), ALU.mult, [sg, C.bank[bu]], [actT])
        for t in range(3):
            for n in range(2):
                for f in range(NF):
                    C.mm(C.pf(n), actT[:, f, t * 128:(t + 1) * 128], Wdn[:, f, n * 512:(n + 1) * 512],
                         f == 0, f == NF - 1, [actT, Wdn_b[f // 2]], [C.bank[n]])
            for n in range(2):
                C.tt("dve", ht[:, t, n * 512:(n + 1) * 512], ht[:, t, n * 512:(n + 1) * 512], C.pf(n), ALU.add,
                     [ht, C.bank[n]], [ht])
        store_h(j, ht)
        if not last:
            emit_norm_T(C, ht, 3, _ColView(gc, 12), nxT, ident, 6, None, ss, hn)
            store_nx(j, nxT)
            end_lblock(j)


def build_fused(depth=2):
    T = T_PAD
    Th = T // 2
    NT = T // 128
    nc = bass.Bass("TRN2", target_bir_lowering=False)
    with contextlib.ExitStack() as st:
        C = Ctx(nc, st)
        P = C.P
        IN = "ExternalInput"
        x_in = C.dram("x_in", [Th, D_MODEL], F32, IN)
        sel_d = C.dram("sel", [128, 2], F32, IN)
        gmix0 = C.dram("gmix0", [128, 8], F32, IN)
        cs_d = C.dram("cs", [128, NT, 16], F32, IN)
        tri_d = C.dram("tri", [128, 128], BF16, IN)
        ident_d = C.dram("ident", [128, 128], BF16, IN)
        L = []
        for l in range(depth):
            L.append(dict(
                w_in=C.dram("w_in%d" % l, [D_MODEL, 1280], F32, IN),
                gqk=C.dram("gqk%d" % l, [128, 512], F32, IN),
                lamv=C.dram("lamv%d" % l, [128, 4, 64], F32, IN),
                gsub=C.dram("gsub%d" % l, [128, 384], F32, IN),
                pc=C.dram("pc%d" % l, [128, 2, 8], F32, IN),
                wrg=C.dram("wrg%d" % l, [4, 64, 64], F32, IN),
                wig=C.dram("wig%d" % l, [4, 64, 64], F32, IN),
                w_out=C.dram("w_out%d" % l, [1024, D_MODEL], F32, IN),
                w_gu=C.dram("w_gu%d" % l, [D_MODEL, 2 * D_FF], F32, IN),
                w_down=C.dram("w_down%d" % l, [D_FF, D_MODEL], F32, IN),
                gcols=C.dram("gcols%d" % l, [128, 20], F32, IN)))
        h_out = C.dram("h_out", [Th, D_MODEL], F32, "ExternalOutput")
        hres = nc.dram_tensor("hres", [Th, D_MODEL], F32).ap()
        hres_b = [Buf("hres%d" % j) for j in range(NLB)]
        x1s = [nc.dram_tensor("x1s%d" % i, [D_MODEL, n * BLK], BF16).ap() for i, (b0, n) in enumerate(X1_CH)]
        x1d = [nc.dram_tensor("x1d%d" % i, [2 * D_MODEL, n * BLK], BF16).ap() for i, (b0, n) in enumerate(X1_CH)]
        x1s_b = [Buf("x1s%d" % i) for i in range(len(X1_CH))]
        x1d_b = [Buf("x1d%d" % i) for i in range(len(X1_CH))]
        NX2 = (T // BLK) // 2
        x2s = [nc.dram_tensor("x2s%d" % i, [512, 2 * BLK], BF16).ap() for i in range(NX2)]
        x2d = [nc.dram_tensor("x2d%d" % i, [1024, 2 * BLK], BF16).ap() for i in range(NX2)]
        x2s_b = [Buf("x2s%d" % i) for i in range(NX2)]
        x2d_b = [Buf("x2d%d" % i) for i in range(NX2)]

        WB = []
        conv = []
        for l in range(0):
            d = {}
            for key, shp in (("w_out", [1024, D_MODEL]), ("w_gu", [D_MODEL, 2 * D_FF]), ("w_down", [D_FF, D_MODEL])):
                dst = nc.dram_tensor("%s_b%d" % (key, l), shp, BF16).ap()
                bufw = Buf("%s_b%d" % (key, l))
                d[key] = (dst, bufw)
                src = L[l][key]
                for r0 in range(0, shp[0], 128):
                    for c0 in range(0, shp[1], 2048):
                        c1 = min(shp[1], c0 + 2048)
                        conv.append((dst[r0:r0 + 128, c0:c1], src[r0:r0 + 128, c0:c1], bufw))
            WB.append(d)
        conv.reverse()

        def pump_conv(n):
            for _ in range(n):
                if not conv:
                    return
                o, i_, bufw = conv.pop()
                P.dma("pool", o, i_, writes=[bufw], par=True)

        C.consts()
        ident = C.sb("ident_sb", [128, 128], BF16)
        sel = C.sb("sel_sb", [128, 2], F32)
        P.dma("sp", ident[:], ident_d, writes=[ident])
        P.dma("sp", sel[:], sel_d, writes=[sel])
        C.use_arena(53100)

        def x1_chunk_of(jl):
            for i, (b0, n) in enumerate(X1_CH):
                if b0 <= jl < b0 + n:
                    return i, jl - b0, n
            raise AssertionError

        def store_nx(jl, oT):
            i, k, n = x1_chunk_of(jl)
            P.dma("sp", x1s[i].rearrange("(c p) t -> p c t", p=128)[:, :, k * BLK:(k + 1) * BLK], oT[:, :, :],
                  reads=[oT], writes=[x1s_b[i]])

        def end_lblock(jl):
            i, k, n = x1_chunk_of(jl)
            if k == n - 1:
                src, dst = x1s[i], x1d[i]
                P.cc(lambda e: e.collective_compute("AllGather", ALU.bypass, replica_groups=PAIRS,
                                                    ins=[src.opt()], outs=[dst.opt()]),
                     reads=[x1s_b[i]], writes=[x1d_b[i]])

        def mix_load_x(j, xT):
            half, jl = j // NLB, j % NLB
            i, k, n = x1_chunk_of(jl)
            v = x1d[i].rearrange("(r c p) t -> r p c t", r=2, p=128)
            P.dma("sp", xT[:, :, :], v[half][:, :, k * BLK:(k + 1) * BLK], reads=[x1d_b[i]], writes=[xT])

        def store_orec(j, ct, Y):
            c, k = j // 2, j % 2
            P.dma("sp", x2s[c][256 + ct * 128:256 + (ct + 1) * 128, k * BLK:(k + 1) * BLK], Y[:, :],
                  reads=[Y], writes=[x2s_b[c]])

        def store_oattn(j, hl, oT):
            c, k = j // 2, j % 2
            P.dma("sp", x2s[c][hl * 128:(hl + 1) * 128, k * BLK:(k + 1) * BLK], oT[:, :],
                  reads=[oT], writes=[x2s_b[c]])

        def mix_end_block(j):
            if j % 2 == 1:
                c = j // 2
                src, dst = x2s[c], x2d[c]
                P.cc(lambda e: e.collective_compute("AllGather", ALU.bypass, replica_groups=PAIRS,
                                                    ins=[src.opt()], outs=[dst.opt()]),
                     reads=[x2s_b[c]], writes=[x2d_b[c]])

        def load_ab(jl, mixA, mixB):
            for tile_, j in ((mixA, jl), (mixB, NLB + jl)):
                c, k = j // 2, j % 2
                v = x2d[c].rearrange("(q p) t -> p q t", p=128)
                P.dma("sp", tile_[:, :, :], v[:, :, k * BLK:(k + 1) * BLK], reads=[x2d_b[c]], writes=[tile_])

        keep = C.arena_off
        emit_pre_f(C, x_in, gmix0, ident, store_nx, end_lblock)
        for l in range(depth):
            last = l == depth - 1
            lam_init = 0.8 - 0.6 * math.exp(-0.3 * l)
            P.barrier()
            C.arena_reset(keep)
            emit_mix(C, T, lam_init, None, L[l]["w_in"], L[l]["gqk"], L[l]["lamv"], L[l]["gsub"], L[l]["pc"],
                     L[l]["wrg"], L[l]["wig"], cs_d, tri_d, ident_d, None, None,
                     hooks=dict(load_x=mix_load_x, store_orec=store_orec, store_oattn=store_oattn,
                                end_block=mix_end_block))
            P.barrier()
            C.arena_reset(keep)
            src_h = x_in if l == 0 else hres
            dst_h = h_out if last else hres

            def load_h(jl, ht, src_h=src_h, l=l):
                rd = [] if l == 0 else [hres_b[jl]]
                P.dma("sp", ht[:, :, :], src_h.rearrange("(t p) d -> p t d", p=128)[:, 3 * jl:3 * jl + 3, :],
                      reads=rd, writes=[ht])

            def store_h(jl, ht, dst_h=dst_h, last=last):
                wr = [] if last else [hres_b[jl]]
                P.dma("sp", dst_h.rearrange("(t p) d -> p t d", p=128)[:, 3 * jl:3 * jl + 3, :], ht[:, :, :],
                      reads=[ht], writes=wr)

            emit_post_f(C, last, L[l]["w_out"], L[l]["w_gu"], L[l]["w_down"], L[l]["gcols"], ident, sel,
                        load_ab, load_h, store_h, store_nx, end_lblock)
        P.emit()
        print("fused stats", P.stats)
    return nc


def kernel(**inp):
    inp = {k: np.asarray(v) for k, v in inp.items()}
    x = inp["x"]
    B = x.shape[0]
    depth = inp["w_in"].shape[0]
    T = T_PAD
    Th = T // 2
    cores = [(b, p) for b in range(B) for p in range(2)]
    cs, tri, ident = _const_tables(T)
    h = np.zeros((B, T, D_MODEL), np.float32)
    h[:, :N_META] = inp["meta_tokens"][None]
    h[:, N_META:N_META + SEQ] = x
    nc = build_fused(depth)
    f = np.float32
    shared = {}
    for l in range(depth):
        last = l == depth - 1
        gnext = inp["norm_mix_g"][l + 1] if not last else np.ones(D_MODEL, f)
        shared["gcols%d" % l] = np.ascontiguousarray(np.concatenate(
            [_colform(inp["rec_norm_g"][l], 4), _colform(inp["norm_ffn_g"][l], 8), _colform(gnext, 8)], axis=1))
        wo = inp["w_out"][l].astype(f)
        shared["w_out%d" % l] = np.ascontiguousarray(np.concatenate([wo[0:256], wo[512:768], wo[256:512], wo[768:1024]], 0))
        shared["w_gu%d" % l] = np.ascontiguousarray(inp["w_gu"][l].astype(f))
        shared["w_down%d" % l] = np.ascontiguousarray(inp["w_down"][l].astype(f))
    gm0 = _colform(inp["norm_mix_g"][0], 8)
    maps = []
    for b, p in cores:
        d = dict(shared)
        d["x_in"] = np.ascontiguousarray(h[b, p * Th:(p + 1) * Th])
        selv = np.zeros((128, 2), f)
        selv[:, p] = 1.0
        d["sel"] = selv
        d["gmix0"] = gm0
        d["cs"] = cs
        d["tri"] = tri
        d["ident"] = ident
        for l in range(depth):
            m = _mix_inputs(inp, l, p, cs, tri, ident)
            for k in ("w_in", "gqk", "lamv", "gsub", "pc", "wrg", "wig"):
                d["%s%d" % (k, l)] = m[k]
        maps.append(d)
    res = run_bass_kernel_spmd(nc, maps, core_ids=list(range(len(cores)))).results
    out = np.zeros((B, T, D_MODEL), np.float32)
    for i, (b, p) in enumerate(cores):
        out[b, p * Th:(p + 1) * Th] = res[i]["h_out"]
    return np.ascontiguousarray(out[:, N_META:N_META + SEQ])
```
